# Optimizing a Trainium2 kernel written in Bass

```python
import math
import jax, jax.numpy as jnp
from jax import lax
import numpy as np

D_MODEL = 2048
BATCH = 4
SEQ = 2048
DEPTH = 2
DEC_BATCH = 128
DEC_SEQ = 1
PAST_LEN = 2048
PAGE_SIZE = 128

N_MIXERS = 4
G_W = D_MODEL // N_MIXERS
POOL_WINDOWS = (2, 4, 8, 16)
POOL_GROUP = G_W // len(POOL_WINDOWS)
POOL_BUF = max(POOL_WINDOWS) - 1
SGU_HEADS = 4
SGU_HEAD_DIM = G_W // SGU_HEADS
SGU_CHUNK = 128
SB_HEADS = 4
SB_HEAD_DIM = G_W // SB_HEADS
SB_BLOCK = 128
SB_BIAS_INIT = -6.0
DN_HEADS = 4
DN_HEAD_DIM = G_W // DN_HEADS
DN_CHUNK = 64
CONV_W = 4
D_FF = 256 * ((8 * D_MODEL // 3 + 255) // 256)
D_IN = 10 * G_W + 2 * DN_HEADS
SPLIT_POINTS = (G_W, 2 * G_W, 3 * G_W, 4 * G_W, 5 * G_W, 6 * G_W, 9 * G_W, 10 * G_W, 10 * G_W + DN_HEADS)
DEEPNORM_ALPHA = (2.0 * DEPTH) ** 0.25
DEEPNORM_BETA = (8.0 * DEPTH) ** -0.25
LN_EPS = 1e-5
NORM_EPS = 1e-6

kernel_name = 'pool_gmlp_stickbreak_deltanet_hybrid_step'


def layer_norm(x, g, b):
    xf = x.astype(jnp.float32)
    mu = jnp.mean(xf, -1, keepdims=True)
    var = jnp.mean(jnp.square(xf - mu), -1, keepdims=True)
    return ((xf - mu) * lax.rsqrt(var + LN_EPS) * g.astype(jnp.float32) + b.astype(jnp.float32)).astype(x.dtype)


def swiglu(x, w_in, w_out):
    gate, up = jnp.split(x @ w_in, 2, axis=-1)
    return (jax.nn.silu(gate) * up) @ w_out


def l2_normalize(x):
    return x * lax.rsqrt(jnp.sum(x * x, -1, keepdims=True) + NORM_EPS)


def pool_mixer(a, buf, pos0, pool_w, pool_scale):
    B, T, _ = a.shape
    af = a.astype(jnp.float32)
    ext = jnp.concatenate([buf.astype(jnp.float32), af], axis=1)
    csum = jnp.concatenate([jnp.zeros((B, 1, G_W), jnp.float32), jnp.cumsum(ext, axis=1)], axis=1)
    pos = pos0 + jnp.arange(T)
    end = csum[:, POOL_BUF + 1:]
    diffs = []
    for gi, w in enumerate(POOL_WINDOWS):
        sl = slice(gi * POOL_GROUP, (gi + 1) * POOL_GROUP)
        start = csum[:, POOL_BUF + 1 - w: POOL_BUF + 1 - w + T, sl]
        cnt = jnp.minimum(pos + 1, w).astype(jnp.float32)[None, :, None]
        diffs.append((end[..., sl] - start) / cnt - af[..., sl])
    d = jnp.stack(diffs, axis=2)
    y = jnp.einsum('btgc,gcd->btgd', d, pool_w.astype(jnp.float32)).reshape(B, T, G_W)
    y = y * pool_scale.astype(jnp.float32)
    return y, ext[:, T:].astype(buf.dtype)


def spatial_gating(u, v, sgu_w, sgu_b):
    B, T, _ = u.shape
    n = -(-T // SGU_CHUNK)
    pad = n * SGU_CHUNK - T
    vc = jnp.pad(v, ((0, 0), (0, pad), (0, 0))).reshape(B, n, SGU_CHUNK, SGU_HEADS, SGU_HEAD_DIM)
    causal = jnp.tril(jnp.ones((SGU_CHUNK, SGU_CHUNK), dtype=bool))
    w = jnp.where(causal, sgu_w, 0)
    mixed = jnp.einsum('hts,bnshd->bnthd', w, vc) + sgu_b.T[None, None, :, :, None]
    mixed = mixed.reshape(B, n * SGU_CHUNK, G_W)[:, :T]
    return u * mixed


def stick_breaking(q, k, v, sb_bias, pos0):
    B, T, H, Dh = q.shape
    Tk = k.shape[1]
    blk = min(SB_BLOCK, T)
    n_blk = -(-T // blk)
    qp = jnp.pad(q, ((0, 0), (0, n_blk * blk - T), (0, 0), (0, 0)))
    q_blocks = jnp.swapaxes(qp.reshape(B, n_blk, blk, H, Dh), 0, 1)
    q_pos = (pos0 + jnp.arange(n_blk * blk)).reshape(n_blk, blk)
    k_pos = jnp.arange(Tk)
    scale = SB_HEAD_DIM ** -0.5
    bias = sb_bias.astype(jnp.float32)[None, :, None, None]

    def one_block(args):
        qb, qpos = args
        z = jnp.einsum('bqhd,bkhd->bhqk', qb, k, preferred_element_type=jnp.float32) * scale + bias
        mask = k_pos[None, :] < qpos[:, None]
        log_fail = jnp.where(mask, jax.nn.log_sigmoid(-z), 0.0)
        later = lax.cumsum(log_fail, axis=3, reverse=True) - log_fail
        att = jnp.where(mask, jnp.exp(jax.nn.log_sigmoid(z) + later), 0.0)
        return jnp.einsum('bhqk,bkhd->bqhd', att.astype(v.dtype), v, preferred_element_type=jnp.float32)

    o = lax.map(one_block, (q_blocks, q_pos))
    return jnp.swapaxes(o, 0, 1).reshape(B, n_blk * blk, H, Dh)[:, :T]


def chunked_gated_delta(q, k, v, g, beta, s0):
    B, T, H, _ = q.shape
    DV = v.shape[-1]
    n = -(-T // DN_CHUNK)
    pad = n * DN_CHUNK - T

    def to_chunks(a):
        a = jnp.pad(a, ((0, 0), (0, pad)) + ((0, 0),) * (a.ndim - 2))
        a = a.reshape((B, n, DN_CHUNK) + a.shape[2:])
        return jnp.moveaxis(a, 3, 1)

    qc, kc, vc, gc, bc = (to_chunks(a) for a in (q, k, v, g, beta))
    G = jnp.cumsum(gc, axis=-1)
    idx = jnp.arange(DN_CHUNK)
    incl = idx[:, None] >= idx[None, :]
    strict = idx[:, None] > idx[None, :]
    diff = G[..., :, None] - G[..., None, :]
    decay = jnp.where(incl, jnp.exp(jnp.where(incl, diff, 0.0)), 0.0)
    kk = jnp.einsum('bhnid,bhnjd->bhnij', kc, kc)
    a_mat = jnp.where(strict, bc[..., :, None] * kk * decay, 0.0)
    lower = a_mat + jnp.eye(DN_CHUNK, dtype=a_mat.dtype)
    rhs = jnp.concatenate([bc[..., None] * vc, (bc * jnp.exp(G))[..., None] * kc], axis=-1)
    sol = lax.linalg.triangular_solve(lower, rhs, left_side=True, lower=True, unit_diagonal=True)
    u_base, w_mat = sol[..., :DV], sol[..., DV:]
    qk = jnp.einsum('bhnid,bhnjd->bhnij', qc, kc) * decay
    q_dec = qc * jnp.exp(G)[..., None]
    k_dec = kc * jnp.exp(G[..., -1:] - G)[..., None]
    g_end = jnp.exp(G[..., -1])
    xs = tuple(jnp.moveaxis(a, 2, 0) for a in (u_base, w_mat, qk, q_dec, k_dec, g_end))

    def step(S, inp):
        u_b, w_c, qk_c, qd_c, kd_c, ge_c = inp
        u = u_b - jnp.einsum('bhck,bhkv->bhcv', w_c, S)
        o = jnp.einsum('bhck,bhkv->bhcv', qd_c, S) + jnp.einsum('bhcs,bhsv->bhcv', qk_c, u)
        S = S * ge_c[..., None, None] + jnp.einsum('bhck,bhcv->bhkv', kd_c, u)
        return S, o

    s_end, o = lax.scan(step, s0, xs)
    o = jnp.moveaxis(jnp.moveaxis(o, 0, 2), 1, 3).reshape(B, n * DN_CHUNK, H, DV)[:, :T]
    return o, s_end


def gated_deltanet(qkv_raw, z, b_raw, a_raw, conv_buf, s0, conv_w, a_log, dt_bias, norm_g):
    B, T, _ = qkv_raw.shape
    xc = jnp.concatenate([conv_buf.astype(qkv_raw.dtype), qkv_raw], axis=1)
    conv = sum(xc[:, j:j + T].astype(jnp.float32) * conv_w[j].astype(jnp.float32) for j in range(CONV_W))
    act = jax.nn.silu(conv)
    q, k, v = jnp.split(act, 3, axis=-1)
    q = l2_normalize(q.reshape(B, T, DN_HEADS, DN_HEAD_DIM)) * (DN_HEAD_DIM ** -0.5)
    k = l2_normalize(k.reshape(B, T, DN_HEADS, DN_HEAD_DIM))
    v = v.reshape(B, T, DN_HEADS, DN_HEAD_DIM)
    beta = jax.nn.sigmoid(b_raw.astype(jnp.float32))
    g = -jnp.exp(a_log.astype(jnp.float32)) * jax.nn.softplus(a_raw.astype(jnp.float32) + dt_bias.astype(jnp.float32))
    o, s_end = chunked_gated_delta(q, k, v, g, beta, s0.astype(jnp.float32))
    o = o * lax.rsqrt(jnp.mean(o * o, -1, keepdims=True) + NORM_EPS) * norm_g.astype(jnp.float32)
    o = o * jax.nn.silu(z.astype(jnp.float32).reshape(B, T, DN_HEADS, DN_HEAD_DIM))
    return o.reshape(B, T, G_W), xc[:, T:], s_end


def trunk_layer(x, pool_buf, conv_buf, s0, k_past, v_past,
                ln_g, ln_b, w_ffn1_in, w_ffn1_out, w_ffn2_in, w_ffn2_out, w_in, w_out,
                pool_w, pool_scale, sgu_w, sgu_b, sb_bias, dn_conv_w, dn_a_log, dn_dt_bias, dn_norm_g):
    B, T, _ = x.shape
    pos0 = k_past.shape[1]
    h = layer_norm(DEEPNORM_ALPHA * x + 0.5 * swiglu(x, w_ffn1_in, w_ffn1_out), ln_g[0], ln_b[0])
    proj = h @ w_in
    p_pool, s_u, s_v, sb_q, sb_k, sb_v, dn_qkv, dn_z, dn_b, dn_a = jnp.split(proj, SPLIT_POINTS, axis=-1)
    y_pool, new_pool = pool_mixer(p_pool, pool_buf, pos0, pool_w, pool_scale)
    y_sgu = spatial_gating(s_u, s_v, sgu_w, sgu_b)
    k_new = sb_k.reshape(B, T, SB_HEADS, SB_HEAD_DIM)
    v_new = sb_v.reshape(B, T, SB_HEADS, SB_HEAD_DIM)
    k_all = jnp.concatenate([k_past.astype(k_new.dtype), k_new], axis=1)
    v_all = jnp.concatenate([v_past.astype(v_new.dtype), v_new], axis=1)
    y_sb = stick_breaking(sb_q.reshape(B, T, SB_HEADS, SB_HEAD_DIM), k_all, v_all, sb_bias, pos0).reshape(B, T, G_W)
    y_dn, new_conv, new_s = gated_deltanet(dn_qkv, dn_z, dn_b, dn_a, conv_buf, s0,
                                           dn_conv_w, dn_a_log, dn_dt_bias, dn_norm_g)
    mixed = jnp.concatenate([y_pool.astype(h.dtype), y_sgu.astype(h.dtype),
                             y_sb.astype(h.dtype), y_dn.astype(h.dtype)], axis=-1) @ w_out
    h = layer_norm(DEEPNORM_ALPHA * h + mixed, ln_g[1], ln_b[1])
    h = layer_norm(DEEPNORM_ALPHA * h + 0.5 * swiglu(h, w_ffn2_in, w_ffn2_out), ln_g[2], ln_b[2])
    return h, k_new, v_new, new_pool, new_conv, new_s, s_v


def setup_inputs(seed: int = 0) -> dict:
    key = jax.random.key(seed)
    ks = jax.random.split(key, 25)
    f32 = jnp.float32
    n_pages = PAST_LEN // PAGE_SIZE
    n_used = DEC_BATCH * n_pages
    n_phys = n_used + n_used // 4

    def nrm(k, shape, scale):
        return scale * jax.random.normal(k, shape, f32)

    page_table = jax.random.permutation(ks[4], n_phys)[:n_used].reshape(DEC_BATCH, n_pages).astype(jnp.int32)
    dt = jnp.exp(jax.random.uniform(ks[21], (DEPTH, DN_HEADS), f32, math.log(1e-3), math.log(1e-1)))
    return {
        'x_prompt': nrm(ks[0], (BATCH, SEQ, D_MODEL), 1.0),
        'x_sample': nrm(ks[1], (DEC_BATCH, DEC_SEQ, D_MODEL), 1.0),
        'cache_k': nrm(ks[2], (DEPTH, n_phys, PAGE_SIZE, SB_HEADS, SB_HEAD_DIM), 1.0),
        'cache_v': nrm(ks[3], (DEPTH, n_phys, PAGE_SIZE, SB_HEADS, SB_HEAD_DIM), 1.0),
        'page_table': page_table,
        'state_pool': nrm(ks[5], (DEPTH, DEC_BATCH, POOL_BUF, G_W), 1.0),
        'state_conv': nrm(ks[6], (DEPTH, DEC_BATCH, CONV_W - 1, 3 * G_W), 1.0),
        'state_delta': nrm(ks[7], (DEPTH, DEC_BATCH, DN_HEADS, DN_HEAD_DIM, DN_HEAD_DIM), 0.1),
        'ln_g': 1.0 + nrm(ks[8], (DEPTH, 3, D_MODEL), 0.02),
        'ln_b': nrm(ks[9], (DEPTH, 3, D_MODEL), 0.02),
        'w_ffn1_in': nrm(ks[10], (DEPTH, D_MODEL, 2 * D_FF), D_MODEL ** -0.5),
        'w_ffn1_out': nrm(ks[11], (DEPTH, D_FF, D_MODEL), DEEPNORM_BETA * D_FF ** -0.5),
        'w_ffn2_in': nrm(ks[12], (DEPTH, D_MODEL, 2 * D_FF), D_MODEL ** -0.5),
        'w_ffn2_out': nrm(ks[13], (DEPTH, D_FF, D_MODEL), DEEPNORM_BETA * D_FF ** -0.5),
        'w_in': nrm(ks[14], (DEPTH, D_MODEL, D_IN), D_MODEL ** -0.5),
        'w_out': nrm(ks[15], (DEPTH, N_MIXERS * G_W, D_MODEL), DEEPNORM_BETA * (N_MIXERS * G_W) ** -0.5),
        'pool_w': nrm(ks[16], (DEPTH, len(POOL_WINDOWS), POOL_GROUP, POOL_GROUP), POOL_GROUP ** -0.5),
        'pool_scale': 1.0 + nrm(ks[17], (DEPTH, G_W), 0.02),
        'sgu_w': nrm(ks[18], (DEPTH, SGU_HEADS, SGU_CHUNK, SGU_CHUNK), SGU_CHUNK ** -0.5),
        'sgu_b': 1.0 + nrm(ks[19], (DEPTH, SGU_HEADS, SGU_CHUNK), 0.02),
        'sb_bias': SB_BIAS_INIT + nrm(ks[24], (DEPTH, SB_HEADS), 0.1),
        'dn_conv_w': nrm(ks[20], (DEPTH, CONV_W, 3 * G_W), CONV_W ** -0.5),
        'dn_a_log': jnp.log(jax.random.uniform(ks[22], (DEPTH, DN_HEADS), f32, 1.0, 16.0)),
        'dn_dt_bias': dt + jnp.log(-jnp.expm1(-dt)),
        'dn_norm_g': 1.0 + nrm(ks[23], (DEPTH, DN_HEAD_DIM), 0.02),
    }


def reference(x_prompt, x_sample, cache_k, cache_v, page_table, state_pool, state_conv, state_delta,
              ln_g, ln_b, w_ffn1_in, w_ffn1_out, w_ffn2_in, w_ffn2_out, w_in, w_out,
              pool_w, pool_scale, sgu_w, sgu_b, sb_bias, dn_conv_w, dn_a_log, dn_dt_bias, dn_norm_g):
    b_p = x_prompt.shape[0]
    b_s = x_sample.shape[0]
    n_pages = page_table.shape[1]
    past = n_pages * cache_k.shape[2]
    dt = x_prompt.dtype
    kp_l, vp_l, poolp_l, convp_l, sp_l = [], [], [], [], []
    ks_l, vs_l, pools_l, convs_l, ss_l, sgus_l = [], [], [], [], [], []
    hp, hs = x_prompt, x_sample
    for l in range(DEPTH):
        weights = (ln_g[l], ln_b[l], w_ffn1_in[l], w_ffn1_out[l], w_ffn2_in[l], w_ffn2_out[l],
                   w_in[l], w_out[l], pool_w[l], pool_scale[l], sgu_w[l], sgu_b[l], sb_bias[l],
                   dn_conv_w[l], dn_a_log[l], dn_dt_bias[l], dn_norm_g[l])
        hp, kp, vp, poolp, convp, sp, _ = trunk_layer(
            hp,
            jnp.zeros((b_p, POOL_BUF, G_W), dt),
            jnp.zeros((b_p, CONV_W - 1, 3 * G_W), dt),
            jnp.zeros((b_p, DN_HEADS, DN_HEAD_DIM, DN_HEAD_DIM), jnp.float32),
            jnp.zeros((b_p, 0, SB_HEADS, SB_HEAD_DIM), dt),
            jnp.zeros((b_p, 0, SB_HEADS, SB_HEAD_DIM), dt),
            *weights)
        k_past = cache_k[l][page_table].reshape(b_s, past, SB_HEADS, SB_HEAD_DIM)
        v_past = cache_v[l][page_table].reshape(b_s, past, SB_HEADS, SB_HEAD_DIM)
        hs, kS, vS, poolS, convS, sS, sguS = trunk_layer(
            hs, state_pool[l], state_conv[l], state_delta[l], k_past, v_past, *weights)
        kp_l.append(kp); vp_l.append(vp); poolp_l.append(poolp); convp_l.append(convp); sp_l.append(sp)
        ks_l.append(kS); vs_l.append(vS); pools_l.append(poolS); convs_l.append(convS); ss_l.append(sS)
        sgus_l.append(sguS)
    return (hp, hs,
            jnp.stack(kp_l), jnp.stack(vp_l), jnp.stack(poolp_l), jnp.stack(convp_l), jnp.stack(sp_l),
            jnp.stack(ks_l), jnp.stack(vs_l), jnp.stack(pools_l), jnp.stack(convs_l), jnp.stack(ss_l),
            jnp.stack(sgus_l))
```

```python
import math
from contextlib import ExitStack
import numpy as np
import concourse.bass as bass
import concourse.mybir as mybir
from concourse.bass_utils import run_bass_kernel_spmd

F32 = mybir.dt.float32
BF16 = mybir.dt.bfloat16
I32 = mybir.dt.int32
AF = mybir.ActivationFunctionType
ALU = mybir.AluOpType
AX = mybir.AxisListType

D = 2048
KC = 16
GW = 512
DIN = 5128
ALPHA = 4.0 ** 0.25
LN_EPS = 1e-5
NORM_EPS = 1e-6
POOL_W = (2, 4, 8, 16)


class Buf:
    __slots__ = ("name", "lw", "rd")

    def __init__(self, name):
        self.name = name
        self.lw = None
        self.rd = {}


class Sched:
    ENGS = ("sync", "scalar", "vector", "gpsimd", "tensor")
    DMA_POOL = 6
    SEM_WRAP = 12000

    def __init__(self, nc, same_engine_sync=True):
        self.nc = nc
        self.prog = {e: [] for e in self.ENGS}
        self.sem_names = []
        self.cur_sem = {}
        self.cnt = {}
        self.obs = {e: {} for e in self.ENGS}
        self.dma_i = {e: 0 for e in self.ENGS}
        self.same = same_engine_sync
        self.n_ops = 0
        self.out_tokens = []
        for e in self.ENGS:
            self._new_eng_sem(e)

    def _new_sem(self, name):
        key = len(self.sem_names)
        self.sem_names.append(name)
        self.cnt[key] = 0
        return key

    def _new_eng_sem(self, e):
        self.cur_sem[e] = self._new_sem("c_%s_%d" % (e, len(self.sem_names)))

    def _waits(self, eng, reads, writes, is_pe_mm):
        need = {}

        def req(tok):
            if tok is None:
                return
            k, v = tok
            if need.get(k, 0) < v:
                need[k] = v
        for b in reads:
            req(b.lw)
        for b in writes:
            req(b.lw)
            for k, v in b.rd.items():
                req((k, v))
        out = []
        for k, v in need.items():
            if self.obs[eng].get(k, 0) >= v:
                continue
            if k == self.cur_sem[eng] and (is_pe_mm or not self.same):
                continue
            self.obs[eng][k] = v
            out.append((k, v))
        return out

    def _mark(self, tok, r, w):
        for b in w:
            b.lw = tok
            b.rd = {}
        for b in r:
            if b.rd.get(tok[0], 0) < tok[1]:
                b.rd[tok[0]] = tok[1]

    def op(self, eng, fn, r=(), w=(), signal=True, pe_mm=False):
        waits = self._waits(eng, r, w, pe_mm)
        k = self.cur_sem[eng]
        if signal:
            self.cnt[k] += 1
            tok = (k, self.cnt[k])
            inc = (k, 1)
        else:
            tok = (k, self.cnt[k] + 1)
            inc = None
        self.prog[eng].append((waits, fn, inc))
        self._mark(tok, r, w)
        if signal and self.cnt[k] >= self.SEM_WRAP:
            self._new_eng_sem(eng)
        self.n_ops += 1
        return tok

    def dma(self, eng, fn, r=(), w=(), is_output=False):
        i = self.dma_i[eng]
        self.dma_i[eng] += 1
        pk = ("dma", eng, i % self.DMA_POOL)
        if pk not in self.cur_sem:
            self.cur_sem[pk] = self._new_sem("d_%s_%d_%d" % (eng, i % self.DMA_POOL, len(self.sem_names)))
        k = self.cur_sem[pk]
        waits = self._waits(eng, r, w, False)
        prev = self.cnt[k]
        if prev > 0 and self.obs[eng].get(k, 0) < prev:
            self.obs[eng][k] = prev
            waits.append((k, prev))
        self.cnt[k] += 16
        tok = (k, self.cnt[k])
        self.prog[eng].append((waits, fn, (k, 16)))
        self._mark(tok, r, w)
        if is_output:
            self.out_tokens.append(tok)
        if self.cnt[k] >= self.SEM_WRAP:
            del self.cur_sem[pk]
        self.n_ops += 1
        return tok

    def alias(self, old, new):
        merged = {}
        for b in old:
            if b.lw is not None and merged.get(b.lw[0], 0) < b.lw[1]:
                merged[b.lw[0]] = b.lw[1]
            for k, v in b.rd.items():
                if merged.get(k, 0) < v:
                    merged[k] = v
        for n in new:
            n.lw = None
            n.rd = dict(merged)

    def emit(self):
        nc = self.nc
        fin = {}
        for k, v in self.out_tokens:
            fin[k] = max(fin.get(k, 0), v)
        with ExitStack() as es:
            sems = [es.enter_context(nc.semaphore(n)) for n in self.sem_names]
            block = es.enter_context(nc.Block())

            def run(ename, extra_final=None):
                def body(e):
                    for waits, fn, inc in self.prog[ename]:
                        for k, v in waits:
                            e.wait_ge(sems[k], v)
                        ins = fn(e)
                        if inc is not None:
                            ins.then_inc(sems[inc[0]], inc[1])
                    if extra_final:
                        for k, v in extra_final.items():
                            e.wait_ge(sems[k], v)
                return body

            block.sync(run("sync", fin))
            block.scalar(run("scalar"))
            block.vector(run("vector"))
            block.gpsimd(run("gpsimd"))
            block.tensor(run("tensor"))


def host_consts(N):
    c = {}
    c["ident"] = np.eye(128, dtype=np.float32)
    i = np.arange(128)
    c["u_ge"] = (i[:, None] >= i[None, :]).astype(np.float32)
    c["u_le"] = (i[:, None] <= i[None, :]).astype(np.float32)
    c["m_low"] = (i[:, None] > i[None, :]).astype(np.float32)
    c["m_upi"] = (i[:, None] <= i[None, :]).astype(np.float32)
    q = np.arange(N)
    sbm = np.zeros((128, 4, N), np.float32)
    for j in range(4):
        sbm[:, j, :] = ((j * 128 + i)[:, None] < q[None, :]).astype(np.float32)
    c["sbmask"] = sbm.reshape(128, 4 * N)
    inv = np.zeros((128, 4, 16), np.float32)
    for g, w in enumerate(POOL_W):
        inv[:, g, :] = 1.0 / np.minimum(np.arange(16) + 1, w).astype(np.float32)[None, :]
    c["invcnt"] = inv.reshape(128, 64)
    return c


class KB:
    def __init__(self, cfg):
        self.cfg = cfg
        self.SEQ = cfg["SEQ"]; self.DFF = cfg["DFF"]; self.NS = cfg["NS"]; self.NPG = cfg["NPG"]
        self.L = cfg["DEPTH"]; self.NPHYS = cfg["NPHYS"]
        self.N = 512
        self.NT = self.SEQ // self.N
        self.JF = self.DFF // 128
        self.stage = cfg.get("stage", 9)
        self.nc = bass.Bass("TRN2", target_bir_lowering=False)
        self.es = ExitStack()
        self.S = Sched(self.nc)
        self.ps_i = 0
        self.ps_held = set()
        self.w_i = 0
        self.tmp_i = 0

    def din(self, name, shape, dt=F32):
        return self.nc.dram_tensor(name, list(shape), dt, kind="ExternalInput").ap()

    def dout(self, name, shape, dt=F32):
        return self.nc.dram_tensor(name, list(shape), dt, kind="ExternalOutput").ap()

    def sb(self, name, shape, dt=F32):
        return self.es.enter_context(self.nc.sbuf_tensor("s_" + name, list(shape), dt))

    def ps(self, hold=False):
        while True:
            i = self.ps_i % 8
            self.ps_i += 1
            if i not in self.ps_held:
                break
        if hold:
            self.ps_held.add(i)
        return self.PS[i], self.PSB[i]

    def ps_release(self, p):
        for i in range(8):
            if self.PS[i] is p:
                self.ps_held.discard(i)

    def V(self, fn, r=(), w=()):
        return self.S.op("vector", fn, r, w)

    def A(self, fn, r=(), w=()):
        return self.S.op("scalar", fn, r, w)

    def G(self, fn, r=(), w=()):
        return self.S.op("gpsimd", fn, r, w)

    def mm(self, out, lhsT, rhs, start, stop, r=(), w=(), sig=None):
        return self.S.op("tensor", lambda e: e.matmul(out, lhsT=lhsT, rhs=rhs, start=start, stop=stop),
                         r, w, signal=bool(stop) if sig is None else sig, pe_mm=True)

    def tr(self, out, in_, ident, r=(), w=()):
        return self.S.op("tensor", lambda e: e.transpose(out=out, in_=in_, identity=ident), r, w, pe_mm=True)

    def evac(self, out, in_, r, w):
        self.tmp_i += 1
        if self.tmp_i % 2:
            return self.V(lambda e: e.tensor_copy(out=out, in_=in_), r, w)
        return self.A(lambda e: e.copy(out=out, in_=in_), r, w)

    def load_w(self, ap, nk):
        i = self.w_i % 2
        self.w_i += 1
        stg, bstg, wb, bwb = self.WSTG[i], self.WSTGB[i], self.WB[i], self.WBB[i]
        src = ap.rearrange("(k p) c -> p k c", p=128)
        dst = stg[:, 0:nk * 128].rearrange("p (k c) -> p k c", k=nk)
        self.S.dma("sync", lambda e: e.dma_start(out=dst, in_=src), w=[bstg])
        self.G(lambda e: e.tensor_copy(out=wb[:, 0:nk * 128], in_=stg[:, 0:nk * 128]), r=[bstg], w=[bwb])
        return wb, bwb

    def build(self):
        nc, S = self.nc, self.S
        L, SEQ, DFF, NS, N, NT, JF = self.L, self.SEQ, self.DFF, self.NS, self.N, self.NT, self.JF
        NB = SEQ // 128
        self.x_d = self.din("x", [SEQ, D])
        self.xs_d = self.din("xs", [NS, D])
        self.w1 = [self.din("w1a", [L, D, 2 * DFF]), self.din("w1b", [L, D, 2 * DFF])]
        self.w2 = [self.din("w2a", [L, DFF, D]), self.din("w2b", [L, DFF, D])]
        self.win = self.din("win", [L, D, DIN])
        self.wout = self.din("wout", [L, D, D])
        self.wba = self.din("wba", [L, D, 8 * 128])
        cst = {}
        for name, shp in [("ident", [128, 128]), ("u_ge", [128, 128]), ("u_le", [128, 128]), ("m_low", [128, 128]),
                          ("m_upi", [128, 128]), ("sbmask", [128, 4 * N]), ("invcnt", [128, 64])]:
            cst[name] = self.din("c_" + name, shp)
        ln_d = self.din("lngb", [128, L * 3 * 2 * 16])
        poolw_d = self.din("poolw", [L, 4, 128, 128])
        pscale_d = self.din("pscale", [128, L * 4])
        sguwT_d = self.din("sguwT", [L, 4, 128, 128])
        sgub_d = self.din("sgub", [1, L * 4 * 128])
        sbbias_d = self.din("sbbias", [1, L * 4])
        cw_d = self.din("convw", [128, L * 4 * 12])
        alog_d = self.din("alog", [1, L * 4])
        dtb_d = self.din("dtb", [1, L * 4])
        ng_d = self.din("normg", [128, L])
        self.ck_d = [self.din("ck%d" % i, [self.NPHYS * 128, GW]) for i in range(L)]
        self.cv_d = [self.din("cv%d" % i, [self.NPHYS * 128, GW]) for i in range(L)]
        ptab_d = self.din("ptab", [1, NS * self.NPG], I32)
        iota_d = self.din("iota", [128, 1], I32)
        sguw00_d = self.din("sguw00", [1, L * 4])
        self.sdelta_d = self.din("sdelta", [L, NS, 4, 128, 128])
        self.spool_d = self.din("spool", [L, NS, 15, GW])
        self.sconv_d = self.din("sconv", [L, NS, 3, 3 * GW])
        self.y_d = self.dout("y", [SEQ, D])
        self.ys_d = self.dout("ys", [NS, D])
        self.nk_d = self.dout("nk", [L, SEQ, GW])
        self.nv_d = self.dout("nv", [L, SEQ, GW])
        self.npool_d = self.dout("npool", [L, 15, GW])
        self.nconv_d = self.dout("nconv", [L, 3, 3 * GW])
        self.ndelta_d = self.dout("ndelta", [L, 4, 128, 128])
        self.nks_d = self.dout("nks", [L, NS, GW])
        self.nvs_d = self.dout("nvs", [L, NS, GW])
        self.npools_d = self.dout("npools", [L, NS, 15, GW])
        self.nconvs_d = self.dout("nconvs", [L, NS, 3, 3 * GW])
        self.ndeltas_d = self.dout("ndeltas", [L, NS, 4, 128, 128])
        self.nsgus_d = self.dout("nsgus", [L, NS, GW])
        if self.cfg.get("debug"):
            self.dbg_d = self.dout("dbg", [L, 128, 64])
            self.dbgq_d = self.dout("dbgq", [L, NS, GW])
        self.ysc = nc.dram_tensor("yscratch", [NT, 128, KC * N], F32, kind="Internal").ap()
        self.YSB = [Buf("ysc%d" % t) for t in range(NT)]
        self.qscr = nc.dram_tensor("qscratch", [NS, GW], F32, kind="Internal").ap()
        self.BQS = Buf("qscr")

        sb = self.sb
        self.X = sb("X", [128, KC * N]); self.BX = Buf("X")
        self.XB = sb("XB", [128, KC * N], BF16); self.BXB = Buf("XB")
        self.XS = sb("XS", [128, KC * NS]); self.BXS = Buf("XS")
        self.XSB = sb("XSB", [128, KC * NS], BF16); self.BXSB = Buf("XSB")
        self.ARENA = sb("ARENA", [128, 11264]); self.BACT = Buf("ACT")
        self.ACTV = self.ARENA[:, :].bitcast(BF16)
        self.MIXT = sb("MIXT", [128, KC * N], BF16); self.BMIX = [Buf("MIX%d" % i) for i in range(4)]
        self.MIXS = sb("MIXS", [128, KC * NS], BF16); self.BMIXS = [Buf("MIXS%d" % i) for i in range(4)]
        self.WSTG = [sb("wstg%d" % i, [128, 2048]) for i in range(2)]; self.WSTGB = [Buf("wstg%d" % i) for i in range(2)]
        self.WB = [sb("wb%d" % i, [128, 2048], BF16) for i in range(2)]; self.WBB = [Buf("wb%d" % i) for i in range(2)]
        self.KT = sb("KT", [128, 4 * SEQ], BF16); self.BKT = Buf("KT")
        self.VS = sb("VS", [128, NB * GW], BF16); self.BVS = Buf("VS")
        self.SG = [sb("sg%d" % i, [128, N]) for i in range(2)]; self.BSG = [Buf("sg%d" % i) for i in range(2)]
        self.T1 = [sb("t1_%d" % i, [128, N]) for i in range(2)]; self.BT1 = [Buf("t1_%d" % i) for i in range(2)]
        self.STAT = [sb("stat%d" % i, [128, N]) for i in range(3)]; self.BSTAT = [Buf("stat%d" % i) for i in range(3)]
        self.PS = [self.es.enter_context(nc.psum_tensor("ps%d" % i, [128, 512], F32)) for i in range(8)]
        self.PSB = [Buf("ps%d" % i) for i in range(8)]
        C = {}
        self.BC = Buf("consts")
        for name in cst:
            shp = cst[name].shape
            C[name] = sb("k_" + name, list(shp))
            S.dma("scalar", (lambda d, s_: lambda e: e.dma_start(out=d[:, :], in_=s_))(C[name], cst[name]), w=[self.BC])
        self.C = C
        self.ONESF = sb("onesf", [128, 128]); self.ONESB = sb("onesb", [128, 128], BF16)
        self.IDB = sb("identb", [128, 128], BF16)
        self.EPSC = sb("epsc", [128, 2])
        self.V(lambda e: e.memset(self.EPSC[:, 0:1], LN_EPS), w=[self.BC])
        self.V(lambda e: e.memset(self.EPSC[:, 1:2], NORM_EPS), w=[self.BC])
        self.V(lambda e: e.memset(self.ONESF[:, :], 1.0), w=[self.BC])
        self.V(lambda e: e.memset(self.ONESB[:, :], 1.0), w=[self.BC])
        self.V(lambda e: e.tensor_copy(out=self.IDB[:, :], in_=C["ident"][:, :]), r=[self.BC], w=[self.BC])
        self.LN = sb("lngb", [128, L * 3 * 2 * 16])
        S.dma("scalar", lambda e: e.dma_start(out=self.LN[:, :], in_=ln_d), w=[self.BC])
        self.PSC = sb("pscale", [128, L * 4])
        S.dma("scalar", lambda e: e.dma_start(out=self.PSC[:, :], in_=pscale_d), w=[self.BC])
        self.SGUB = sb("sgub", [128, L * 4 * 128])
        S.dma("scalar", lambda e: e.dma_start(out=self.SGUB[:, :], in_=sgub_d.partition_broadcast(128)), w=[self.BC])
        self.SBB = sb("sbbias", [128, L * 4])
        S.dma("scalar", lambda e: e.dma_start(out=self.SBB[:, :], in_=sbbias_d.partition_broadcast(128)), w=[self.BC])
        self.CW = sb("convw", [128, L * 4 * 12])
        S.dma("scalar", lambda e: e.dma_start(out=self.CW[:, :], in_=cw_d), w=[self.BC])
        self.NG = sb("normg", [128, L])
        S.dma("scalar", lambda e: e.dma_start(out=self.NG[:, :], in_=ng_d), w=[self.BC])
        self.DTB = sb("dtb", [128, L * 4])
        S.dma("scalar", lambda e: e.dma_start(out=self.DTB[:, :], in_=dtb_d.partition_broadcast(128)), w=[self.BC])
        self.NEGA = sb("nega", [128, L * 4])
        S.dma("scalar", lambda e: e.dma_start(out=self.NEGA[:, :], in_=alog_d.partition_broadcast(128)), w=[self.BC])
        self.A(lambda e: e.activation(out=self.NEGA[:, :], in_=self.NEGA[:, :], func=AF.Exp), r=[self.BC], w=[self.BC])
        self.A(lambda e: e.mul(out=self.NEGA[:, :], in_=self.NEGA[:, :], mul=-1.0), r=[self.BC], w=[self.BC])
        self.PTAB = sb("ptab", [128, NS * self.NPG], I32)
        S.dma("scalar", lambda e: e.dma_start(out=self.PTAB[:, :], in_=ptab_d.partition_broadcast(128)), w=[self.BC])
        self.IOTA = sb("iota", [128, 1], I32)
        S.dma("scalar", lambda e: e.dma_start(out=self.IOTA[:, :], in_=iota_d), w=[self.BC])
        self.IDX = sb("idx", [128, NS * self.NPG], I32)
        self.SW00 = sb("sguw00", [128, L * 4])
        S.dma("scalar", lambda e: e.dma_start(out=self.SW00[:, :], in_=sguw00_d.partition_broadcast(128)), w=[self.BC])
        self.SST = sb("dnstate", [128, 4 * 128]); self.BSST = Buf("dnstate")
        self.CHIST = sb("chist", [128, 12 * 4]); self.BCH = Buf("chist")
        self.W8 = sb("w8", [128, 16 * 8], BF16); self.BW8 = Buf("w8")
        self.POOLW = sb("poolw", [128, L * 4 * 128], BF16)
        self.SGUW = sb("sguw", [128, L * 4 * 128], BF16)
        for l in range(L):
            stg, bstg = self.WSTG[0], self.WSTGB[0]
            S.dma("scalar", (lambda l_: lambda e: e.dma_start(
                out=stg[:, 0:512].rearrange("p (g d) -> p g d", g=4),
                in_=poolw_d[l_].rearrange("g c d -> c g d")))(l), w=[bstg])
            self.V((lambda l_: lambda e: e.tensor_copy(out=self.POOLW[:, l_ * 512:(l_ + 1) * 512], in_=stg[:, 0:512]))(l),
                   r=[bstg], w=[self.BC])
            stg1, bstg1 = self.WSTG[1], self.WSTGB[1]
            S.dma("scalar", (lambda l_: lambda e: e.dma_start(
                out=stg1[:, 0:512].rearrange("p (g d) -> p g d", g=4),
                in_=sguwT_d[l_].rearrange("h s t -> s h t")))(l), w=[bstg1])
            for h in range(4):
                self.V((lambda l_, h_: lambda e: e.tensor_tensor(
                    out=self.SGUW[:, (l_ * 4 + h_) * 128:(l_ * 4 + h_ + 1) * 128],
                    in0=stg1[:, h_ * 128:(h_ + 1) * 128], in1=C["u_le"][:, :], op=ALU.mult))(l, h),
                    r=[bstg1, self.BC], w=[self.BC])

        for l in range(L):
            for t in range(NT):
                self.prompt_tile(l, t)
            if self.stage >= 2:
                self.sample_tile(l)
        S.emit()
        return nc

    def ln_cols(self, l, i, gb, c):
        o = ((l * 3 + i) * 2 + gb) * 16 + c
        return self.LN[:, o:o + 1]

    def load_x_tile(self, t):
        N = self.N
        stg = self.ARENA[:, 0:4 * D]
        self.S.dma("scalar", lambda e: e.dma_start(
            out=stg.rearrange("p (a f) -> p a f", a=4),
            in_=self.x_d[t * N:(t + 1) * N, :].rearrange("(a p) f -> p a f", p=128)), w=[self.BACT])
        for c in range(KC):
            p, bp = self.ps()
            for a in range(4):
                self.tr(p[:, a * 128:(a + 1) * 128], stg[:, a * D + c * 128: a * D + (c + 1) * 128], self.C["ident"][:, :],
                        r=[self.BACT, self.BC], w=[bp])
            self.evac(self.X[:, c * N:(c + 1) * N], p[:, 0:N], r=[bp], w=[self.BX])
        self.V(lambda e: e.tensor_copy(out=self.XB[:, :], in_=self.X[:, :]), r=[self.BX], w=[self.BXB])

    def store_y_tile(self, t, X, BX, N, out_ap):
        nb = max(1, N // 128)
        rows = min(N, 128)
        stg = self.ARENA[:, 0:nb * D]
        for a in range(nb):
            for c4 in range(KC // 4):
                p, bp = self.ps()
                for cc in range(4):
                    c = c4 * 4 + cc
                    self.tr(p[0:rows, cc * 128:(cc + 1) * 128], X[:, c * N + a * rows: c * N + a * rows + rows],
                            self.C["ident"][:, :], r=[BX, self.BC], w=[bp])
                self.evac(stg[0:rows, a * D + c4 * 512: a * D + (c4 + 1) * 512], p[0:rows, 0:512], r=[bp], w=[self.BACT])
        if N >= 128:
            self.S.dma("scalar", lambda e: e.dma_start(
                out=out_ap.rearrange("(a p) f -> p a f", p=128), in_=stg.rearrange("p (a f) -> p a f", a=nb)),
                r=[self.BACT], is_output=True)
        else:
            self.S.dma("scalar", lambda e: e.dma_start(out=out_ap, in_=stg[0:rows, 0:D]), r=[self.BACT], is_output=True)

    def ffn(self, l, which, X, XB, N, BX, BXB):
        JF, DFF = self.JF, self.DFF
        W1, W2 = self.w1[which], self.w2[which]
        ACT = self.ACTV
        for j in range(JF):
            wg, bg = self.load_w(W1[l, :, j * 128:(j + 1) * 128], 16)
            pg, bpg = self.ps()
            for k in range(KC):
                self.mm(pg[:, 0:N], wg[:, k * 128:(k + 1) * 128], XB[:, k * N:(k + 1) * N], k == 0, k == KC - 1,
                        r=[bg, BXB], w=[bpg])
            wu, bu = self.load_w(W1[l, :, DFF + j * 128:DFF + (j + 1) * 128], 16)
            pu, bpu = self.ps()
            for k in range(KC):
                self.mm(pu[:, 0:N], wu[:, k * 128:(k + 1) * 128], XB[:, k * N:(k + 1) * N], k == 0, k == KC - 1,
                        r=[bu, BXB], w=[bpu])
            sg, bsg = self.SG[j % 2], self.BSG[j % 2]
            self.A((lambda sg_, pg_: lambda e: e.activation(out=sg_[:, 0:N], in_=pg_[:, 0:N], func=AF.Silu))(sg, pg),
                   r=[bpg], w=[bsg])
            self.V((lambda sg_, pu_, j_: lambda e: e.tensor_tensor(out=ACT[:, j_ * N:(j_ + 1) * N], in0=sg_[:, 0:N],
                                                                     in1=pu_[:, 0:N], op=ALU.mult))(sg, pu, j),
                   r=[bsg, bpu], w=[self.BACT])
        for oc in range(KC):
            po, bpo = self.ps()
            for j0 in range(0, JF, 16):
                nj = min(16, JF - j0)
                w, bw = self.load_w(W2[l, j0 * 128:(j0 + nj) * 128, oc * 128:(oc + 1) * 128], nj)
                for jj in range(nj):
                    j = j0 + jj
                    self.mm(po[:, 0:N], w[:, jj * 128:(jj + 1) * 128], ACT[:, j * N:(j + 1) * N], j == 0, j == JF - 1,
                            r=[bw, self.BACT], w=[bpo], sig=(jj == nj - 1))
            t1, bt1 = self.T1[oc % 2], self.BT1[oc % 2]
            self.A((lambda t1_, po_: lambda e: e.mul(out=t1_[:, 0:N], in_=po_[:, 0:N], mul=0.5))(t1, po), r=[bpo], w=[bt1])
            self.V((lambda t1_, oc_: lambda e: e.scalar_tensor_tensor(
                out=X[:, oc_ * N:(oc_ + 1) * N], in0=X[:, oc_ * N:(oc_ + 1) * N], scalar=ALPHA, in1=t1_[:, 0:N],
                op0=ALU.mult, op1=ALU.add))(t1, oc), r=[BX, bt1], w=[BX])

    def layernorm(self, l, i, X, XB, N, BX, BXB):
        pm, bpm = self.ps()
        for c in range(KC):
            self.mm(pm[:, 0:N], self.ONESF[:, :], X[:, c * N:(c + 1) * N], c == 0, c == KC - 1, r=[BX, self.BC], w=[bpm])
        pq, bpq = self.ps()
        for c in range(KC):
            sq, bsq = self.SG[c % 2], self.BSG[c % 2]
            self.A((lambda sq_, c_: lambda e: e.activation(out=sq_[:, 0:N], in_=X[:, c_ * N:(c_ + 1) * N], func=AF.Square))(sq, c),
                   r=[BX], w=[bsq])
            self.mm(pq[:, 0:N], self.ONESF[:, :], sq[:, 0:N], c == 0, c == KC - 1, r=[bsq, self.BC], w=[bpq], sig=True)
        mean, bmean = self.STAT[0], self.BSTAT[0]
        rstd, brstd = self.STAT[1], self.BSTAT[1]
        m2, bm2 = self.STAT[2], self.BSTAT[2]
        self.A(lambda e: e.mul(out=mean[:, 0:N], in_=pm[:, 0:N], mul=1.0 / D), r=[bpm], w=[bmean])
        self.V(lambda e: e.tensor_tensor(out=m2[:, 0:N], in0=mean[:, 0:N], in1=mean[:, 0:N], op=ALU.mult), r=[bmean], w=[bm2])
        self.V(lambda e: e.scalar_tensor_tensor(out=rstd[:, 0:N], in0=pq[:, 0:N], scalar=1.0 / D, in1=m2[:, 0:N],
                                                op0=ALU.mult, op1=ALU.subtract), r=[bpq, bm2], w=[brstd])
        self.A(lambda e: e.activation(out=rstd[:, 0:N], in_=rstd[:, 0:N], func=AF.Sqrt, bias=self.EPSC[:, 0:1]), r=[brstd, self.BC], w=[brstd])
        self.V(lambda e: e.reciprocal(out=rstd[:, 0:N], in_=rstd[:, 0:N]), r=[brstd], w=[brstd])
        for c in range(KC):
            xs = X[:, c * N:(c + 1) * N]
            self.V((lambda xs_: lambda e: e.tensor_tensor(out=xs_, in0=xs_, in1=mean[:, 0:N], op=ALU.subtract))(xs),
                   r=[BX, bmean], w=[BX])
            self.V((lambda xs_: lambda e: e.tensor_tensor(out=xs_, in0=xs_, in1=rstd[:, 0:N], op=ALU.mult))(xs),
                   r=[BX, brstd], w=[BX])
            self.V((lambda xs_, c_: lambda e: e.tensor_scalar(out=xs_, in0=xs_, scalar1=self.ln_cols(l, i, 0, c_),
                                                               scalar2=self.ln_cols(l, i, 1, c_), op0=ALU.mult, op1=ALU.add))(xs, c),
                   r=[BX, self.BC], w=[BX])
            self.A((lambda xs_, c_: lambda e: e.copy(out=XB[:, c_ * N:(c_ + 1) * N], in_=xs_))(xs, c), r=[BX], w=[BXB])

    def proj_fm(self, l, W, c0, XB, N, BXB):
        w, bw = self.load_w(W[l, :, c0:c0 + 128], 16)
        p, bp = self.ps()
        for k in range(KC):
            self.mm(p[:, 0:N], w[:, k * 128:(k + 1) * 128], XB[:, k * N:(k + 1) * N], k == 0, k == KC - 1, r=[bw, BXB], w=[bp])
        return p, bp

    def wout_res(self, l, X, N, BX, MIX, BMIX):
        for oc in range(KC):
            w, bw = self.load_w(self.wout[l, :, oc * 128:(oc + 1) * 128], 16)
            p, bp = self.ps()
            for k in range(KC):
                self.mm(p[:, 0:N], w[:, k * 128:(k + 1) * 128], MIX[:, k * N:(k + 1) * N], k == 0, k == KC - 1,
                        r=[bw] + list(BMIX), w=[bp])
            self.V((lambda oc_, p_: lambda e: e.scalar_tensor_tensor(
                out=X[:, oc_ * N:(oc_ + 1) * N], in0=X[:, oc_ * N:(oc_ + 1) * N], scalar=ALPHA, in1=p_[:, 0:N],
                op0=ALU.mult, op1=ALU.add))(oc, p), r=[BX, bp], w=[BX])

    def prompt_tile(self, l, t):
        S, N, L = self.S, self.N, self.L
        X, XB, BX, BXB = self.X, self.XB, self.BX, self.BXB
        if l == 0:
            self.load_x_tile(t)
        else:
            S.dma("scalar", lambda e: e.dma_start(out=X[:, :], in_=self.ysc[t]), r=[self.YSB[t]], w=[BX])
            self.V(lambda e: e.tensor_copy(out=XB[:, :], in_=X[:, :]), r=[BX], w=[BXB])
        if self.cfg.get("cut") == 1:
            self.store_y_tile(t, X, BX, N, self.y_d[t * N:(t + 1) * N, :]); return
        self.ffn(l, 0, X, XB, N, BX, BXB)
        if self.cfg.get("cut") == 2:
            self.store_y_tile(t, X, BX, N, self.y_d[t * N:(t + 1) * N, :]); return
        self.layernorm(l, 0, X, XB, N, BX, BXB)
        if self.cfg.get("cut") == 3:
            self.store_y_tile(t, X, BX, N, self.y_d[t * N:(t + 1) * N, :]); return
        ar = self.ARENA
        off = [0]

        def aalloc(n):
            a = ar[:, off[0]:off[0] + n]
            off[0] += n
            assert off[0] <= 11264
            return a
        PT = aalloc(4 * (N + 16)); BPT = Buf("PT")
        PTMP = aalloc(2 * (N + 16)); BPTMP = Buf("PTMP")
        UT = aalloc(4 * N); BUT = Buf("UT")
        VSG = aalloc(2 * N)[:, :].bitcast(BF16); BVSG = Buf("VSG")
        QT = aalloc(2 * N)[:, :].bitcast(BF16); BQT = Buf("QT")
        ZS = aalloc(N); BZS = Buf("ZS")
        SP = aalloc(N); BSP = Buf("SP")
        TT = aalloc(N); BTT = Buf("TT")
        RR = aalloc(N); BRR = Buf("RR")
        ATT = aalloc(N // 2)[:, :].bitcast(BF16); BATT = Buf("ATT")
        DM = aalloc(N); BDM = Buf("DM")
        PHIST = self.PHIST; BPH = self.BPH
        new = [BPT, BPTMP, BUT, BVSG, BQT, BZS, BSP, BTT, BRR, BATT, BDM]
        S.alias([self.BACT], new)
        W = self.win
        MIX, BMIX = self.MIXT, self.BMIX
        NP = N + 16
        for g in range(4):
            p, bp = self.proj_fm(l, W, g * 128, XB, N, BXB)
            self.evac(PT[:, g * NP + 16:g * NP + 16 + N], p[:, 0:N], r=[bp], w=[BPT])
            if t == 0:
                self.V((lambda g_: lambda e: e.memset(PT[:, g_ * NP:g_ * NP + 16], 0.0))(g), w=[BPT])
            else:
                self.V((lambda g_: lambda e: e.tensor_copy(out=PT[:, g_ * NP:g_ * NP + 16], in_=PHIST[:, g_ * 16:(g_ + 1) * 16]))(g),
                       r=[BPH], w=[BPT])
        for g in range(4):
            self.V((lambda g_: lambda e: e.tensor_copy(out=PHIST[:, g_ * 16:(g_ + 1) * 16], in_=PT[:, g_ * NP + N:g_ * NP + N + 16]))(g),
                   r=[BPT], w=[BPH])
        if t == self.NT - 1:
            p, bp = self.ps()
            for g in range(4):
                self.tr(p[:, g * 128:(g + 1) * 128], PT[:, g * NP + 16 + N - 128:g * NP + 16 + N], self.C["ident"][:, :],
                        r=[BPT, self.BC], w=[bp])
            self.evac(TT[:, 0:512], p[:, 0:512], r=[bp], w=[BTT])
            S.dma("scalar", lambda e: e.dma_start(out=self.npool_d[l], in_=TT[113:128, 0:512]), r=[BTT], is_output=True)
        for g, wdw in enumerate(POOL_W):
            a = PT[:, g * NP:(g + 1) * NP]
            cur = a
            sh = 1
            k = 0
            while sh < wdw:
                dst = PTMP[:, k * NP:(k + 1) * NP]
                self.V((lambda cur_, dst_, sh_: lambda e: e.tensor_tensor(out=dst_[:, sh_:NP], in0=cur_[:, sh_:NP], in1=cur_[:, 0:NP - sh_],
                                                                          op=ALU.add))(cur, dst, sh), r=[BPT, BPTMP], w=[BPTMP])
                cur = dst
                sh *= 2
                k = 1 - k
            dd = PTMP[:, k * NP + 16:k * NP + 16 + N]
            self.V((lambda cur_, dd_, a_, w_: lambda e: e.scalar_tensor_tensor(out=dd_, in0=cur_[:, 16:16 + N], scalar=1.0 / w_,
                                                                              in1=a_[:, 16:16 + N], op0=ALU.mult, op1=ALU.subtract))(cur, dd, a, wdw),
                   r=[BPTMP, BPT], w=[BPTMP])
            if t == 0:
                self.V((lambda cur_, dd_, g_: lambda e: e.tensor_tensor(out=dd_[:, 0:16], in0=cur_[:, 16:32],
                                                                         in1=self.C["invcnt"][:, g_ * 16:(g_ + 1) * 16], op=ALU.mult))(cur, dd, g),
                       r=[BPTMP, self.BC], w=[BPTMP])
                self.V((lambda dd_, a_: lambda e: e.tensor_tensor(out=dd_[:, 0:16], in0=dd_[:, 0:16], in1=a_[:, 16:32], op=ALU.subtract))(dd, a),
                       r=[BPTMP, BPT], w=[BPTMP])
            db = ATT
            self.A((lambda dd_: lambda e: e.copy(out=db[:, 0:N], in_=dd_))(dd), r=[BPTMP], w=[BATT])
            p, bp = self.ps()
            self.mm(p[:, 0:N], self.POOLW[:, (l * 4 + g) * 128:(l * 4 + g + 1) * 128], db[:, 0:N], True, True, r=[self.BC, BATT], w=[bp])
            self.V((lambda g_, p_: lambda e: e.tensor_scalar(out=MIX[:, g_ * N:(g_ + 1) * N], in0=p_[:, 0:N],
                                                              scalar1=self.PSC[:, l * 4 + g_:l * 4 + g_ + 1], scalar2=None,
                                                              op0=ALU.mult))(g, p), r=[bp, self.BC], w=[BMIX[0]])
        if self.cfg.get("cut") == 4:
            S.alias(new, [self.BACT]); self.store_y_tile(t, X, BX, N, self.y_d[t * N:(t + 1) * N, :]); return
        for h in range(4):
            p, bp = self.proj_fm(l, W, 512 + h * 128, XB, N, BXB)
            self.evac(UT[:, h * N:(h + 1) * N], p[:, 0:N], r=[bp], w=[BUT])
        for h in range(4):
            w, bw = self.load_w(W[l, :, 1024 + h * 128:1024 + (h + 1) * 128], 16)
            p, bp = self.ps()
            for tb in range(4):
                for k in range(KC):
                    self.mm(p[:, tb * 128:(tb + 1) * 128], XB[:, k * N + tb * 128:k * N + (tb + 1) * 128], w[:, k * 128:(k + 1) * 128],
                            k == 0, k == KC - 1, r=[bw, BXB], w=[bp])
            for tb in range(4):
                self.evac(VSG[:, tb * 512 + h * 128:tb * 512 + (h + 1) * 128], p[:, tb * 128:(tb + 1) * 128], r=[bp], w=[BVSG])
        for h in range(4):
            p, bp = self.ps()
            for tb in range(4):
                self.mm(p[:, tb * 128:(tb + 1) * 128], VSG[:, tb * 512 + h * 128:tb * 512 + (h + 1) * 128],
                        self.SGUW[:, (l * 4 + h) * 128:(l * 4 + h + 1) * 128], True, True, r=[BVSG, self.BC], w=[bp])
            for tb in range(4):
                self.V((lambda tb_, h_, p_: lambda e: e.tensor_tensor(out=TT[:, tb_ * 128:(tb_ + 1) * 128], in0=p_[:, tb_ * 128:(tb_ + 1) * 128],
                                                                      in1=self.SGUB[:, (l * 4 + h_) * 128:(l * 4 + h_ + 1) * 128], op=ALU.add))(tb, h, p),
                       r=[bp, self.BC], w=[BTT])
            self.V((lambda h_: lambda e: e.tensor_tensor(out=MIX[:, (4 + h_) * N:(5 + h_) * N], in0=TT[:, 0:N], in1=UT[:, h_ * N:(h_ + 1) * N],
                                                         op=ALU.mult))(h), r=[BTT, BUT], w=[BMIX[1]])
        if self.cfg.get("cut") == 5:
            S.alias(new, [self.BACT]); self.store_y_tile(t, X, BX, N, self.y_d[t * N:(t + 1) * N, :]); return
        SEQ = self.SEQ
        for h in range(4):
            p, bp = self.proj_fm(l, W, 1536 + h * 128, XB, N, BXB)
            self.evac(QT[:, h * N:(h + 1) * N], p[:, 0:N], r=[bp], w=[BQT])
            p, bp = self.proj_fm(l, W, 2048 + h * 128, XB, N, BXB)
            self.evac(self.KT[:, h * SEQ + t * N:h * SEQ + (t + 1) * N], p[:, 0:N], r=[bp], w=[self.BKT])
        for grp, dst_d in ((2048, self.nk_d), (2560, self.nv_d)):
            for h in range(4):
                w, bw = self.load_w(W[l, :, grp + h * 128:grp + (h + 1) * 128], 16)
                p, bp = self.ps()
                for tb in range(4):
                    for k in range(KC):
                        self.mm(p[:, tb * 128:(tb + 1) * 128], XB[:, k * N + tb * 128:k * N + (tb + 1) * 128], w[:, k * 128:(k + 1) * 128],
                                k == 0, k == KC - 1, r=[bw, BXB], w=[bp])
                self.evac(DM[:, 0:N], p[:, 0:N], r=[bp], w=[BDM])
                if grp == 2560:
                    for tb in range(4):
                        blk = t * 4 + tb
                        self.A((lambda tb_, blk_, h_: lambda e: e.copy(out=self.VS[:, blk_ * 512 + h_ * 128:blk_ * 512 + (h_ + 1) * 128],
                                                                       in_=DM[:, tb_ * 128:(tb_ + 1) * 128]))(tb, blk, h), r=[BDM], w=[self.BVS])
                S.dma("scalar", (lambda dst_, h_: lambda e: e.dma_start(
                    out=dst_[l, t * N:(t + 1) * N, h_ * 128:(h_ + 1) * 128].rearrange("(a p) c -> p a c", p=128),
                    in_=DM[:, 0:N].rearrange("p (a c) -> p a c", a=4)))(dst_d, h), r=[BDM], is_output=True)
        scale = 128.0 ** -0.5
        for h in range(4):
            po, bpo = self.ps(hold=True)
            nkb = (t + 1) * 4
            first = True
            for kc in range(nkb - 1, -1, -1):
                diag = kc >= t * 4
                jloc = kc - t * 4
                pz, bpz = self.ps()
                self.mm(pz[:, 0:N], self.KT[:, h * SEQ + kc * 128:h * SEQ + (kc + 1) * 128], QT[:, h * N:(h + 1) * N], True, True,
                        r=[self.BKT, BQT], w=[bpz])
                bias = self.SBB[:, l * 4 + h:l * 4 + h + 1]
                self.A((lambda pz_, bias_: lambda e: e.activation(out=ZS[:, 0:N], in_=pz_[:, 0:N], func=AF.Identity, bias=bias_, scale=scale))(pz, bias),
                       r=[bpz, self.BC], w=[BZS])
                self.A(lambda e: e.activation(out=SP[:, 0:N], in_=ZS[:, 0:N], func=AF.Exp), r=[BZS], w=[BSP])
                self.A(lambda e: e.activation(out=SP[:, 0:N], in_=SP[:, 0:N], func=AF.Ln, bias=1.0), r=[BSP], w=[BSP])
                if diag:
                    self.V((lambda j_: lambda e: e.tensor_tensor(out=SP[:, 0:N], in0=SP[:, 0:N], in1=self.C["sbmask"][:, j_ * N:(j_ + 1) * N],
                                                                 op=ALU.mult))(jloc), r=[BSP, self.BC], w=[BSP])
                pst, bpst = self.ps()
                self.mm(pst[:, 0:N], self.C["u_ge"][:, :], SP[:, 0:N], True, True, r=[self.BC, BSP], w=[bpst])
                self.V((lambda pst_: lambda e: e.tensor_tensor(out=TT[:, 0:N], in0=ZS[:, 0:N], in1=pst_[:, 0:N], op=ALU.subtract))(pst),
                       r=[BZS, bpst], w=[BTT])
                if not first:
                    self.V(lambda e: e.tensor_tensor(out=TT[:, 0:N], in0=TT[:, 0:N], in1=RR[:, 0:N], op=ALU.subtract), r=[BTT, BRR], w=[BTT])
                self.A(lambda e: e.activation(out=TT[:, 0:N], in_=TT[:, 0:N], func=AF.Exp), r=[BTT], w=[BTT])
                if diag:
                    self.V((lambda j_: lambda e: e.tensor_tensor(out=ATT[:, 0:N], in0=TT[:, 0:N], in1=self.C["sbmask"][:, j_ * N:(j_ + 1) * N],
                                                                 op=ALU.mult))(jloc), r=[BTT, self.BC], w=[BATT])
                else:
                    self.V(lambda e: e.tensor_copy(out=ATT[:, 0:N], in_=TT[:, 0:N]), r=[BTT], w=[BATT])
                self.mm(po[:, 0:N], self.VS[:, kc * 512 + h * 128:kc * 512 + (h + 1) * 128], ATT[:, 0:N], first, kc == 0,
                        r=[self.BVS, BATT], w=[bpo], sig=True)
                if kc > 0:
                    pcs, bpcs = self.ps()
                    self.mm(pcs[:, 0:N], self.ONESF[:, :], SP[:, 0:N], True, True, r=[self.BC, BSP], w=[bpcs])
                    if first:
                        self.V((lambda pcs_: lambda e: e.tensor_copy(out=RR[:, 0:N], in_=pcs_[:, 0:N]))(pcs), r=[bpcs], w=[BRR])
                    else:
                        self.V((lambda pcs_: lambda e: e.tensor_tensor(out=RR[:, 0:N], in0=RR[:, 0:N], in1=pcs_[:, 0:N], op=ALU.add))(pcs),
                               r=[bpcs, BRR], w=[BRR])
                first = False
            self.evac(MIX[:, (8 + h) * N:(9 + h) * N], po[:, 0:N], r=[bpo], w=[BMIX[2]])
            self.ps_release(po)
        if t == self.NT - 1:
            CV = UT
            for c4 in range(3):
                p, bp = self.ps()
                for cc in range(4):
                    c0 = 3072 + (c4 * 4 + cc) * 128
                    w, bw = self.load_w(W[l, :, c0:c0 + 128], 16)
                    for k in range(KC):
                        self.mm(p[:, cc * 128:(cc + 1) * 128], XB[:, k * N + N - 128:k * N + N], w[:, k * 128:(k + 1) * 128],
                                k == 0, k == KC - 1, r=[bw, BXB], w=[bp])
                self.evac(CV[:, c4 * 512:(c4 + 1) * 512], p[:, 0:512], r=[bp], w=[BUT])
            S.dma("scalar", lambda e: e.dma_start(out=self.nconv_d[l], in_=CV[125:128, 0:1536]), r=[BUT], is_output=True)
        en = self.cfg.get("mixers", "pool,sgu,sb,dn")
        if "dn" in en:
            try:
                self.deltanet_prompt(l, t, new)
            except StopIteration:
                self.V(lambda e: e.memset(MIX[:, 12 * N:16 * N], 0.0), w=[BMIX[3]])
        else:
            self.V(lambda e: e.memset(MIX[:, 12 * N:16 * N], 0.0), w=[BMIX[3]])
        for gi, nm in enumerate(("pool", "sgu", "sb")):
            if nm not in en:
                self.V((lambda gi_: lambda e: e.memset(MIX[:, gi_ * 4 * N:(gi_ + 1) * 4 * N], 0.0))(gi), w=[BMIX[gi]])
        S.alias(new + getattr(self, "dn_bufs", []), [self.BACT])
        self.wout_res(l, X, N, BX, MIX, BMIX)
        self.layernorm(l, 1, X, XB, N, BX, BXB)
        self.ffn(l, 1, X, XB, N, BX, BXB)
        self.layernorm(l, 2, X, XB, N, BX, BXB)
        if l == L - 1:
            self.store_y_tile(t, X, BX, N, self.y_d[t * N:(t + 1) * N, :])
        else:
            S.dma("scalar", lambda e: e.dma_start(out=self.ysc[t], in_=X[:, :]), r=[BX], w=[self.YSB[t]])


    def deltanet_prompt(self, l, t, old_bufs):
        S, N, L = self.S, self.N, self.L
        XB, BXB = self.XB, self.BXB
        MIX, BMIX = self.MIXT, self.BMIX
        W = self.win
        ar = self.ARENA
        off = [0]
        bufs = []

        def al(n, name):
            a = ar[:, off[0]:off[0] + n]
            off[0] += n
            assert off[0] <= 11264, off[0]
            b = Buf(name)
            bufs.append(b)
            return a, b
        NR = N + 16
        RAW, BRAW = al(3 * NR, "RAW")
        QKV, BQKV = al(3 * N, "QKV")
        OT, BOT = al(N, "OT")
        GB, BGB = al(N, "GB")
        BBR, BBBR = al(N, "BBR")
        SQ, BSQ = al(N, "SQ")
        RI, BRI = al(N, "RI")
        COLS, BCOLS = al(64, "COLS")
        SC, BSC = al(16, "SC")
        names = ["gbc", "Gb", "Dm", "E", "ET", "A0", "B0", "A1", "B1", "X0", "X1", "bv", "kbg", "kdec", "nwT", "u", "qkT", "eGb", "qdT", "tmp"]
        T_ = {}
        for nm in names:
            T_[nm] = al(128, nm)
        self.dn_bufs = bufs
        S.alias(old_bufs, bufs)
        C = self.C
        ident = C["ident"]
        SST, BSST = self.SST, self.BSST
        if t == 0:
            self.V(lambda e: e.memset(SST[:, :], 0.0), w=[BSST])
            self.V(lambda e: e.memset(self.CHIST[:, :], 0.0), w=[self.BCH])
        S.dma("scalar", lambda e: e.dma_start(out=self.WSTG[0][:, 0:128].rearrange("p (k c) -> p k c", k=16),
                                               in_=W[l, :, 5120:5128].rearrange("(k p) c -> p k c", p=128)), w=[self.WSTGB[0]])
        self.V(lambda e: e.tensor_copy(out=self.W8[:, :], in_=self.WSTG[0][:, 0:128]), r=[self.WSTGB[0]], w=[self.BW8])
        p, bp = self.ps()
        for tb in range(4):
            for k in range(KC):
                self.mm(p[:, tb * 8:(tb + 1) * 8], XB[:, k * N + tb * 128:k * N + (tb + 1) * 128], self.W8[:, k * 8:(k + 1) * 8],
                        k == 0, k == KC - 1, r=[BXB, self.BW8], w=[bp])
        for tb in range(4):
            self.A((lambda tb_, p_: lambda e: e.activation(out=COLS[:, 16 + tb_ * 4:16 + tb_ * 4 + 4], in_=p_[:, tb_ * 8:tb_ * 8 + 4], func=AF.Sigmoid))(tb, p),
                   r=[bp], w=[BCOLS])
            self.V((lambda tb_, p_: lambda e: e.tensor_tensor(out=COLS[:, 32 + tb_ * 4:32 + tb_ * 4 + 4], in0=p_[:, tb_ * 8 + 4:tb_ * 8 + 8],
                                                               in1=self.DTB[:, l * 4:l * 4 + 4], op=ALU.add))(tb, p), r=[bp, self.BC], w=[BCOLS])
        self.A(lambda e: e.activation(out=COLS[:, 32:48], in_=COLS[:, 32:48], func=AF.Exp), r=[BCOLS], w=[BCOLS])
        self.A(lambda e: e.activation(out=COLS[:, 32:48], in_=COLS[:, 32:48], func=AF.Ln, bias=1.0), r=[BCOLS], w=[BCOLS])
        for tb in range(4):
            self.V((lambda tb_: lambda e: e.tensor_tensor(out=COLS[:, tb_ * 4:tb_ * 4 + 4], in0=COLS[:, 32 + tb_ * 4:32 + tb_ * 4 + 4],
                                                          in1=self.NEGA[:, l * 4:l * 4 + 4], op=ALU.mult))(tb), r=[BCOLS, self.BC], w=[BCOLS])
        if self.cfg.get("dncut") == 1: raise StopIteration
        for h in range(4):
            Sh = SST[:, h * 128:(h + 1) * 128]
            for gi in range(3):
                c0 = 3072 + gi * 512 + h * 128
                p, bp = self.proj_fm(l, W, c0, XB, N, BXB)
                self.evac(RAW[:, gi * NR + 16:gi * NR + 16 + N], p[:, 0:N], r=[bp], w=[BRAW])
                hc = (gi * 4 + h) * 4
                self.V((lambda gi_, hc_: lambda e: e.tensor_copy(out=RAW[:, gi_ * NR + 13:gi_ * NR + 16], in_=self.CHIST[:, hc_:hc_ + 3]))(gi, hc),
                       r=[self.BCH], w=[BRAW])
                self.V((lambda gi_, hc_: lambda e: e.tensor_copy(out=self.CHIST[:, hc_:hc_ + 3], in_=RAW[:, gi_ * NR + 13 + N:gi_ * NR + 16 + N]))(gi, hc),
                       r=[BRAW], w=[self.BCH])
                cch = gi * 4 + h
                dst = QKV[:, gi * N:(gi + 1) * N]
                for j in range(4):
                    wcol = self.CW[:, (l * 4 + j) * 12 + cch:(l * 4 + j) * 12 + cch + 1]
                    srcj = RAW[:, gi * NR + 13 + j:gi * NR + 13 + j + N]
                    if j == 0:
                        self.V((lambda d_, s_, w_: lambda e: e.tensor_scalar(out=d_, in0=s_, scalar1=w_, scalar2=None, op0=ALU.mult))(dst, srcj, wcol),
                               r=[BRAW, self.BC], w=[BQKV])
                    else:
                        self.V((lambda d_, s_, w_: lambda e: e.scalar_tensor_tensor(out=d_, in0=s_, scalar=w_, in1=d_, op0=ALU.mult, op1=ALU.add))(dst, srcj, wcol),
                               r=[BRAW, self.BC, BQKV], w=[BQKV])
                self.A((lambda d_: lambda e: e.activation(out=d_, in_=d_, func=AF.Silu))(dst), r=[BQKV], w=[BQKV])
                if gi < 2:
                    self.A((lambda d_: lambda e: e.activation(out=SQ[:, 0:N], in_=d_, func=AF.Square))(dst), r=[BQKV], w=[BSQ])
                    pn, bpn = self.ps()
                    self.mm(pn[:, 0:N], self.ONESF[:, :], SQ[:, 0:N], True, True, r=[self.BC, BSQ], w=[bpn])
                    self.A((lambda pn_: lambda e: e.activation(out=RI[:, 0:N], in_=pn_[:, 0:N], func=AF.Sqrt, bias=self.EPSC[:, 1:2]))(pn),
                           r=[bpn, self.BC], w=[BRI])
                    self.V(lambda e: e.reciprocal(out=RI[:, 0:N], in_=RI[:, 0:N]), r=[BRI], w=[BRI])
                    sc = (128.0 ** -0.5) if gi == 0 else 1.0
                    self.V((lambda d_, sc_: lambda e: e.scalar_tensor_tensor(out=d_, in0=d_, scalar=sc_, in1=RI[:, 0:N], op0=ALU.mult, op1=ALU.mult))(dst, sc),
                           r=[BQKV, BRI], w=[BQKV])
            if self.cfg.get("dncut") == 2: raise StopIteration
            for tb in range(4):
                cs0 = tb * 128
                qT = QKV[:, 0 * N + cs0:0 * N + cs0 + 128]
                kT = QKV[:, 1 * N + cs0:1 * N + cs0 + 128]
                vT = QKV[:, 2 * N + cs0:2 * N + cs0 + 128]
                gcol = COLS[:, tb * 4 + h:tb * 4 + h + 1]
                bcol = COLS[:, 16 + tb * 4 + h:16 + tb * 4 + h + 1]
                (gbc, Bgbc), (Gb, BGb), (Dm, BDm), (E, BE), (ET, BET) = T_["gbc"], T_["Gb"], T_["Dm"], T_["E"], T_["ET"]
                (tmp, Btmp) = T_["tmp"]
                self.V(lambda e, gcol=gcol, bcol=bcol, qT=qT, kT=kT, vT=vT: e.tensor_scalar(out=gbc, in0=self.ONESF[:, :], scalar1=gcol, scalar2=None, op0=ALU.mult), r=[self.BC, BCOLS], w=[Bgbc])
                p, bp = self.ps()
                self.mm(p[:, 0:128], C["u_le"][:, :], gbc, True, True, r=[self.BC, Bgbc], w=[bp])
                self.mm(p[:, 128:256], gbc, C["u_le"][:, :], True, True, r=[self.BC, Bgbc], w=[bp])
                self.A((lambda p_: lambda e, gcol=gcol, bcol=bcol, qT=qT, kT=kT, vT=vT: e.copy(out=Gb, in_=p_[:, 128:256]))(p), r=[bp], w=[BGb])
                self.A((lambda p_: lambda e, gcol=gcol, bcol=bcol, qT=qT, kT=kT, vT=vT: e.copy(out=SC[:, 0:1], in_=p_[:, 0:1]))(p), r=[bp], w=[BSC])
                self.V((lambda p_: lambda e, gcol=gcol, bcol=bcol, qT=qT, kT=kT, vT=vT: e.tensor_tensor(out=Dm, in0=p_[:, 0:128], in1=Gb, op=ALU.subtract))(p), r=[bp, BGb], w=[BDm])
                self.V(lambda e, gcol=gcol, bcol=bcol, qT=qT, kT=kT, vT=vT: e.tensor_scalar(out=E, in0=Dm, scalar1=0.0, scalar2=None, op0=ALU.min), r=[BDm], w=[BE])
                self.A(lambda e, gcol=gcol, bcol=bcol, qT=qT, kT=kT, vT=vT: e.activation(out=E, in_=E, func=AF.Exp), r=[BE], w=[BE])
                self.V(lambda e, gcol=gcol, bcol=bcol, qT=qT, kT=kT, vT=vT: e.tensor_scalar(out=ET, in0=Dm, scalar1=0.0, scalar2=None, op0=ALU.max), r=[BDm], w=[BET])
                self.A(lambda e, gcol=gcol, bcol=bcol, qT=qT, kT=kT, vT=vT: e.activation(out=ET, in_=ET, func=AF.Exp, scale=-1.0), r=[BET], w=[BET])
                self.A(lambda e, gcol=gcol, bcol=bcol, qT=qT, kT=kT, vT=vT: e.activation(out=SC[:, 1:2], in_=SC[:, 0:1], func=AF.Exp), r=[BSC], w=[BSC])
                self.V(lambda e, gcol=gcol, bcol=bcol, qT=qT, kT=kT, vT=vT: e.tensor_tensor(out=SC[:, 2:3], in0=SC[:, 1:2], in1=bcol, op=ALU.mult), r=[BSC, BCOLS], w=[BSC])
                self.A(lambda e, gcol=gcol, bcol=bcol, qT=qT, kT=kT, vT=vT: e.activation(out=SC[:, 3:4], in_=SC[:, 0:1], func=AF.Exp, scale=-1.0, bias=Gb[:, 127:128]), r=[BSC, BGb], w=[BSC])
                self.A(lambda e, gcol=gcol, bcol=bcol, qT=qT, kT=kT, vT=vT: e.activation(out=SC[:, 4:5], in_=Gb[:, 127:128], func=AF.Exp), r=[BGb], w=[BSC])
                if self.cfg.get("dncut") == 3: raise StopIteration
                (A0, BA0), (B0, BB0), (A1, BA1), (B1, BB1) = T_["A0"], T_["B0"], T_["A1"], T_["B1"]
                (X0, BX0), (X1, BX1) = T_["X0"], T_["X1"]
                p, bp = self.ps()
                self.mm(p[:, 0:128], kT, kT, True, True, r=[BQKV], w=[bp])
                self.V((lambda p_: lambda e, gcol=gcol, bcol=bcol, qT=qT, kT=kT, vT=vT: e.tensor_tensor(out=tmp, in0=p_[:, 0:128], in1=E, op=ALU.mult))(p), r=[bp, BE], w=[Btmp])
                self.V(lambda e, gcol=gcol, bcol=bcol, qT=qT, kT=kT, vT=vT: e.scalar_tensor_tensor(out=A0, in0=tmp, scalar=bcol, in1=C["m_low"][:, :], op0=ALU.mult, op1=ALU.mult),
                       r=[Btmp, BCOLS, self.BC], w=[BA0])
                self.tr(p[:, 128:256], A0, ident[:, :], r=[BA0, self.BC], w=[bp])
                self.A((lambda p_: lambda e, gcol=gcol, bcol=bcol, qT=qT, kT=kT, vT=vT: e.copy(out=B0, in_=p_[:, 128:256]))(p), r=[bp], w=[BB0])
                self.V(lambda e, gcol=gcol, bcol=bcol, qT=qT, kT=kT, vT=vT: e.tensor_tensor(out=X0, in0=ident[:, :], in1=B0, op=ALU.subtract), r=[self.BC, BB0], w=[BX0])
                if self.cfg.get("dncut") == 4: raise StopIteration
                PA, BPA, PB, BPB = A0, BA0, B0, BB0
                NA, BNA, NB_, BNB = A1, BA1, B1, BB1
                XC, BXC, XN, BXN = X0, BX0, X1, BX1
                for step in range(self.cfg.get("nsteps", 6)):
                    p, bp = self.ps()
                    self.mm(p[:, 0:128], PB, PA, True, True, r=[BPA, BPB], w=[bp])
                    if step < 5:
                        self.mm(p[:, 128:256], PA, PB, True, True, r=[BPA, BPB], w=[bp])
                    self.V((lambda p_, d_: lambda e, gcol=gcol, bcol=bcol, qT=qT, kT=kT, vT=vT: e.tensor_copy(out=d_, in_=p_[:, 0:128]))(p, NA), r=[bp], w=[BNA])
                    if step < 5:
                        self.V((lambda p_, d_: lambda e, gcol=gcol, bcol=bcol, qT=qT, kT=kT, vT=vT: e.tensor_copy(out=d_, in_=p_[:, 128:256]))(p, NB_), r=[bp], w=[BNB])
                    self.mm(p[:, 256:384], NA, XC, True, True, r=[BNA, BXC], w=[bp])
                    self.V((lambda p_, d_, s_: lambda e, gcol=gcol, bcol=bcol, qT=qT, kT=kT, vT=vT: e.tensor_tensor(out=d_, in0=p_[:, 256:384], in1=s_, op=ALU.add))(p, XN, XC), r=[bp, BXC], w=[BXN])
                    PA, BPA, NA, BNA = NA, BNA, PA, BPA
                    PB, BPB, NB_, BNB = NB_, BNB, PB, BPB
                    XC, BXC, XN, BXN = XN, BXN, XC, BXC
                TT_, BTT_ = XC, BXC
                if self.cfg.get("dncut") == 5: raise StopIteration
                (bv, Bbv), (kbg, Bkbg), (kdec, Bkdec), (nwT, BnwT), (u, Bu) = T_["bv"], T_["kbg"], T_["kdec"], T_["nwT"], T_["u"]
                (qkT, BqkT), (eGb, BeGb), (qdT, BqdT) = T_["qkT"], T_["eGb"], T_["qdT"]
                p, bp = self.ps()
                self.tr(p[:, 0:128], kT, ident[:, :], r=[BQKV, self.BC], w=[bp])
                self.tr(p[:, 128:256], vT, ident[:, :], r=[BQKV, self.BC], w=[bp])
                self.V((lambda p_: lambda e, gcol=gcol, bcol=bcol, qT=qT, kT=kT, vT=vT: e.tensor_scalar(out=bv, in0=p_[:, 128:256], scalar1=bcol, scalar2=None, op0=ALU.mult))(p), r=[bp, BCOLS], w=[Bbv])
                self.V((lambda p_: lambda e, gcol=gcol, bcol=bcol, qT=qT, kT=kT, vT=vT: e.tensor_scalar(out=kbg, in0=p_[:, 0:128], scalar1=SC[:, 2:3], scalar2=None, op0=ALU.mult))(p), r=[bp, BSC], w=[Bkbg])
                self.V((lambda p_: lambda e, gcol=gcol, bcol=bcol, qT=qT, kT=kT, vT=vT: e.tensor_scalar(out=kdec, in0=p_[:, 0:128], scalar1=SC[:, 3:4], scalar2=None, op0=ALU.mult))(p), r=[bp, BSC], w=[Bkdec])
                p, bp = self.ps()
                self.mm(p[:, 0:128], kbg, TT_, True, True, r=[Bkbg, BTT_], w=[bp])
                self.A((lambda p_: lambda e, gcol=gcol, bcol=bcol, qT=qT, kT=kT, vT=vT: e.mul(out=nwT, in_=p_[:, 0:128], mul=-1.0))(p), r=[bp], w=[BnwT])
                self.mm(p[:, 128:256], TT_, bv, True, False, r=[BTT_, Bbv], w=[bp], sig=True)
                self.mm(p[:, 128:256], nwT, Sh, False, True, r=[BnwT, BSST], w=[bp])
                self.V((lambda p_: lambda e, gcol=gcol, bcol=bcol, qT=qT, kT=kT, vT=vT: e.tensor_copy(out=u, in_=p_[:, 128:256]))(p), r=[bp], w=[Bu])
                if self.cfg.get("dncut") == 6: raise StopIteration
                p, bp = self.ps()
                self.mm(p[:, 0:128], kT, qT, True, True, r=[BQKV], w=[bp])
                self.V((lambda p_: lambda e, gcol=gcol, bcol=bcol, qT=qT, kT=kT, vT=vT: e.tensor_tensor(out=tmp, in0=p_[:, 0:128], in1=ET, op=ALU.mult))(p), r=[bp, BET], w=[Btmp])
                self.V(lambda e, gcol=gcol, bcol=bcol, qT=qT, kT=kT, vT=vT: e.tensor_tensor(out=qkT, in0=tmp, in1=C["m_upi"][:, :], op=ALU.mult), r=[Btmp, self.BC], w=[BqkT])
                self.A(lambda e, gcol=gcol, bcol=bcol, qT=qT, kT=kT, vT=vT: e.activation(out=eGb, in_=Gb, func=AF.Exp), r=[BGb], w=[BeGb])
                self.V(lambda e, gcol=gcol, bcol=bcol, qT=qT, kT=kT, vT=vT: e.tensor_tensor(out=qdT, in0=qT, in1=eGb, op=ALU.mult), r=[BQKV, BeGb], w=[BqdT])
                self.mm(p[:, 128:256], Sh, qdT, True, False, r=[BSST, BqdT], w=[bp], sig=True)
                self.mm(p[:, 128:256], u, qkT, False, True, r=[Bu, BqkT], w=[bp])
                self.evac(OT[:, cs0:cs0 + 128], p[:, 128:256], r=[bp], w=[BOT])
                self.mm(p[:, 256:384], kdec, u, True, True, r=[Bkdec, Bu], w=[bp])
                self.V((lambda p_, Sh_: lambda e, gcol=gcol, bcol=bcol, qT=qT, kT=kT, vT=vT: e.scalar_tensor_tensor(out=Sh_, in0=Sh_, scalar=SC[:, 4:5], in1=p_[:, 256:384], op0=ALU.mult, op1=ALU.add))(p, Sh),
                       r=[BSST, BSC, bp], w=[BSST])
            self.A(lambda e: e.activation(out=SQ[:, 0:N], in_=OT[:, 0:N], func=AF.Square), r=[BOT], w=[BSQ])
            pn, bpn = self.ps()
            self.mm(pn[:, 0:N], self.ONESF[:, :], SQ[:, 0:N], True, True, r=[self.BC, BSQ], w=[bpn])
            self.A((lambda pn_: lambda e: e.activation(out=RI[:, 0:N], in_=pn_[:, 0:N], func=AF.Sqrt, bias=self.EPSC[:, 1:2], scale=1.0 / 128.0))(pn),
                   r=[bpn, self.BC], w=[BRI])
            self.V(lambda e: e.reciprocal(out=RI[:, 0:N], in_=RI[:, 0:N]), r=[BRI], w=[BRI])
            self.V(lambda e: e.scalar_tensor_tensor(out=OT[:, 0:N], in0=OT[:, 0:N], scalar=self.NG[:, l:l + 1], in1=RI[:, 0:N], op0=ALU.mult, op1=ALU.mult),
                   r=[BOT, BRI, self.BC], w=[BOT])
            pz, bpz = self.proj_fm(l, W, 4608 + h * 128, XB, N, BXB)
            self.A((lambda pz_: lambda e: e.activation(out=SQ[:, 0:N], in_=pz_[:, 0:N], func=AF.Silu))(pz), r=[bpz], w=[BSQ])
            self.V((lambda h_: lambda e: e.tensor_tensor(out=MIX[:, (12 + h_) * N:(13 + h_) * N], in0=OT[:, 0:N], in1=SQ[:, 0:N], op=ALU.mult))(h),
                   r=[BOT, BSQ], w=[BMIX[3]])
        if t == self.NT - 1:
            S.dma("scalar", lambda e: e.dma_start(out=self.ndelta_d[l].rearrange("h k v -> k h v"), in_=SST[:, :].rearrange("p (h v) -> p h v", h=4)),
                  r=[BSST], is_output=True)

    def sample_tile(self, l):
        S, NS, L = self.S, self.NS, self.L
        X, XB, BX, BXB = self.XS, self.XSB, self.BXS, self.BXSB
        N = NS
        if l == 0:
            stg = self.ARENA[:, 0:D]
            S.dma("scalar", lambda e: e.dma_start(out=stg[0:NS, :], in_=self.xs_d), w=[self.BACT])
            for c in range(KC):
                p, bp = self.ps()
                self.tr(p[:, 0:NS], stg[0:NS, c * 128:(c + 1) * 128], self.C["ident"][0:NS, 0:NS], r=[self.BACT, self.BC], w=[bp])
                self.evac(X[:, c * N:(c + 1) * N], p[:, 0:N], r=[bp], w=[BX])
            self.V(lambda e: e.tensor_copy(out=XB[:, :], in_=X[:, :]), r=[BX], w=[BXB])
        self.ffn(l, 0, X, XB, N, BX, BXB)
        self.layernorm(l, 0, X, XB, N, BX, BXB)
        MIX, BMIX = self.MIXS, self.BMIXS
        self.V(lambda e: e.memset(MIX[:, :], 0.0), w=list(BMIX))
        PRS = self.ARENA[0:NS, 0:5120]
        W = self.win
        for c4 in range(10):
            p, bp = self.ps()
            for cc in range(4):
                c0 = (c4 * 4 + cc) * 128
                w, bw = self.load_w(W[l, :, c0:c0 + 128], 16)
                for k in range(KC):
                    self.mm(p[0:NS, cc * 128:(cc + 1) * 128], XB[:, k * N:(k + 1) * N], w[:, k * 128:(k + 1) * 128],
                            k == 0, k == KC - 1, r=[bw, BXB], w=[bp])
            self.evac(PRS[:, c4 * 512:(c4 + 1) * 512], p[0:NS, 0:512], r=[bp], w=[self.BACT])
        o = lambda fn: S.dma("scalar", fn, r=[self.BACT], is_output=True)
        o(lambda e: e.dma_start(out=self.nks_d[l], in_=PRS[:, 2048:2560]))
        o(lambda e: e.dma_start(out=self.nvs_d[l], in_=PRS[:, 2560:3072]))
        o(lambda e: e.dma_start(out=self.nsgus_d[l], in_=PRS[:, 1024:1536]))
        o(lambda e: e.dma_start(out=self.npools_d[l, :, 14, :], in_=PRS[:, 0:512]))
        o(lambda e: e.dma_start(out=self.nconvs_d[l, :, 2, :], in_=PRS[:, 3072:4608]))
        S.dma("scalar", lambda e: e.dma_start(out=self.npools_d[l, :, 0:14, :], in_=self.spool_d[l, :, 1:15, :]), is_output=True)
        S.dma("scalar", lambda e: e.dma_start(out=self.nconvs_d[l, :, 0:2, :], in_=self.sconv_d[l, :, 1:3, :]), is_output=True)
        if self.stage >= 3:
            self.sample_mixers(l)
        self.wout_res(l, X, N, BX, MIX, BMIX)
        self.layernorm(l, 1, X, XB, N, BX, BXB)
        self.ffn(l, 1, X, XB, N, BX, BXB)
        self.layernorm(l, 2, X, XB, N, BX, BXB)
        if l == L - 1:
            self.store_y_tile(0, X, BX, N, self.ys_d)


    def sample_mixers(self, l):
        S, NS, L, NPG = self.S, self.NS, self.L, self.NPG
        XB, BXB = self.XSB, self.BXSB
        MIX, BMIX = self.MIXS, self.BMIXS
        W = self.win
        C = self.C
        ident = C["ident"]
        N = NS
        ar = self.ARENA
        off = [5120]
        PRS = ar[0:NS, 0:5120]

        allb = []

        def al(n, name):
            a = ar[:, off[0]:off[0] + n]
            off[0] += n
            assert off[0] <= 11264, off[0]
            b = Buf(name)
            S.alias([self.BACT], [b])
            allb.append(b)
            return a, b
        en = self.cfg.get("smixers", "pool,sgu,sb,dn")
        STG, BSTG = al(512, "p_stg")
        STT, BSTT = al(4 * 240, "p_stt")
        SM, BSM = al(64, "p_sm")
        DB_, BDB = al(32, "p_db")
        DBb = DB_[:, :].bitcast(BF16)
        for half in range(2):
            S.dma("scalar", (lambda hf: lambda e: e.dma_start(
                out=STG[0:120, 0:512], in_=self.spool_d[l, hf * 8:(hf + 1) * 8].rearrange("s r c -> (s r) c")))(half), w=[BSTG])
            p, bp = self.ps()
            for g in range(4):
                self.tr(p[:, g * 128:g * 128 + 120], STG[0:120, g * 128:(g + 1) * 128], ident[0:120, 0:120], r=[BSTG, self.BC], w=[bp])
            for g in range(4):
                self.V((lambda g_, hf, p_: lambda e: e.tensor_copy(out=STT[:, g_ * 240 + hf * 120:g_ * 240 + hf * 120 + 120],
                                                                     in_=p_[:, g_ * 128:g_ * 128 + 120]))(g, half, p), r=[bp], w=[BSTT])
        for g, wdw in enumerate(POOL_W):
            pa, bpa = self.proj_fm(l, W, g * 128, XB, N, BXB)
            view = STT[:, g * 240:(g + 1) * 240].rearrange("p (s r) -> p s r", r=15)
            self.V((lambda v_, w_, g_: lambda e: e.tensor_reduce(out=SM[:, g_ * 16:(g_ + 1) * 16], in_=v_[:, :, 15 - (w_ - 1):15],
                                                                  axis=AX.X, op=ALU.add))(view, wdw, g), r=[BSTT], w=[BSM])
            self.V((lambda g_, pa_: lambda e: e.tensor_tensor(out=SM[:, g_ * 16:(g_ + 1) * 16], in0=SM[:, g_ * 16:(g_ + 1) * 16],
                                                               in1=pa_[:, 0:N], op=ALU.add))(g, pa), r=[BSM, bpa], w=[BSM])
            self.V((lambda g_, pa_, w_: lambda e: e.scalar_tensor_tensor(out=DBb[:, g_ * 16:(g_ + 1) * 16], in0=SM[:, g_ * 16:(g_ + 1) * 16],
                                                                          scalar=1.0 / w_, in1=pa_[:, 0:N], op0=ALU.mult, op1=ALU.subtract))(g, pa, wdw),
                   r=[BSM, bpa], w=[BDB])
            p, bp = self.ps()
            self.mm(p[:, 0:N], self.POOLW[:, (l * 4 + g) * 128:(l * 4 + g + 1) * 128], DBb[:, g * 16:(g + 1) * 16], True, True, r=[self.BC, BDB], w=[bp])
            self.V((lambda g_, p_: lambda e: e.tensor_scalar(out=MIX[:, g_ * N:(g_ + 1) * N], in0=p_[:, 0:N],
                                                              scalar1=self.PSC[:, l * 4 + g_:l * 4 + g_ + 1], scalar2=None, op0=ALU.mult))(g, p),
                   r=[bp, self.BC], w=[BMIX[0]])
        TS, BTS = al(16, "s_t")
        for h in range(4):
            pv, bpv = self.proj_fm(l, W, 1024 + h * 128, XB, N, BXB)
            self.V((lambda h_, pv_: lambda e: e.tensor_scalar(out=TS[:, 0:N], in0=pv_[:, 0:N], scalar1=self.SW00[:, l * 4 + h_:l * 4 + h_ + 1],
                                                               scalar2=self.SGUB[:, (l * 4 + h_) * 128:(l * 4 + h_) * 128 + 1],
                                                               op0=ALU.mult, op1=ALU.add))(h, pv), r=[bpv, self.BC], w=[BTS])
            pu, bpu = self.proj_fm(l, W, 512 + h * 128, XB, N, BXB)
            self.V((lambda h_, pu_: lambda e: e.tensor_tensor(out=MIX[:, (4 + h_) * N:(5 + h_) * N], in0=pu_[:, 0:N], in1=TS[:, 0:N], op=ALU.mult))(h, pu),
                   r=[bpu, BTS], w=[BMIX[1]])
        mark = off[0]
        n_mark = len(allb)
        for gi_, nm in enumerate(("pool", "sgu")):
            if nm not in en:
                self.V((lambda g_: lambda e: e.memset(MIX[:, g_ * 4 * N:(g_ + 1) * 4 * N], 0.0))(gi_), w=[BMIX[gi_]])
        if "sb" in en:
            KP = [al(512, "kp%d" % i) for i in range(2)]
            VP = [al(512, "vp%d" % i) for i in range(2)]
            QB, BQB = al(512, "qb")
            PROD, BPROD = al(512, "prod")
            NZ = NPG * 4
            Z, BZ = al(NZ, "z"); ZS, BZS = al(NZ, "zs"); SP, BSP = al(NZ, "sp"); TOT, BTOT = al(NZ, "tot")
            R, BR = al(NZ, "r"); ATTs, BATTs = al(NZ, "att"); BIASR, BBIASR = al(NZ, "biasr")
            ATTP, BATTP = al(NPG * 16, "attp")
            for pg in range(NPG):
                self.V((lambda pg_: lambda e: e.tensor_copy(out=BIASR[:, pg_ * 4:(pg_ + 1) * 4], in_=self.SBB[:, l * 4:(l + 1) * 4]))(pg),
                       r=[self.BC], w=[BBIASR])
            if l == 0:
                self.V(lambda e: e.tensor_scalar(out=self.IDX[:, :], in0=self.PTAB[:, :], scalar1=7, scalar2=self.IOTA[:, 0:1],
                                                 op0=ALU.logical_shift_left, op1=ALU.bitwise_or), r=[self.BC], w=[self.BC])
            ck = self.ck_d[l]
            cv = self.cv_d[l]
            scale = 128.0 ** -0.5
            POH = [self.ps(hold=True) for _ in range(4)]
            gi = 0
            for s in range(NS):
                if s == 0:
                    S.dma("scalar", lambda e: e.dma_start(out=self.qscr, in_=PRS[:, 1536:2048]), r=[self.BACT], w=[self.BQS])
                S.dma("scalar", (lambda s_: lambda e: e.dma_start(out=QB[:, :], in_=self.qscr[s_:s_ + 1, :].partition_broadcast(128)))(s),
                      r=[self.BQS], w=[BQB])
                for pg in range(NPG):
                    kp, bkp = KP[gi % 2]
                    gi += 1
                    col = s * NPG + pg
                    S.dma("gpsimd", (lambda kp_, col_: lambda e: e.indirect_dma_start(
                        out=kp_[:, :], out_offset=None, in_=ck,
                        in_offset=bass.IndirectOffsetOnAxis(ap=self.IDX[:, col_:col_ + 1], axis=0)))(kp, col), r=[self.BC], w=[bkp])
                    self.V((lambda kp_: lambda e: e.tensor_tensor(out=PROD[:, :], in0=kp_[:, :], in1=QB[:, :], op=ALU.mult))(kp), r=[bkp, BQB], w=[BPROD])
                    self.V((lambda pg_: lambda e: e.tensor_reduce(out=Z[:, pg_ * 4:(pg_ + 1) * 4], in_=PROD[:, :].rearrange("p (h d) -> p h d", h=4),
                                                                  axis=AX.X, op=ALU.add))(pg), r=[BPROD], w=[BZ])
                self.V(lambda e: e.scalar_tensor_tensor(out=ZS[:, :], in0=Z[:, :], scalar=scale, in1=BIASR[:, :], op0=ALU.mult, op1=ALU.add),
                       r=[BZ, BBIASR], w=[BZS])
                self.A(lambda e: e.activation(out=SP[:, :], in_=ZS[:, :], func=AF.Exp), r=[BZS], w=[BSP])
                self.A(lambda e: e.activation(out=SP[:, :], in_=SP[:, :], func=AF.Ln, bias=1.0), r=[BSP], w=[BSP])
                pst, bpst = self.ps()
                self.mm(pst[:, 0:NZ], C["u_ge"][:, :], SP[:, :], True, True, r=[self.BC, BSP], w=[bpst])
                self.mm(pst[:, 128:128 + NZ], self.ONESF[:, :], SP[:, :], True, True, r=[self.BC, BSP], w=[bpst])
                self.V((lambda p_: lambda e: e.tensor_copy(out=TOT[:, :], in_=p_[:, 128:128 + NZ]))(pst), r=[bpst], w=[BTOT])
                self.V(lambda e: e.memset(R[:, (NPG - 1) * 4:NPG * 4], 0.0), w=[BR])
                for pg in range(NPG - 2, -1, -1):
                    self.V((lambda pg_: lambda e: e.tensor_tensor(out=R[:, pg_ * 4:(pg_ + 1) * 4], in0=R[:, (pg_ + 1) * 4:(pg_ + 2) * 4],
                                                                  in1=TOT[:, (pg_ + 1) * 4:(pg_ + 2) * 4], op=ALU.add))(pg), r=[BR, BTOT], w=[BR])
                self.V((lambda p_: lambda e: e.tensor_tensor(out=ATTs[:, :], in0=ZS[:, :], in1=p_[:, 0:NZ], op=ALU.subtract))(pst), r=[BZS, bpst], w=[BATTs])
                self.V(lambda e: e.tensor_tensor(out=ATTs[:, :], in0=ATTs[:, :], in1=R[:, :], op=ALU.subtract), r=[BATTs, BR], w=[BATTs])
                self.A(lambda e: e.activation(out=ATTs[:, :], in_=ATTs[:, :], func=AF.Exp), r=[BATTs], w=[BATTs])
                self.V(lambda e: e.tensor_copy(out=ATTP[:, :].rearrange("p (g c) -> p g c", c=16)[:, :, 0:4],
                                               in_=ATTs[:, :].rearrange("p (g c) -> p g c", c=4)), r=[BATTs], w=[BATTP])
                for pg in range(NPG):
                    vp, bvp = VP[pg % 2]
                    col = s * NPG + pg
                    S.dma("gpsimd", (lambda vp_, col_: lambda e: e.indirect_dma_start(
                        out=vp_[:, :], out_offset=None, in_=cv,
                        in_offset=bass.IndirectOffsetOnAxis(ap=self.IDX[:, col_:col_ + 1], axis=0)))(vp, col), r=[self.BC], w=[bvp])
                    for h in range(4):
                        self.mm(POH[h][0][:, s * 4:s * 4 + 4], vp[:, h * 128:(h + 1) * 128], ATTP[:, pg * 16:pg * 16 + 4],
                                pg == 0, pg == NPG - 1, r=[bvp, BATTP], w=[POH[h][1]], sig=True)
            for h in range(4):
                self.V((lambda h_, po_: lambda e: e.tensor_copy(
                    out=MIX[:, (8 + h_) * NS:(9 + h_) * NS],
                    in_=po_[:, 0:NS * 4].rearrange("p (s c) -> p s c", c=4)[:, :, h_]))(h, POH[h][0]), r=[POH[h][1]], w=[BMIX[2]])
            for h in range(4):
                self.ps_release(POH[h][0])
            if self.cfg.get("debug"):
                DBG, BDBG = al(64, "dbg")
                self.V(lambda e: e.tensor_copy(out=DBG[:, :], in_=MIX[:, 8 * NS:12 * NS]), r=[BMIX[2]], w=[BDBG])
                S.dma("scalar", lambda e: e.dma_start(out=self.dbg_d[l], in_=DBG[:, :]), r=[BDBG], is_output=True)
                S.dma("scalar", lambda e: e.dma_start(out=self.dbgq_d[l], in_=PRS[:, 1536:2048]), r=[self.BACT], is_output=True)
        if "dn" in en:
            off[0] = mark
            sb_bufs = allb[n_mark:]
            n_dn = len(allb)
            CSTG, BCSTG = al(1536, "c_stg")
            STC, BSTC = al(12 * 48, "c_stt")
            QKVs, BQKVs = al(12 * 16, "c_qkv")
            SQs, BSQs = al(16, "c_sq"); RIs, BRIs = al(16, "c_ri")
            BRW, BBRW = al(64, "c_b"); EGR, BEGR = al(64, "c_eg"); NBE, BNBE = al(64, "c_nbe")
            COLA, BCOLA = al(64, "c_cola"); COLB, BCOLB = al(64, "c_colb"); OS, BOS = al(64, "c_os")
            BCA = [al(128, "c_bca%d" % i) for i in range(2)]
            BCB = [al(128, "c_bcb%d" % i) for i in range(2)]
            S0 = [al(128, "c_s0%d" % i) for i in range(2)]
            SN = [al(128, "c_sn%d" % i) for i in range(2)]
            SQ4, BSQ4 = al(64, "c_sq4"); RI4, BRI4 = al(64, "c_ri4")
            S.alias(sb_bufs, allb[n_dn:])
            for j in range(3):
                S.dma("scalar", (lambda j_: lambda e: e.dma_start(out=CSTG[j_ * 16:(j_ + 1) * 16, 0:1536], in_=self.sconv_d[l, :, j_, :]))(j), w=[BCSTG])
            for c4 in range(3):
                p, bp = self.ps()
                for cc in range(4):
                    cch = c4 * 4 + cc
                    self.tr(p[:, cc * 128:cc * 128 + 48], CSTG[0:48, cch * 128:(cch + 1) * 128], ident[0:48, 0:48], r=[BCSTG, self.BC], w=[bp])
                for cc in range(4):
                    cch = c4 * 4 + cc
                    self.V((lambda cc_, cch_, p_: lambda e: e.tensor_copy(out=STC[:, cch_ * 48:(cch_ + 1) * 48], in_=p_[:, cc_ * 128:cc_ * 128 + 48]))(cc, cch, p),
                           r=[bp], w=[BSTC])
            for cch in range(12):
                gi_, h = cch // 4, cch % 4
                pr, bpr = self.proj_fm(l, W, 3072 + cch * 128, XB, N, BXB)
                dst = QKVs[:, cch * 16:(cch + 1) * 16]
                wc = lambda j: self.CW[:, (l * 4 + j) * 12 + cch:(l * 4 + j) * 12 + cch + 1]
                self.V((lambda d_, pr_, w_: lambda e: e.tensor_scalar(out=d_, in0=pr_[:, 0:N], scalar1=w_, scalar2=None, op0=ALU.mult))(dst, pr, wc(3)),
                       r=[bpr, self.BC], w=[BQKVs])
                for j in range(3):
                    self.V((lambda d_, s_, w_: lambda e: e.scalar_tensor_tensor(out=d_, in0=s_, scalar=w_, in1=d_, op0=ALU.mult, op1=ALU.add))(
                        dst, STC[:, cch * 48 + j * 16:cch * 48 + (j + 1) * 16], wc(j)), r=[BSTC, self.BC, BQKVs], w=[BQKVs])
                self.A((lambda d_: lambda e: e.activation(out=d_, in_=d_, func=AF.Silu))(dst), r=[BQKVs], w=[BQKVs])
                if gi_ < 2:
                    self.A((lambda d_: lambda e: e.activation(out=SQs[:, 0:N], in_=d_, func=AF.Square))(dst), r=[BQKVs], w=[BSQs])
                    pn, bpn = self.ps()
                    self.mm(pn[:, 0:N], self.ONESF[:, :], SQs[:, 0:N], True, True, r=[self.BC, BSQs], w=[bpn])
                    self.A((lambda pn_: lambda e: e.activation(out=RIs[:, 0:N], in_=pn_[:, 0:N], func=AF.Sqrt, bias=self.EPSC[:, 1:2]))(pn),
                           r=[bpn, self.BC], w=[BRIs])
                    self.V(lambda e: e.reciprocal(out=RIs[:, 0:N], in_=RIs[:, 0:N]), r=[BRIs], w=[BRIs])
                    sc = (128.0 ** -0.5) if gi_ == 0 else 1.0
                    self.V((lambda d_, sc_: lambda e: e.scalar_tensor_tensor(out=d_, in0=d_, scalar=sc_, in1=RIs[:, 0:N], op0=ALU.mult, op1=ALU.mult))(dst, sc),
                           r=[BQKVs, BRIs], w=[BQKVs])
            for h in range(4):
                pb_, bpb = self.proj_fm(l, self.wba, h * 128, XB, N, BXB)
                self.A((lambda h_, p_: lambda e: e.activation(out=BRW[:, h_ * 16:(h_ + 1) * 16], in_=p_[:, 0:N], func=AF.Sigmoid))(h, pb_), r=[bpb], w=[BBRW])
                pa_, bpa = self.proj_fm(l, self.wba, (4 + h) * 128, XB, N, BXB)
                self.A((lambda h_, p_: lambda e: e.activation(out=EGR[:, h_ * 16:(h_ + 1) * 16], in_=p_[:, 0:N], func=AF.Exp,
                                                               bias=self.DTB[:, l * 4 + h_:l * 4 + h_ + 1]))(h, pa_), r=[bpa, self.BC], w=[BEGR])
                self.A((lambda h_: lambda e: e.activation(out=EGR[:, h_ * 16:(h_ + 1) * 16], in_=EGR[:, h_ * 16:(h_ + 1) * 16], func=AF.Ln, bias=1.0))(h),
                       r=[BEGR], w=[BEGR])
                self.A((lambda h_: lambda e: e.activation(out=EGR[:, h_ * 16:(h_ + 1) * 16], in_=EGR[:, h_ * 16:(h_ + 1) * 16], func=AF.Exp,
                                                          scale=self.NEGA[:, l * 4 + h_:l * 4 + h_ + 1]))(h), r=[BEGR, self.BC], w=[BEGR])
            self.V(lambda e: e.scalar_tensor_tensor(out=NBE[:, :], in0=BRW[:, :], scalar=-1.0, in1=EGR[:, :], op0=ALU.mult, op1=ALU.mult), r=[BBRW, BEGR], w=[BNBE])
            self.V(lambda e: e.tensor_tensor(out=COLA[:, :], in0=QKVs[:, 8 * 16:12 * 16], in1=BRW[:, :], op=ALU.mult), r=[BQKVs, BBRW], w=[BCOLA])
            self.V(lambda e: e.tensor_tensor(out=COLB[:, :], in0=QKVs[:, 4 * 16:8 * 16], in1=NBE[:, :], op=ALU.mult), r=[BQKVs, BNBE], w=[BCOLB])
            it = 0
            for s in range(NS):
                for h in range(4):
                    i2 = it % 2
                    it += 1
                    (bca, Bbca), (bcb, Bbcb), (s0, Bs0), (sn, Bsn) = BCA[i2], BCB[i2], S0[i2], SN[i2]
                    cidx = h * 16 + s
                    S.dma("scalar", (lambda s0_, s_, h_: lambda e: e.dma_start(out=s0_[:, :], in_=self.sdelta_d[l, s_, h_]))(s0, s, h), w=[Bs0])
                    self.V((lambda d_, c_: lambda e: e.tensor_scalar(out=d_, in0=self.ONESF[:, :], scalar1=COLA[:, c_:c_ + 1], scalar2=None, op0=ALU.mult))(bca, cidx),
                           r=[self.BC, BCOLA], w=[Bbca])
                    self.V((lambda d_, c_: lambda e: e.tensor_scalar(out=d_, in0=self.ONESF[:, :], scalar1=COLB[:, c_:c_ + 1], scalar2=None, op0=ALU.mult))(bcb, cidx),
                           r=[self.BC, BCOLB], w=[Bbcb])
                    pu, bpu = self.ps()
                    self.mm(pu[:, 0:128], bca, ident[:, :], True, False, r=[Bbca, self.BC], w=[bpu], sig=True)
                    self.mm(pu[:, 0:128], bcb, s0, False, True, r=[Bbcb, Bs0], w=[bpu])
                    self.V((lambda sn_, s0_, c_: lambda e: e.tensor_scalar(out=sn_, in0=s0_, scalar1=EGR[:, c_:c_ + 1], scalar2=None, op0=ALU.mult))(sn, s0, cidx),
                           r=[Bs0, BEGR], w=[Bsn])
                    kc = (4 + h) * 16 + s
                    self.V((lambda sn_, pu_, kc_: lambda e: e.scalar_tensor_tensor(out=sn_, in0=pu_[:, 0:128], scalar=QKVs[:, kc_:kc_ + 1], in1=sn_,
                                                                                  op0=ALU.mult, op1=ALU.add))(sn, pu, kc), r=[bpu, BQKVs, Bsn], w=[Bsn])
                    po2, bpo2 = self.ps()
                    self.mm(po2[:, 0:16], sn, QKVs[:, h * 16:(h + 1) * 16], True, True, r=[Bsn, BQKVs], w=[bpo2])
                    self.V((lambda p_, c_, s_: lambda e: e.tensor_copy(out=OS[:, c_:c_ + 1], in_=p_[:, s_:s_ + 1]))(po2, cidx, s), r=[bpo2], w=[BOS])
                    S.dma("scalar", (lambda sn_, s_, h_: lambda e: e.dma_start(out=self.ndeltas_d[l, s_, h_], in_=sn_[:, :]))(sn, s, h), r=[Bsn], is_output=True)
            self.A(lambda e: e.activation(out=SQ4[:, :], in_=OS[:, :], func=AF.Square), r=[BOS], w=[BSQ4])
            pn, bpn = self.ps()
            self.mm(pn[:, 0:64], self.ONESF[:, :], SQ4[:, :], True, True, r=[self.BC, BSQ4], w=[bpn])
            self.A((lambda pn_: lambda e: e.activation(out=RI4[:, :], in_=pn_[:, 0:64], func=AF.Sqrt, bias=self.EPSC[:, 1:2], scale=1.0 / 128.0))(pn),
                   r=[bpn, self.BC], w=[BRI4])
            self.V(lambda e: e.reciprocal(out=RI4[:, :], in_=RI4[:, :]), r=[BRI4], w=[BRI4])
            self.V(lambda e: e.scalar_tensor_tensor(out=OS[:, :], in0=OS[:, :], scalar=self.NG[:, l:l + 1], in1=RI4[:, :], op0=ALU.mult, op1=ALU.mult),
                   r=[BOS, BRI4, self.BC], w=[BOS])
            for h in range(4):
                pz, bpz = self.proj_fm(l, W, 4608 + h * 128, XB, N, BXB)
                self.A((lambda pz_: lambda e: e.activation(out=SQ4[:, 0:N], in_=pz_[:, 0:N], func=AF.Silu))(pz), r=[bpz], w=[BSQ4])
                self.V((lambda h_: lambda e: e.tensor_tensor(out=MIX[:, (12 + h_) * N:(13 + h_) * N], in0=OS[:, h_ * 16:(h_ + 1) * 16], in1=SQ4[:, 0:N],
                                                             op=ALU.mult))(h), r=[BOS, BSQ4], w=[BMIX[3]])
        S.alias(allb + [self.BACT], [self.BACT])


def make_cfg(SEQ=2048, DFF=5632, NS=16, NPG=16, DEPTH=2, NPHYS=2560, **kw):
    d = dict(SEQ=SEQ, DFF=DFF, NS=NS, NPG=NPG, DEPTH=DEPTH, NPHYS=NPHYS)
    d.update(kw)
    return d


def build_program(cfg):
    kb = KB(cfg)
    with kb.es:
        kb.PHIST = kb.sb("PHIST", [128, 64]); kb.BPH = Buf("PHIST")
        nc = kb.build()
    return nc, kb


def prepare_inputs(cfg, inp, n_cores=8):
    L, NS = cfg["DEPTH"], cfg["NS"]
    f = lambda a: np.ascontiguousarray(np.asarray(a, dtype=np.float32))
    consts = host_consts(512)
    lng, lnb = f(inp["ln_g"]), f(inp["ln_b"])
    lngb = np.stack([lng, lnb], axis=2)
    lngb = lngb.reshape(L, 3, 2, 16, 128).transpose(4, 0, 1, 2, 3).reshape(128, L * 3 * 2 * 16)
    pscale = f(inp["pool_scale"]).reshape(L, 4, 128).transpose(2, 0, 1).reshape(128, L * 4)
    sguwT = f(inp["sgu_w"]).transpose(0, 1, 3, 2)
    win = f(inp["w_in"])
    wba = np.repeat(win[:, :, 5120:5128], 128, axis=2)
    shared = {
        "w1a": f(inp["w_ffn1_in"]), "w1b": f(inp["w_ffn2_in"]), "w2a": f(inp["w_ffn1_out"]), "w2b": f(inp["w_ffn2_out"]),
        "win": win, "wout": f(inp["w_out"]), "wba": np.ascontiguousarray(wba),
        "lngb": np.ascontiguousarray(lngb), "poolw": f(inp["pool_w"]), "pscale": np.ascontiguousarray(pscale),
        "sguwT": np.ascontiguousarray(sguwT), "sgub": f(inp["sgu_b"]).reshape(1, -1), "sbbias": f(inp["sb_bias"]).reshape(1, -1),
    }
    cw = f(inp["dn_conv_w"]).reshape(L, 4, 12, 128).transpose(3, 0, 1, 2).reshape(128, L * 4 * 12)
    shared["convw"] = np.ascontiguousarray(cw)
    shared["alog"] = f(inp["dn_a_log"]).reshape(1, -1)
    shared["dtb"] = f(inp["dn_dt_bias"]).reshape(1, -1)
    shared["normg"] = np.ascontiguousarray(f(inp["dn_norm_g"]).T)
    NPHYS = cfg["NPHYS"]
    ck_, cv_ = f(inp["cache_k"]), f(inp["cache_v"])
    for l_ in range(L):
        shared["ck%d" % l_] = ck_[l_].reshape(NPHYS * 128, 512)
        shared["cv%d" % l_] = cv_[l_].reshape(NPHYS * 128, 512)
    shared["iota"] = np.arange(128, dtype=np.int32).reshape(128, 1)
    shared["sguw00"] = np.ascontiguousarray(f(inp["sgu_w"])[:, :, 0, 0]).reshape(1, -1)
    for k, v in consts.items():
        shared["c_" + k] = v
    xp, xs = f(inp["x_prompt"]), f(inp["x_sample"])
    spool, sconv = f(inp["state_pool"]), f(inp["state_conv"])
    ptab = np.asarray(inp["page_table"], dtype=np.int32)
    sdelta = f(inp["state_delta"])
    maps = []
    for c in range(n_cores):
        m = dict(shared)
        m["x"] = xp[(c // 2) % xp.shape[0]]
        m["xs"] = np.ascontiguousarray(xs[c * NS:(c + 1) * NS, 0, :])
        m["spool"] = np.ascontiguousarray(spool[:, c * NS:(c + 1) * NS])
        m["ptab"] = np.ascontiguousarray(ptab[c * NS:(c + 1) * NS]).reshape(1, -1)
        m["sdelta"] = np.ascontiguousarray(sdelta[:, c * NS:(c + 1) * NS])
        m["sconv"] = np.ascontiguousarray(sconv[:, c * NS:(c + 1) * NS])
        maps.append(m)
    return maps


def assemble(cfg, res, B, n_cores=8):
    L, NS, SEQ = cfg["DEPTH"], cfg["NS"], cfg["SEQ"]
    pc = [res[2 * b] for b in range(B)]
    y_p = np.stack([r["y"] for r in pc])
    y_s = np.concatenate([r["ys"] for r in res], 0)[:, None, :]
    nk_p = np.stack([r["nk"] for r in pc], 1).reshape(L, B, SEQ, 4, 128)
    nv_p = np.stack([r["nv"] for r in pc], 1).reshape(L, B, SEQ, 4, 128)
    npool_p = np.stack([r["npool"] for r in pc], 1)
    nconv_p = np.stack([r["nconv"] for r in pc], 1)
    ndelta_p = np.stack([r["ndelta"] for r in pc], 1)
    cat = lambda k: np.concatenate([r[k] for r in res], 1)
    nk_s = cat("nks").reshape(L, -1, 1, 4, 128)
    nv_s = cat("nvs").reshape(L, -1, 1, 4, 128)
    npool_s = cat("npools")
    nconv_s = cat("nconvs")
    ndelta_s = cat("ndeltas")
    nsgu_s = cat("nsgus")[:, :, None, :]
    return (y_p, y_s, nk_p, nv_p, npool_p, nconv_p, ndelta_p, nk_s, nv_s, npool_s, nconv_s, ndelta_s, nsgu_s)


def kernel(**inputs):
    cfg = make_cfg()
    nc, kb = build_program(cfg)
    maps = prepare_inputs(cfg, inputs)
    res = run_bass_kernel_spmd(nc, maps, core_ids=list(range(8)))
    outs = assemble(cfg, res.results, 4)
    return tuple(np.ascontiguousarray(o, dtype=np.float32) for o in outs)
```

```python
import math
from contextlib import ExitStack
import numpy as np
import concourse.bass as bass
import concourse.mybir as mybir
from concourse.bass_utils import run_bass_kernel_spmd

F32 = mybir.dt.float32
BF16 = mybir.dt.bfloat16
I32 = mybir.dt.int32
AF = mybir.ActivationFunctionType
ALU = mybir.AluOpType
AX = mybir.AxisListType

D = 2048
KC = 16
GW = 512
DIN = 5128
ALPHA = 4.0 ** 0.25
LN_EPS = 1e-5
NORM_EPS = 1e-6
POOL_W = (2, 4, 8, 16)


class Buf:
    __slots__ = ("name", "lw", "rd")

    def __init__(self, name):
        self.name = name
        self.lw = None
        self.rd = {}


class Sched:
    ENGS = ("sync", "scalar", "vector", "gpsimd", "tensor")
    DMA_POOL = 6
    SEM_WRAP = 12000

    def __init__(self, nc, same_engine_sync=True):
        self.nc = nc
        self.prog = {e: [] for e in self.ENGS}
        self.sem_names = []
        self.cur_sem = {}
        self.cnt = {}
        self.obs = {e: {} for e in self.ENGS}
        self.dma_i = {e: 0 for e in self.ENGS}
        self.same = same_engine_sync
        self.n_ops = 0
        self.out_tokens = []
        for e in self.ENGS:
            self._new_eng_sem(e)

    def _new_sem(self, name):
        key = len(self.sem_names)
        self.sem_names.append(name)
        self.cnt[key] = 0
        return key

    def _new_eng_sem(self, e):
        self.cur_sem[e] = self._new_sem("c_%s_%d" % (e, len(self.sem_names)))

    def _waits(self, eng, reads, writes, is_pe_mm):
        need = {}

        def req(tok):
            if tok is None:
                return
            k, v = tok
            if need.get(k, 0) < v:
                need[k] = v
        for b in reads:
            req(b.lw)
        for b in writes:
            req(b.lw)
            for k, v in b.rd.items():
                req((k, v))
        out = []
        for k, v in need.items():
            if self.obs[eng].get(k, 0) >= v:
                continue
            if k == self.cur_sem[eng] and (is_pe_mm or not self.same):
                continue
            self.obs[eng][k] = v
            out.append((k, v))
        return out

    def _mark(self, tok, r, w):
        for b in w:
            b.lw = tok
            b.rd = {}
        for b in r:
            if b.rd.get(tok[0], 0) < tok[1]:
                b.rd[tok[0]] = tok[1]

    def op(self, eng, fn, r=(), w=(), signal=True, pe_mm=False):
        waits = self._waits(eng, r, w, pe_mm)
        k = self.cur_sem[eng]
        if signal:
            self.cnt[k] += 1
            tok = (k, self.cnt[k])
            inc = (k, 1)
        else:
            tok = (k, self.cnt[k] + 1)
            inc = None
        self.prog[eng].append((waits, fn, inc))
        self._mark(tok, r, w)
        if signal and self.cnt[k] >= self.SEM_WRAP:
            self._new_eng_sem(eng)
        self.n_ops += 1
        return tok

    def dma(self, eng, fn, r=(), w=(), is_output=False):
        i = self.dma_i[eng]
        self.dma_i[eng] += 1
        pk = ("dma", eng, i % self.DMA_POOL)
        if pk not in self.cur_sem:
            self.cur_sem[pk] = self._new_sem("d_%s_%d_%d" % (eng, i % self.DMA_POOL, len(self.sem_names)))
        k = self.cur_sem[pk]
        waits = self._waits(eng, r, w, False)
        prev = self.cnt[k]
        if prev > 0 and self.obs[eng].get(k, 0) < prev:
            self.obs[eng][k] = prev
            waits.append((k, prev))
        self.cnt[k] += 16
        tok = (k, self.cnt[k])
        self.prog[eng].append((waits, fn, (k, 16)))
        self._mark(tok, r, w)
        if is_output:
            self.out_tokens.append(tok)
        if self.cnt[k] >= self.SEM_WRAP:
            del self.cur_sem[pk]
        self.n_ops += 1
        return tok

    def alias(self, old, new):
        merged = {}
        for b in old:
            if b.lw is not None and merged.get(b.lw[0], 0) < b.lw[1]:
                merged[b.lw[0]] = b.lw[1]
            for k, v in b.rd.items():
                if merged.get(k, 0) < v:
                    merged[k] = v
        for n in new:
            n.lw = None
            n.rd = dict(merged)

    def emit(self):
        nc = self.nc
        fin = {}
        for k, v in self.out_tokens:
            fin[k] = max(fin.get(k, 0), v)
        with ExitStack() as es:
            sems = [es.enter_context(nc.semaphore(n)) for n in self.sem_names]
            block = es.enter_context(nc.Block())

            def run(ename, extra_final=None):
                def body(e):
                    for waits, fn, inc in self.prog[ename]:
                        for k, v in waits:
                            e.wait_ge(sems[k], v)
                        ins = fn(e)
                        if inc is not None:
                            ins.then_inc(sems[inc[0]], inc[1])
                    if extra_final:
                        for k, v in extra_final.items():
                            e.wait_ge(sems[k], v)
                return body

            block.sync(run("sync", fin))
            block.scalar(run("scalar"))
            block.vector(run("vector"))
            block.gpsimd(run("gpsimd"))
            block.tensor(run("tensor"))


def host_consts(N):
    c = {}
    c["ident"] = np.eye(128, dtype=np.float32)
    i = np.arange(128)
    c["u_ge"] = (i[:, None] >= i[None, :]).astype(np.float32)
    c["u_le"] = (i[:, None] <= i[None, :]).astype(np.float32)
    c["m_low"] = (i[:, None] > i[None, :]).astype(np.float32)
    c["m_upi"] = (i[:, None] <= i[None, :]).astype(np.float32)
    q = np.arange(N)
    sbm = np.zeros((128, 4, N), np.float32)
    for j in range(4):
        sbm[:, j, :] = ((j * 128 + i)[:, None] < q[None, :]).astype(np.float32)
    c["sbmask"] = sbm.reshape(128, 4 * N)
    inv = np.zeros((128, 4, 16), np.float32)
    for g, w in enumerate(POOL_W):
        inv[:, g, :] = 1.0 / np.minimum(np.arange(16) + 1, w).astype(np.float32)[None, :]
    c["invcnt"] = inv.reshape(128, 64)
    return c


class KB:
    def __init__(self, cfg):
        self.cfg = cfg
        self.SEQ = cfg["SEQ"]; self.DFF = cfg["DFF"]; self.NS = cfg["NS"]; self.NPG = cfg["NPG"]
        self.L = cfg["DEPTH"]; self.NPHYS = cfg["NPHYS"]
        self.N = 512
        self.NT = self.SEQ // self.N
        self.JF = self.DFF // 128
        self.stage = cfg.get("stage", 9)
        self.nc = bass.Bass("TRN2", target_bir_lowering=False)
        self.es = ExitStack()
        self.S = Sched(self.nc)
        self.ps_i = 0
        self.ps_held = set()
        self.w_i = 0
        self.tmp_i = 0

    def din(self, name, shape, dt=F32):
        return self.nc.dram_tensor(name, list(shape), dt, kind="ExternalInput").ap()

    def dout(self, name, shape, dt=F32):
        return self.nc.dram_tensor(name, list(shape), dt, kind="ExternalOutput").ap()

    def sb(self, name, shape, dt=F32):
        return self.es.enter_context(self.nc.sbuf_tensor("s_" + name, list(shape), dt))

    def ps(self, hold=False):
        while True:
            i = self.ps_i % 8
            self.ps_i += 1
            if i not in self.ps_held:
                break
        if hold:
            self.ps_held.add(i)
        return self.PS[i], self.PSB[i]

    def ps_release(self, p):
        for i in range(8):
            if self.PS[i] is p:
                self.ps_held.discard(i)

    def V(self, fn, r=(), w=()):
        return self.S.op("vector", fn, r, w)

    def A(self, fn, r=(), w=()):
        return self.S.op("scalar", fn, r, w)

    def G(self, fn, r=(), w=()):
        return self.S.op("gpsimd", fn, r, w)

    def mm(self, out, lhsT, rhs, start, stop, r=(), w=(), sig=None):
        return self.S.op("tensor", lambda e: e.matmul(out, lhsT=lhsT, rhs=rhs, start=start, stop=stop),
                         r, w, signal=bool(stop) if sig is None else sig, pe_mm=True)

    def tr(self, out, in_, ident, r=(), w=()):
        return self.S.op("tensor", lambda e: e.transpose(out=out, in_=in_, identity=ident), r, w, pe_mm=True)

    def evac(self, out, in_, r, w):
        self.tmp_i += 1
        if self.tmp_i % 2:
            return self.V(lambda e: e.tensor_copy(out=out, in_=in_), r, w)
        return self.A(lambda e: e.copy(out=out, in_=in_), r, w)

    def load_w(self, ap, nk, ncol=128):
        i = self.w_i % 2
        self.w_i += 1
        stg, bstg, wb, bwb = self.WSTG[i], self.WSTGB[i], self.WB[i], self.WBB[i]
        src = ap.rearrange("(k p) c -> p k c", p=128)
        dst = stg[:, 0:nk * ncol].rearrange("p (k c) -> p k c", k=nk)
        self.S.dma("sync", lambda e: e.dma_start(out=dst, in_=src), w=[bstg])
        self.G(lambda e: e.tensor_copy(out=wb[:, 0:nk * ncol], in_=stg[:, 0:nk * ncol]), r=[bstg], w=[bwb])
        return wb, bwb

    def build(self):
        nc, S = self.nc, self.S
        L, SEQ, DFF, NS, N, NT, JF = self.L, self.SEQ, self.DFF, self.NS, self.N, self.NT, self.JF
        NB = SEQ // 128
        self.x_d = self.din("x", [SEQ, D])
        self.xs_d = self.din("xs", [NS, D])
        self.w1 = [self.din("w1a", [L, D, 2 * DFF]), self.din("w1b", [L, D, 2 * DFF])]
        self.w2 = [self.din("w2a", [L, DFF, D]), self.din("w2b", [L, DFF, D])]
        self.win = self.din("win", [L, D, DIN])
        self.wout = self.din("wout", [L, D, D])
        self.wba = self.din("wba", [L, D, 8 * 128])
        cst = {}
        for name, shp in [("ident", [128, 128]), ("u_ge", [128, 128]), ("u_le", [128, 128]), ("m_low", [128, 128]),
                          ("m_upi", [128, 128]), ("sbmask", [128, 4 * N]), ("invcnt", [128, 64])]:
            cst[name] = self.din("c_" + name, shp)
        ln_d = self.din("lngb", [128, L * 3 * 2 * 16])
        poolw_d = self.din("poolw", [L, 4, 128, 128])
        pscale_d = self.din("pscale", [128, L * 4])
        sguwT_d = self.din("sguwT", [L, 4, 128, 128])
        sgub_d = self.din("sgub", [1, L * 4 * 128])
        sbbias_d = self.din("sbbias", [1, L * 4])
        cw_d = self.din("convw", [128, L * 4 * 12])
        alog_d = self.din("alog", [1, L * 4])
        dtb_d = self.din("dtb", [1, L * 4])
        ng_d = self.din("normg", [128, L])
        self.ck_d = [self.din("ck%d" % i, [self.NPHYS * 128, GW]) for i in range(L)]
        self.cv_d = [self.din("cv%d" % i, [self.NPHYS * 128, GW]) for i in range(L)]
        ptab_d = self.din("ptab", [1, NS * self.NPG], I32)
        iota_d = self.din("iota", [128, 1], I32)
        sguw00_d = self.din("sguw00", [1, L * 4])
        self.sdelta_d = self.din("sdelta", [L, NS, 4, 128, 128])
        self.spool_d = self.din("spool", [L, NS, 15, GW])
        self.sconv_d = self.din("sconv", [L, NS, 3, 3 * GW])
        self.y_d = self.dout("y", [SEQ, D])
        self.ys_d = self.dout("ys", [NS, D])
        self.nk_d = self.dout("nk", [L, SEQ, GW])
        self.nv_d = self.dout("nv", [L, SEQ, GW])
        self.npool_d = self.dout("npool", [L, 15, GW])
        self.nconv_d = self.dout("nconv", [L, 3, 3 * GW])
        self.ndelta_d = self.dout("ndelta", [L, 4, 128, 128])
        self.nks_d = self.dout("nks", [L, NS, GW])
        self.nvs_d = self.dout("nvs", [L, NS, GW])
        self.npools_d = self.dout("npools", [L, NS, 15, GW])
        self.nconvs_d = self.dout("nconvs", [L, NS, 3, 3 * GW])
        self.ndeltas_d = self.dout("ndeltas", [L, NS, 4, 128, 128])
        self.nsgus_d = self.dout("nsgus", [L, NS, GW])
        if self.cfg.get("debug"):
            self.dbg_d = self.dout("dbg", [L, 128, 64])
            self.dbgq_d = self.dout("dbgq", [L, NS, GW])
        self.ysc = nc.dram_tensor("yscratch", [NT, 128, KC * N], F32, kind="Internal").ap()
        self.YSB = [Buf("ysc%d" % t) for t in range(NT)]
        self.qscr = nc.dram_tensor("qscratch", [NS, GW], F32, kind="Internal").ap()
        self.BQS = Buf("qscr")

        sb = self.sb
        self.X = sb("X", [128, KC * N]); self.BX = Buf("X")
        self.XB = sb("XB", [128, KC * N], BF16); self.BXB = Buf("XB")
        self.XS = sb("XS", [128, KC * NS]); self.BXS = Buf("XS")
        self.XSB = sb("XSB", [128, KC * NS], BF16); self.BXSB = Buf("XSB")
        self.ARENA = sb("ARENA", [128, 11264]); self.BACT = Buf("ACT")
        self.ACTV = self.ARENA[:, :].bitcast(BF16)
        self.MIXT = sb("MIXT", [128, KC * N], BF16); self.BMIX = [Buf("MIX%d" % i) for i in range(4)]
        self.MIXS = sb("MIXS", [128, KC * NS], BF16); self.BMIXS = [Buf("MIXS%d" % i) for i in range(4)]
        self.WSTG = [sb("wstg%d" % i, [128, 2048]) for i in range(2)]; self.WSTGB = [Buf("wstg%d" % i) for i in range(2)]
        self.WB = [sb("wb%d" % i, [128, 2048], BF16) for i in range(2)]; self.WBB = [Buf("wb%d" % i) for i in range(2)]
        self.KT = sb("KT", [128, 4 * SEQ], BF16); self.BKT = Buf("KT")
        self.VS = sb("VS", [128, NB * GW], BF16); self.BVS = Buf("VS")
        self.SG = [sb("sg%d" % i, [128, N]) for i in range(2)]; self.BSG = [Buf("sg%d" % i) for i in range(2)]
        self.T1 = [sb("t1_%d" % i, [128, N]) for i in range(2)]; self.BT1 = [Buf("t1_%d" % i) for i in range(2)]
        self.STAT = [sb("stat%d" % i, [128, N]) for i in range(3)]; self.BSTAT = [Buf("stat%d" % i) for i in range(3)]
        self.PS = [self.es.enter_context(nc.psum_tensor("ps%d" % i, [128, 512], F32)) for i in range(8)]
        self.PSB = [Buf("ps%d" % i) for i in range(8)]
        C = {}
        self.BC = Buf("consts")
        for name in cst:
            shp = cst[name].shape
            C[name] = sb("k_" + name, list(shp))
            S.dma("scalar", (lambda d, s_: lambda e: e.dma_start(out=d[:, :], in_=s_))(C[name], cst[name]), w=[self.BC])
        self.C = C
        self.ONESF = sb("onesf", [128, 128]); self.ONESB = sb("onesb", [128, 128], BF16)
        self.IDB = sb("identb", [128, 128], BF16)
        self.EPSC = sb("epsc", [128, 2])
        self.V(lambda e: e.memset(self.EPSC[:, 0:1], LN_EPS), w=[self.BC])
        self.V(lambda e: e.memset(self.EPSC[:, 1:2], NORM_EPS), w=[self.BC])
        self.V(lambda e: e.memset(self.ONESF[:, :], 1.0), w=[self.BC])
        self.V(lambda e: e.memset(self.ONESB[:, :], 1.0), w=[self.BC])
        self.V(lambda e: e.tensor_copy(out=self.IDB[:, :], in_=C["ident"][:, :]), r=[self.BC], w=[self.BC])
        self.LN = sb("lngb", [128, L * 3 * 2 * 16])
        S.dma("scalar", lambda e: e.dma_start(out=self.LN[:, :], in_=ln_d), w=[self.BC])
        self.PSC = sb("pscale", [128, L * 4])
        S.dma("scalar", lambda e: e.dma_start(out=self.PSC[:, :], in_=pscale_d), w=[self.BC])
        self.SGUB = sb("sgub", [128, L * 4 * 128])
        S.dma("scalar", lambda e: e.dma_start(out=self.SGUB[:, :], in_=sgub_d.partition_broadcast(128)), w=[self.BC])
        self.SBB = sb("sbbias", [128, L * 4])
        S.dma("scalar", lambda e: e.dma_start(out=self.SBB[:, :], in_=sbbias_d.partition_broadcast(128)), w=[self.BC])
        self.CW = sb("convw", [128, L * 4 * 12])
        S.dma("scalar", lambda e: e.dma_start(out=self.CW[:, :], in_=cw_d), w=[self.BC])
        self.NG = sb("normg", [128, L])
        S.dma("scalar", lambda e: e.dma_start(out=self.NG[:, :], in_=ng_d), w=[self.BC])
        self.DTB = sb("dtb", [128, L * 4])
        S.dma("scalar", lambda e: e.dma_start(out=self.DTB[:, :], in_=dtb_d.partition_broadcast(128)), w=[self.BC])
        self.NEGA = sb("nega", [128, L * 4])
        S.dma("scalar", lambda e: e.dma_start(out=self.NEGA[:, :], in_=alog_d.partition_broadcast(128)), w=[self.BC])
        self.A(lambda e: e.activation(out=self.NEGA[:, :], in_=self.NEGA[:, :], func=AF.Exp), r=[self.BC], w=[self.BC])
        self.A(lambda e: e.mul(out=self.NEGA[:, :], in_=self.NEGA[:, :], mul=-1.0), r=[self.BC], w=[self.BC])
        self.PTAB = sb("ptab", [128, NS * self.NPG], I32)
        S.dma("scalar", lambda e: e.dma_start(out=self.PTAB[:, :], in_=ptab_d.partition_broadcast(128)), w=[self.BC])
        self.IOTA = sb("iota", [128, 1], I32)
        S.dma("scalar", lambda e: e.dma_start(out=self.IOTA[:, :], in_=iota_d), w=[self.BC])
        self.IDX = sb("idx", [128, NS * self.NPG], I32)
        self.SW00 = sb("sguw00", [128, L * 4])
        S.dma("scalar", lambda e: e.dma_start(out=self.SW00[:, :], in_=sguw00_d.partition_broadcast(128)), w=[self.BC])
        self.SST = sb("dnstate", [128, 4 * 128]); self.BSST = Buf("dnstate")
        self.CHIST = sb("chist", [128, 12 * 4]); self.BCH = Buf("chist")
        self.W8 = sb("w8", [128, 16 * 8], BF16); self.BW8 = Buf("w8")
        self.POOLW = sb("poolw", [128, L * 4 * 128], BF16)
        self.SGUW = sb("sguw", [128, L * 4 * 128], BF16)
        for l in range(L):
            stg, bstg = self.WSTG[0], self.WSTGB[0]
            S.dma("scalar", (lambda l_: lambda e: e.dma_start(
                out=stg[:, 0:512].rearrange("p (g d) -> p g d", g=4),
                in_=poolw_d[l_].rearrange("g c d -> c g d")))(l), w=[bstg])
            self.V((lambda l_: lambda e: e.tensor_copy(out=self.POOLW[:, l_ * 512:(l_ + 1) * 512], in_=stg[:, 0:512]))(l),
                   r=[bstg], w=[self.BC])
            stg1, bstg1 = self.WSTG[1], self.WSTGB[1]
            S.dma("scalar", (lambda l_: lambda e: e.dma_start(
                out=stg1[:, 0:512].rearrange("p (g d) -> p g d", g=4),
                in_=sguwT_d[l_].rearrange("h s t -> s h t")))(l), w=[bstg1])
            for h in range(4):
                self.V((lambda l_, h_: lambda e: e.tensor_tensor(
                    out=self.SGUW[:, (l_ * 4 + h_) * 128:(l_ * 4 + h_ + 1) * 128],
                    in0=stg1[:, h_ * 128:(h_ + 1) * 128], in1=C["u_le"][:, :], op=ALU.mult))(l, h),
                    r=[bstg1, self.BC], w=[self.BC])

        for l in range(L):
            for t in range(NT):
                self.prompt_tile(l, t)
            if self.stage >= 2:
                self.sample_tile(l)
        S.emit()
        return nc

    def ln_cols(self, l, i, gb, c):
        o = ((l * 3 + i) * 2 + gb) * 16 + c
        return self.LN[:, o:o + 1]

    def load_x_tile(self, t):
        N = self.N
        stg = self.ARENA[:, 0:4 * D]
        self.S.dma("scalar", lambda e: e.dma_start(
            out=stg.rearrange("p (a f) -> p a f", a=4),
            in_=self.x_d[t * N:(t + 1) * N, :].rearrange("(a p) f -> p a f", p=128)), w=[self.BACT])
        for c in range(KC):
            p, bp = self.ps()
            for a in range(4):
                self.tr(p[:, a * 128:(a + 1) * 128], stg[:, a * D + c * 128: a * D + (c + 1) * 128], self.C["ident"][:, :],
                        r=[self.BACT, self.BC], w=[bp])
            self.evac(self.X[:, c * N:(c + 1) * N], p[:, 0:N], r=[bp], w=[self.BX])
        self.V(lambda e: e.tensor_copy(out=self.XB[:, :], in_=self.X[:, :]), r=[self.BX], w=[self.BXB])

    def store_y_tile(self, t, X, BX, N, out_ap):
        nb = max(1, N // 128)
        rows = min(N, 128)
        stg = self.ARENA[:, 0:nb * D]
        for a in range(nb):
            for c4 in range(KC // 4):
                p, bp = self.ps()
                for cc in range(4):
                    c = c4 * 4 + cc
                    self.tr(p[0:rows, cc * 128:(cc + 1) * 128], X[:, c * N + a * rows: c * N + a * rows + rows],
                            self.C["ident"][:, :], r=[BX, self.BC], w=[bp])
                self.evac(stg[0:rows, a * D + c4 * 512: a * D + (c4 + 1) * 512], p[0:rows, 0:512], r=[bp], w=[self.BACT])
        if N >= 128:
            self.S.dma("scalar", lambda e: e.dma_start(
                out=out_ap.rearrange("(a p) f -> p a f", p=128), in_=stg.rearrange("p (a f) -> p a f", a=nb)),
                r=[self.BACT], is_output=True)
        else:
            self.S.dma("scalar", lambda e: e.dma_start(out=out_ap, in_=stg[0:rows, 0:D]), r=[self.BACT], is_output=True)

    def ffn(self, l, which, X, XB, N, BX, BXB):
        JF, DFF = self.JF, self.DFF
        W1, W2 = self.w1[which], self.w2[which]
        ACT = self.ACTV
        for jp in range(JF // 2):
            banks = {}
            for nm in ("g0", "g1", "u0", "u1"):
                banks[nm] = self.ps(hold=True)
            for kb in range(2):
                for half, cbase in (("g", jp * 256), ("u", DFF + jp * 256)):
                    w, bw = self.load_w(W1[l, kb * 1024:(kb + 1) * 1024, cbase:cbase + 256], 8, 256)
                    for c in range(2):
                        p, bp = banks[half + str(c)]
                        for kk in range(8):
                            k = kb * 8 + kk
                            self.mm(p[:, 0:N], w[:, kk * 256 + c * 128:kk * 256 + (c + 1) * 128], XB[:, k * N:(k + 1) * N],
                                    k == 0, k == KC - 1, r=[bw, BXB], w=[bp], sig=(kk == 7))
            for c in range(2):
                j = jp * 2 + c
                pg, bpg = banks["g" + str(c)]
                pu, bpu = banks["u" + str(c)]
                sg, bsg = self.SG[j % 2], self.BSG[j % 2]
                self.A((lambda sg_, pg_: lambda e: e.activation(out=sg_[:, 0:N], in_=pg_[:, 0:N], func=AF.Silu))(sg, pg),
                       r=[bpg], w=[bsg])
                self.V((lambda sg_, pu_, j_: lambda e: e.tensor_tensor(out=ACT[:, j_ * N:(j_ + 1) * N], in0=sg_[:, 0:N],
                                                                         in1=pu_[:, 0:N], op=ALU.mult))(sg, pu, j),
                       r=[bsg, bpu], w=[self.BACT])
            for nm in banks:
                self.ps_release(banks[nm][0])
        for op_ in range(KC // 2):
            pos = [self.ps(hold=True) for _ in range(2)]
            for j0 in range(0, JF, 8):
                nj = min(8, JF - j0)
                w, bw = self.load_w(W2[l, j0 * 128:(j0 + nj) * 128, op_ * 256:(op_ + 1) * 256], nj, 256)
                for c in range(2):
                    po, bpo = pos[c]
                    for jj in range(nj):
                        j = j0 + jj
                        self.mm(po[:, 0:N], w[:, jj * 256 + c * 128:jj * 256 + (c + 1) * 128], ACT[:, j * N:(j + 1) * N], j == 0, j == JF - 1,
                                r=[bw, self.BACT], w=[bpo], sig=(jj == nj - 1))
            for c in range(2):
                oc = op_ * 2 + c
                po, bpo = pos[c]
                t1, bt1 = self.T1[oc % 2], self.BT1[oc % 2]
                self.A((lambda t1_, po_: lambda e: e.mul(out=t1_[:, 0:N], in_=po_[:, 0:N], mul=0.5))(t1, po), r=[bpo], w=[bt1])
                self.V((lambda t1_, oc_: lambda e: e.scalar_tensor_tensor(
                    out=X[:, oc_ * N:(oc_ + 1) * N], in0=X[:, oc_ * N:(oc_ + 1) * N], scalar=ALPHA, in1=t1_[:, 0:N],
                    op0=ALU.mult, op1=ALU.add))(t1, oc), r=[BX, bt1], w=[BX])
            for c in range(2):
                self.ps_release(pos[c][0])

    def layernorm(self, l, i, X, XB, N, BX, BXB):
        pm, bpm = self.ps()
        for c in range(KC):
            self.mm(pm[:, 0:N], self.ONESF[:, :], X[:, c * N:(c + 1) * N], c == 0, c == KC - 1, r=[BX, self.BC], w=[bpm])
        pq, bpq = self.ps()
        for c in range(KC):
            sq, bsq = self.SG[c % 2], self.BSG[c % 2]
            self.A((lambda sq_, c_: lambda e: e.activation(out=sq_[:, 0:N], in_=X[:, c_ * N:(c_ + 1) * N], func=AF.Square))(sq, c),
                   r=[BX], w=[bsq])
            self.mm(pq[:, 0:N], self.ONESF[:, :], sq[:, 0:N], c == 0, c == KC - 1, r=[bsq, self.BC], w=[bpq], sig=True)
        mean, bmean = self.STAT[0], self.BSTAT[0]
        rstd, brstd = self.STAT[1], self.BSTAT[1]
        m2, bm2 = self.STAT[2], self.BSTAT[2]
        self.A(lambda e: e.mul(out=mean[:, 0:N], in_=pm[:, 0:N], mul=1.0 / D), r=[bpm], w=[bmean])
        self.V(lambda e: e.tensor_tensor(out=m2[:, 0:N], in0=mean[:, 0:N], in1=mean[:, 0:N], op=ALU.mult), r=[bmean], w=[bm2])
        self.V(lambda e: e.scalar_tensor_tensor(out=rstd[:, 0:N], in0=pq[:, 0:N], scalar=1.0 / D, in1=m2[:, 0:N],
                                                op0=ALU.mult, op1=ALU.subtract), r=[bpq, bm2], w=[brstd])
        self.A(lambda e: e.activation(out=rstd[:, 0:N], in_=rstd[:, 0:N], func=AF.Sqrt, bias=self.EPSC[:, 0:1]), r=[brstd, self.BC], w=[brstd])
        self.V(lambda e: e.reciprocal(out=rstd[:, 0:N], in_=rstd[:, 0:N]), r=[brstd], w=[brstd])
        for c in range(KC):
            xs = X[:, c * N:(c + 1) * N]
            self.V((lambda xs_: lambda e: e.tensor_tensor(out=xs_, in0=xs_, in1=mean[:, 0:N], op=ALU.subtract))(xs),
                   r=[BX, bmean], w=[BX])
            self.V((lambda xs_: lambda e: e.tensor_tensor(out=xs_, in0=xs_, in1=rstd[:, 0:N], op=ALU.mult))(xs),
                   r=[BX, brstd], w=[BX])
            self.V((lambda xs_, c_: lambda e: e.tensor_scalar(out=xs_, in0=xs_, scalar1=self.ln_cols(l, i, 0, c_),
                                                               scalar2=self.ln_cols(l, i, 1, c_), op0=ALU.mult, op1=ALU.add))(xs, c),
                   r=[BX, self.BC], w=[BX])
            self.A((lambda xs_, c_: lambda e: e.copy(out=XB[:, c_ * N:(c_ + 1) * N], in_=xs_))(xs, c), r=[BX], w=[BXB])

    def proj_fm(self, l, W, c0, XB, N, BXB):
        w, bw = self.load_w(W[l, :, c0:c0 + 128], 16)
        p, bp = self.ps()
        for k in range(KC):
            self.mm(p[:, 0:N], w[:, k * 128:(k + 1) * 128], XB[:, k * N:(k + 1) * N], k == 0, k == KC - 1, r=[bw, BXB], w=[bp])
        return p, bp

    def wout_res(self, l, X, N, BX, MIX, BMIX):
        for oc in range(KC):
            w, bw = self.load_w(self.wout[l, :, oc * 128:(oc + 1) * 128], 16)
            p, bp = self.ps()
            for k in range(KC):
                self.mm(p[:, 0:N], w[:, k * 128:(k + 1) * 128], MIX[:, k * N:(k + 1) * N], k == 0, k == KC - 1,
                        r=[bw] + list(BMIX), w=[bp])
            self.V((lambda oc_, p_: lambda e: e.scalar_tensor_tensor(
                out=X[:, oc_ * N:(oc_ + 1) * N], in0=X[:, oc_ * N:(oc_ + 1) * N], scalar=ALPHA, in1=p_[:, 0:N],
                op0=ALU.mult, op1=ALU.add))(oc, p), r=[BX, bp], w=[BX])

    def prompt_tile(self, l, t):
        S, N, L = self.S, self.N, self.L
        X, XB, BX, BXB = self.X, self.XB, self.BX, self.BXB
        if l == 0:
            self.load_x_tile(t)
        else:
            S.dma("scalar", lambda e: e.dma_start(out=X[:, :], in_=self.ysc[t]), r=[self.YSB[t]], w=[BX])
            self.V(lambda e: e.tensor_copy(out=XB[:, :], in_=X[:, :]), r=[BX], w=[BXB])
        if self.cfg.get("cut") == 1:
            self.store_y_tile(t, X, BX, N, self.y_d[t * N:(t + 1) * N, :]); return
        self.ffn(l, 0, X, XB, N, BX, BXB)
        if self.cfg.get("cut") == 2:
            self.store_y_tile(t, X, BX, N, self.y_d[t * N:(t + 1) * N, :]); return
        self.layernorm(l, 0, X, XB, N, BX, BXB)
        if self.cfg.get("cut") == 3:
            self.store_y_tile(t, X, BX, N, self.y_d[t * N:(t + 1) * N, :]); return
        ar = self.ARENA
        off = [0]

        def aalloc(n):
            a = ar[:, off[0]:off[0] + n]
            off[0] += n
            assert off[0] <= 11264
            return a
        PT = aalloc(4 * (N + 16)); BPT = Buf("PT")
        PTMP = aalloc(2 * (N + 16)); BPTMP = Buf("PTMP")
        UT = aalloc(4 * N); BUT = Buf("UT")
        VSG = aalloc(2 * N)[:, :].bitcast(BF16); BVSG = Buf("VSG")
        QT = aalloc(2 * N)[:, :].bitcast(BF16); BQT = Buf("QT")
        ZS = aalloc(N); BZS = Buf("ZS")
        SP = aalloc(N); BSP = Buf("SP")
        TT = aalloc(N); BTT = Buf("TT")
        RR = aalloc(N); BRR = Buf("RR")
        ATT = aalloc(N // 2)[:, :].bitcast(BF16); BATT = Buf("ATT")
        DM = aalloc(N); BDM = Buf("DM")
        PHIST = self.PHIST; BPH = self.BPH
        new = [BPT, BPTMP, BUT, BVSG, BQT, BZS, BSP, BTT, BRR, BATT, BDM]
        S.alias([self.BACT], new)
        W = self.win
        MIX, BMIX = self.MIXT, self.BMIX
        NP = N + 16
        for g in range(4):
            p, bp = self.proj_fm(l, W, g * 128, XB, N, BXB)
            self.evac(PT[:, g * NP + 16:g * NP + 16 + N], p[:, 0:N], r=[bp], w=[BPT])
            if t == 0:
                self.V((lambda g_: lambda e: e.memset(PT[:, g_ * NP:g_ * NP + 16], 0.0))(g), w=[BPT])
            else:
                self.V((lambda g_: lambda e: e.tensor_copy(out=PT[:, g_ * NP:g_ * NP + 16], in_=PHIST[:, g_ * 16:(g_ + 1) * 16]))(g),
                       r=[BPH], w=[BPT])
        for g in range(4):
            self.V((lambda g_: lambda e: e.tensor_copy(out=PHIST[:, g_ * 16:(g_ + 1) * 16], in_=PT[:, g_ * NP + N:g_ * NP + N + 16]))(g),
                   r=[BPT], w=[BPH])
        if t == self.NT - 1:
            p, bp = self.ps()
            for g in range(4):
                self.tr(p[:, g * 128:(g + 1) * 128], PT[:, g * NP + 16 + N - 128:g * NP + 16 + N], self.C["ident"][:, :],
                        r=[BPT, self.BC], w=[bp])
            self.evac(TT[:, 0:512], p[:, 0:512], r=[bp], w=[BTT])
            S.dma("scalar", lambda e: e.dma_start(out=self.npool_d[l], in_=TT[113:128, 0:512]), r=[BTT], is_output=True)
        for g, wdw in enumerate(POOL_W):
            a = PT[:, g * NP:(g + 1) * NP]
            cur = a
            sh = 1
            k = 0
            while sh < wdw:
                dst = PTMP[:, k * NP:(k + 1) * NP]
                self.V((lambda cur_, dst_, sh_: lambda e: e.tensor_tensor(out=dst_[:, sh_:NP], in0=cur_[:, sh_:NP], in1=cur_[:, 0:NP - sh_],
                                                                          op=ALU.add))(cur, dst, sh), r=[BPT, BPTMP], w=[BPTMP])
                cur = dst
                sh *= 2
                k = 1 - k
            dd = PTMP[:, k * NP + 16:k * NP + 16 + N]
            self.V((lambda cur_, dd_, a_, w_: lambda e: e.scalar_tensor_tensor(out=dd_, in0=cur_[:, 16:16 + N], scalar=1.0 / w_,
                                                                              in1=a_[:, 16:16 + N], op0=ALU.mult, op1=ALU.subtract))(cur, dd, a, wdw),
                   r=[BPTMP, BPT], w=[BPTMP])
            if t == 0:
                self.V((lambda cur_, dd_, g_: lambda e: e.tensor_tensor(out=dd_[:, 0:16], in0=cur_[:, 16:32],
                                                                         in1=self.C["invcnt"][:, g_ * 16:(g_ + 1) * 16], op=ALU.mult))(cur, dd, g),
                       r=[BPTMP, self.BC], w=[BPTMP])
                self.V((lambda dd_, a_: lambda e: e.tensor_tensor(out=dd_[:, 0:16], in0=dd_[:, 0:16], in1=a_[:, 16:32], op=ALU.subtract))(dd, a),
                       r=[BPTMP, BPT], w=[BPTMP])
            db = ATT
            self.A((lambda dd_: lambda e: e.copy(out=db[:, 0:N], in_=dd_))(dd), r=[BPTMP], w=[BATT])
            p, bp = self.ps()
            self.mm(p[:, 0:N], self.POOLW[:, (l * 4 + g) * 128:(l * 4 + g + 1) * 128], db[:, 0:N], True, True, r=[self.BC, BATT], w=[bp])
            self.V((lambda g_, p_: lambda e: e.tensor_scalar(out=MIX[:, g_ * N:(g_ + 1) * N], in0=p_[:, 0:N],
                                                              scalar1=self.PSC[:, l * 4 + g_:l * 4 + g_ + 1], scalar2=None,
                                                              op0=ALU.mult))(g, p), r=[bp, self.BC], w=[BMIX[0]])
        if self.cfg.get("cut") == 4:
            S.alias(new, [self.BACT]); self.store_y_tile(t, X, BX, N, self.y_d[t * N:(t + 1) * N, :]); return
        for h in range(4):
            p, bp = self.proj_fm(l, W, 512 + h * 128, XB, N, BXB)
            self.evac(UT[:, h * N:(h + 1) * N], p[:, 0:N], r=[bp], w=[BUT])
        for h in range(4):
            w, bw = self.load_w(W[l, :, 1024 + h * 128:1024 + (h + 1) * 128], 16)
            p, bp = self.ps()
            for tb in range(4):
                for k in range(KC):
                    self.mm(p[:, tb * 128:(tb + 1) * 128], XB[:, k * N + tb * 128:k * N + (tb + 1) * 128], w[:, k * 128:(k + 1) * 128],
                            k == 0, k == KC - 1, r=[bw, BXB], w=[bp])
            for tb in range(4):
                self.evac(VSG[:, tb * 512 + h * 128:tb * 512 + (h + 1) * 128], p[:, tb * 128:(tb + 1) * 128], r=[bp], w=[BVSG])
        for h in range(4):
            p, bp = self.ps()
            for tb in range(4):
                self.mm(p[:, tb * 128:(tb + 1) * 128], VSG[:, tb * 512 + h * 128:tb * 512 + (h + 1) * 128],
                        self.SGUW[:, (l * 4 + h) * 128:(l * 4 + h + 1) * 128], True, True, r=[BVSG, self.BC], w=[bp])
            for tb in range(4):
                self.V((lambda tb_, h_, p_: lambda e: e.tensor_tensor(out=TT[:, tb_ * 128:(tb_ + 1) * 128], in0=p_[:, tb_ * 128:(tb_ + 1) * 128],
                                                                      in1=self.SGUB[:, (l * 4 + h_) * 128:(l * 4 + h_ + 1) * 128], op=ALU.add))(tb, h, p),
                       r=[bp, self.BC], w=[BTT])
            self.V((lambda h_: lambda e: e.tensor_tensor(out=MIX[:, (4 + h_) * N:(5 + h_) * N], in0=TT[:, 0:N], in1=UT[:, h_ * N:(h_ + 1) * N],
                                                         op=ALU.mult))(h), r=[BTT, BUT], w=[BMIX[1]])
        if self.cfg.get("cut") == 5:
            S.alias(new, [self.BACT]); self.store_y_tile(t, X, BX, N, self.y_d[t * N:(t + 1) * N, :]); return
        SEQ = self.SEQ
        for h in range(4):
            p, bp = self.proj_fm(l, W, 1536 + h * 128, XB, N, BXB)
            self.evac(QT[:, h * N:(h + 1) * N], p[:, 0:N], r=[bp], w=[BQT])
            p, bp = self.proj_fm(l, W, 2048 + h * 128, XB, N, BXB)
            self.evac(self.KT[:, h * SEQ + t * N:h * SEQ + (t + 1) * N], p[:, 0:N], r=[bp], w=[self.BKT])
        for grp, dst_d in ((2048, self.nk_d), (2560, self.nv_d)):
            for h in range(4):
                w, bw = self.load_w(W[l, :, grp + h * 128:grp + (h + 1) * 128], 16)
                p, bp = self.ps()
                for tb in range(4):
                    for k in range(KC):
                        self.mm(p[:, tb * 128:(tb + 1) * 128], XB[:, k * N + tb * 128:k * N + (tb + 1) * 128], w[:, k * 128:(k + 1) * 128],
                                k == 0, k == KC - 1, r=[bw, BXB], w=[bp])
                self.evac(DM[:, 0:N], p[:, 0:N], r=[bp], w=[BDM])
                if grp == 2560:
                    for tb in range(4):
                        blk = t * 4 + tb
                        self.A((lambda tb_, blk_, h_: lambda e: e.copy(out=self.VS[:, blk_ * 512 + h_ * 128:blk_ * 512 + (h_ + 1) * 128],
                                                                       in_=DM[:, tb_ * 128:(tb_ + 1) * 128]))(tb, blk, h), r=[BDM], w=[self.BVS])
                S.dma("scalar", (lambda dst_, h_: lambda e: e.dma_start(
                    out=dst_[l, t * N:(t + 1) * N, h_ * 128:(h_ + 1) * 128].rearrange("(a p) c -> p a c", p=128),
                    in_=DM[:, 0:N].rearrange("p (a c) -> p a c", a=4)))(dst_d, h), r=[BDM], is_output=True)
        scale = 128.0 ** -0.5
        for h in range(4):
            po, bpo = self.ps(hold=True)
            nkb = (t + 1) * 4
            first = True
            for kc in range(nkb - 1, -1, -1):
                diag = kc >= t * 4
                jloc = kc - t * 4
                pz, bpz = self.ps()
                self.mm(pz[:, 0:N], self.KT[:, h * SEQ + kc * 128:h * SEQ + (kc + 1) * 128], QT[:, h * N:(h + 1) * N], True, True,
                        r=[self.BKT, BQT], w=[bpz])
                bias = self.SBB[:, l * 4 + h:l * 4 + h + 1]
                self.A((lambda pz_, bias_: lambda e: e.activation(out=ZS[:, 0:N], in_=pz_[:, 0:N], func=AF.Identity, bias=bias_, scale=scale))(pz, bias),
                       r=[bpz, self.BC], w=[BZS])
                self.A(lambda e: e.activation(out=SP[:, 0:N], in_=ZS[:, 0:N], func=AF.Exp), r=[BZS], w=[BSP])
                self.A(lambda e: e.activation(out=SP[:, 0:N], in_=SP[:, 0:N], func=AF.Ln, bias=1.0), r=[BSP], w=[BSP])
                if diag:
                    self.V((lambda j_: lambda e: e.tensor_tensor(out=SP[:, 0:N], in0=SP[:, 0:N], in1=self.C["sbmask"][:, j_ * N:(j_ + 1) * N],
                                                                 op=ALU.mult))(jloc), r=[BSP, self.BC], w=[BSP])
                pst, bpst = self.ps()
                self.mm(pst[:, 0:N], self.C["u_ge"][:, :], SP[:, 0:N], True, True, r=[self.BC, BSP], w=[bpst])
                self.V((lambda pst_: lambda e: e.tensor_tensor(out=TT[:, 0:N], in0=ZS[:, 0:N], in1=pst_[:, 0:N], op=ALU.subtract))(pst),
                       r=[BZS, bpst], w=[BTT])
                if not first:
                    self.V(lambda e: e.tensor_tensor(out=TT[:, 0:N], in0=TT[:, 0:N], in1=RR[:, 0:N], op=ALU.subtract), r=[BTT, BRR], w=[BTT])
                self.A(lambda e: e.activation(out=TT[:, 0:N], in_=TT[:, 0:N], func=AF.Exp), r=[BTT], w=[BTT])
                if diag:
                    self.V((lambda j_: lambda e: e.tensor_tensor(out=ATT[:, 0:N], in0=TT[:, 0:N], in1=self.C["sbmask"][:, j_ * N:(j_ + 1) * N],
                                                                 op=ALU.mult))(jloc), r=[BTT, self.BC], w=[BATT])
                else:
                    self.V(lambda e: e.tensor_copy(out=ATT[:, 0:N], in_=TT[:, 0:N]), r=[BTT], w=[BATT])
                self.mm(po[:, 0:N], self.VS[:, kc * 512 + h * 128:kc * 512 + (h + 1) * 128], ATT[:, 0:N], first, kc == 0,
                        r=[self.BVS, BATT], w=[bpo], sig=True)
                if kc > 0:
                    pcs, bpcs = self.ps()
                    self.mm(pcs[:, 0:N], self.ONESF[:, :], SP[:, 0:N], True, True, r=[self.BC, BSP], w=[bpcs])
                    if first:
                        self.V((lambda pcs_: lambda e: e.tensor_copy(out=RR[:, 0:N], in_=pcs_[:, 0:N]))(pcs), r=[bpcs], w=[BRR])
                    else:
                        self.V((lambda pcs_: lambda e: e.tensor_tensor(out=RR[:, 0:N], in0=RR[:, 0:N], in1=pcs_[:, 0:N], op=ALU.add))(pcs),
                               r=[bpcs, BRR], w=[BRR])
                first = False
            self.evac(MIX[:, (8 + h) * N:(9 + h) * N], po[:, 0:N], r=[bpo], w=[BMIX[2]])
            self.ps_release(po)
        if t == self.NT - 1:
            CV = UT
            for c4 in range(3):
                p, bp = self.ps()
                for cc in range(4):
                    c0 = 3072 + (c4 * 4 + cc) * 128
                    w, bw = self.load_w(W[l, :, c0:c0 + 128], 16)
                    for k in range(KC):
                        self.mm(p[:, cc * 128:(cc + 1) * 128], XB[:, k * N + N - 128:k * N + N], w[:, k * 128:(k + 1) * 128],
                                k == 0, k == KC - 1, r=[bw, BXB], w=[bp])
                self.evac(CV[:, c4 * 512:(c4 + 1) * 512], p[:, 0:512], r=[bp], w=[BUT])
            S.dma("scalar", lambda e: e.dma_start(out=self.nconv_d[l], in_=CV[125:128, 0:1536]), r=[BUT], is_output=True)
        en = self.cfg.get("mixers", "pool,sgu,sb,dn")
        if "dn" in en:
            try:
                self.deltanet_prompt(l, t, new)
            except StopIteration:
                self.V(lambda e: e.memset(MIX[:, 12 * N:16 * N], 0.0), w=[BMIX[3]])
        else:
            self.V(lambda e: e.memset(MIX[:, 12 * N:16 * N], 0.0), w=[BMIX[3]])
        for gi, nm in enumerate(("pool", "sgu", "sb")):
            if nm not in en:
                self.V((lambda gi_: lambda e: e.memset(MIX[:, gi_ * 4 * N:(gi_ + 1) * 4 * N], 0.0))(gi), w=[BMIX[gi]])
        S.alias(new + getattr(self, "dn_bufs", []), [self.BACT])
        self.wout_res(l, X, N, BX, MIX, BMIX)
        self.layernorm(l, 1, X, XB, N, BX, BXB)
        self.ffn(l, 1, X, XB, N, BX, BXB)
        self.layernorm(l, 2, X, XB, N, BX, BXB)
        if l == L - 1:
            self.store_y_tile(t, X, BX, N, self.y_d[t * N:(t + 1) * N, :])
        else:
            S.dma("scalar", lambda e: e.dma_start(out=self.ysc[t], in_=X[:, :]), r=[BX], w=[self.YSB[t]])


    def deltanet_prompt(self, l, t, old_bufs):
        S, N, L = self.S, self.N, self.L
        XB, BXB = self.XB, self.BXB
        MIX, BMIX = self.MIXT, self.BMIX
        W = self.win
        ar = self.ARENA
        off = [0]
        bufs = []

        def al(n, name):
            a = ar[:, off[0]:off[0] + n]
            off[0] += n
            assert off[0] <= 11264, off[0]
            b = Buf(name)
            bufs.append(b)
            return a, b
        NR = N + 16
        RAW, BRAW = al(3 * NR, "RAW")
        QKV, BQKV = al(3 * N, "QKV")
        OT, BOT = al(N, "OT")
        GB, BGB = al(N, "GB")
        BBR, BBBR = al(N, "BBR")
        SQ, BSQ = al(N, "SQ")
        RI, BRI = al(N, "RI")
        COLS, BCOLS = al(64, "COLS")
        SC, BSC = al(16, "SC")
        names = ["gbc", "Gb", "Dm", "E", "ET", "A0", "B0", "A1", "B1", "X0", "X1", "bv", "kbg", "kdec", "nwT", "u", "qkT", "eGb", "qdT", "tmp"]
        T_ = {}
        for nm in names:
            T_[nm] = al(128, nm)
        self.dn_bufs = bufs
        S.alias(old_bufs, bufs)
        C = self.C
        ident = C["ident"]
        SST, BSST = self.SST, self.BSST
        if t == 0:
            self.V(lambda e: e.memset(SST[:, :], 0.0), w=[BSST])
            self.V(lambda e: e.memset(self.CHIST[:, :], 0.0), w=[self.BCH])
        S.dma("scalar", lambda e: e.dma_start(out=self.WSTG[0][:, 0:128].rearrange("p (k c) -> p k c", k=16),
                                               in_=W[l, :, 5120:5128].rearrange("(k p) c -> p k c", p=128)), w=[self.WSTGB[0]])
        self.V(lambda e: e.tensor_copy(out=self.W8[:, :], in_=self.WSTG[0][:, 0:128]), r=[self.WSTGB[0]], w=[self.BW8])
        p, bp = self.ps()
        for tb in range(4):
            for k in range(KC):
                self.mm(p[:, tb * 8:(tb + 1) * 8], XB[:, k * N + tb * 128:k * N + (tb + 1) * 128], self.W8[:, k * 8:(k + 1) * 8],
                        k == 0, k == KC - 1, r=[BXB, self.BW8], w=[bp])
        for tb in range(4):
            self.A((lambda tb_, p_: lambda e: e.activation(out=COLS[:, 16 + tb_ * 4:16 + tb_ * 4 + 4], in_=p_[:, tb_ * 8:tb_ * 8 + 4], func=AF.Sigmoid))(tb, p),
                   r=[bp], w=[BCOLS])
            self.V((lambda tb_, p_: lambda e: e.tensor_tensor(out=COLS[:, 32 + tb_ * 4:32 + tb_ * 4 + 4], in0=p_[:, tb_ * 8 + 4:tb_ * 8 + 8],
                                                               in1=self.DTB[:, l * 4:l * 4 + 4], op=ALU.add))(tb, p), r=[bp, self.BC], w=[BCOLS])
        self.A(lambda e: e.activation(out=COLS[:, 32:48], in_=COLS[:, 32:48], func=AF.Exp), r=[BCOLS], w=[BCOLS])
        self.A(lambda e: e.activation(out=COLS[:, 32:48], in_=COLS[:, 32:48], func=AF.Ln, bias=1.0), r=[BCOLS], w=[BCOLS])
        for tb in range(4):
            self.V((lambda tb_: lambda e: e.tensor_tensor(out=COLS[:, tb_ * 4:tb_ * 4 + 4], in0=COLS[:, 32 + tb_ * 4:32 + tb_ * 4 + 4],
                                                          in1=self.NEGA[:, l * 4:l * 4 + 4], op=ALU.mult))(tb), r=[BCOLS, self.BC], w=[BCOLS])
        if self.cfg.get("dncut") == 1: raise StopIteration
        for h in range(4):
            Sh = SST[:, h * 128:(h + 1) * 128]
            for gi in range(3):
                c0 = 3072 + gi * 512 + h * 128
                p, bp = self.proj_fm(l, W, c0, XB, N, BXB)
                self.evac(RAW[:, gi * NR + 16:gi * NR + 16 + N], p[:, 0:N], r=[bp], w=[BRAW])
                hc = (gi * 4 + h) * 4
                self.V((lambda gi_, hc_: lambda e: e.tensor_copy(out=RAW[:, gi_ * NR + 13:gi_ * NR + 16], in_=self.CHIST[:, hc_:hc_ + 3]))(gi, hc),
                       r=[self.BCH], w=[BRAW])
                self.V((lambda gi_, hc_: lambda e: e.tensor_copy(out=self.CHIST[:, hc_:hc_ + 3], in_=RAW[:, gi_ * NR + 13 + N:gi_ * NR + 16 + N]))(gi, hc),
                       r=[BRAW], w=[self.BCH])
                cch = gi * 4 + h
                dst = QKV[:, gi * N:(gi + 1) * N]
                for j in range(4):
                    wcol = self.CW[:, (l * 4 + j) * 12 + cch:(l * 4 + j) * 12 + cch + 1]
                    srcj = RAW[:, gi * NR + 13 + j:gi * NR + 13 + j + N]
                    if j == 0:
                        self.V((lambda d_, s_, w_: lambda e: e.tensor_scalar(out=d_, in0=s_, scalar1=w_, scalar2=None, op0=ALU.mult))(dst, srcj, wcol),
                               r=[BRAW, self.BC], w=[BQKV])
                    else:
                        self.V((lambda d_, s_, w_: lambda e: e.scalar_tensor_tensor(out=d_, in0=s_, scalar=w_, in1=d_, op0=ALU.mult, op1=ALU.add))(dst, srcj, wcol),
                               r=[BRAW, self.BC, BQKV], w=[BQKV])
                self.A((lambda d_: lambda e: e.activation(out=d_, in_=d_, func=AF.Silu))(dst), r=[BQKV], w=[BQKV])
                if gi < 2:
                    self.A((lambda d_: lambda e: e.activation(out=SQ[:, 0:N], in_=d_, func=AF.Square))(dst), r=[BQKV], w=[BSQ])
                    pn, bpn = self.ps()
                    self.mm(pn[:, 0:N], self.ONESF[:, :], SQ[:, 0:N], True, True, r=[self.BC, BSQ], w=[bpn])
                    self.A((lambda pn_: lambda e: e.activation(out=RI[:, 0:N], in_=pn_[:, 0:N], func=AF.Sqrt, bias=self.EPSC[:, 1:2]))(pn),
                           r=[bpn, self.BC], w=[BRI])
                    self.V(lambda e: e.reciprocal(out=RI[:, 0:N], in_=RI[:, 0:N]), r=[BRI], w=[BRI])
                    sc = (128.0 ** -0.5) if gi == 0 else 1.0
                    self.V((lambda d_, sc_: lambda e: e.scalar_tensor_tensor(out=d_, in0=d_, scalar=sc_, in1=RI[:, 0:N], op0=ALU.mult, op1=ALU.mult))(dst, sc),
                           r=[BQKV, BRI], w=[BQKV])
            if self.cfg.get("dncut") == 2: raise StopIteration
            for tb in range(4):
                cs0 = tb * 128
                qT = QKV[:, 0 * N + cs0:0 * N + cs0 + 128]
                kT = QKV[:, 1 * N + cs0:1 * N + cs0 + 128]
                vT = QKV[:, 2 * N + cs0:2 * N + cs0 + 128]
                gcol = COLS[:, tb * 4 + h:tb * 4 + h + 1]
                bcol = COLS[:, 16 + tb * 4 + h:16 + tb * 4 + h + 1]
                (gbc, Bgbc), (Gb, BGb), (Dm, BDm), (E, BE), (ET, BET) = T_["gbc"], T_["Gb"], T_["Dm"], T_["E"], T_["ET"]
                (tmp, Btmp) = T_["tmp"]
                self.V(lambda e, gcol=gcol, bcol=bcol, qT=qT, kT=kT, vT=vT: e.tensor_scalar(out=gbc, in0=self.ONESF[:, :], scalar1=gcol, scalar2=None, op0=ALU.mult), r=[self.BC, BCOLS], w=[Bgbc])
                p, bp = self.ps()
                self.mm(p[:, 0:128], C["u_le"][:, :], gbc, True, True, r=[self.BC, Bgbc], w=[bp])
                self.mm(p[:, 128:256], gbc, C["u_le"][:, :], True, True, r=[self.BC, Bgbc], w=[bp])
                self.A((lambda p_: lambda e, gcol=gcol, bcol=bcol, qT=qT, kT=kT, vT=vT: e.copy(out=Gb, in_=p_[:, 128:256]))(p), r=[bp], w=[BGb])
                self.A((lambda p_: lambda e, gcol=gcol, bcol=bcol, qT=qT, kT=kT, vT=vT: e.copy(out=SC[:, 0:1], in_=p_[:, 0:1]))(p), r=[bp], w=[BSC])
                self.V((lambda p_: lambda e, gcol=gcol, bcol=bcol, qT=qT, kT=kT, vT=vT: e.tensor_tensor(out=Dm, in0=p_[:, 0:128], in1=Gb, op=ALU.subtract))(p), r=[bp, BGb], w=[BDm])
                self.V(lambda e, gcol=gcol, bcol=bcol, qT=qT, kT=kT, vT=vT: e.tensor_scalar(out=E, in0=Dm, scalar1=0.0, scalar2=None, op0=ALU.min), r=[BDm], w=[BE])
                self.A(lambda e, gcol=gcol, bcol=bcol, qT=qT, kT=kT, vT=vT: e.activation(out=E, in_=E, func=AF.Exp), r=[BE], w=[BE])
                self.V(lambda e, gcol=gcol, bcol=bcol, qT=qT, kT=kT, vT=vT: e.tensor_scalar(out=ET, in0=Dm, scalar1=0.0, scalar2=None, op0=ALU.max), r=[BDm], w=[BET])
                self.A(lambda e, gcol=gcol, bcol=bcol, qT=qT, kT=kT, vT=vT: e.activation(out=ET, in_=ET, func=AF.Exp, scale=-1.0), r=[BET], w=[BET])
                self.A(lambda e, gcol=gcol, bcol=bcol, qT=qT, kT=kT, vT=vT: e.activation(out=SC[:, 1:2], in_=SC[:, 0:1], func=AF.Exp), r=[BSC], w=[BSC])
                self.V(lambda e, gcol=gcol, bcol=bcol, qT=qT, kT=kT, vT=vT: e.tensor_tensor(out=SC[:, 2:3], in0=SC[:, 1:2], in1=bcol, op=ALU.mult), r=[BSC, BCOLS], w=[BSC])
                self.A(lambda e, gcol=gcol, bcol=bcol, qT=qT, kT=kT, vT=vT: e.activation(out=SC[:, 3:4], in_=SC[:, 0:1], func=AF.Exp, scale=-1.0, bias=Gb[:, 127:128]), r=[BSC, BGb], w=[BSC])
                self.A(lambda e, gcol=gcol, bcol=bcol, qT=qT, kT=kT, vT=vT: e.activation(out=SC[:, 4:5], in_=Gb[:, 127:128], func=AF.Exp), r=[BGb], w=[BSC])
                if self.cfg.get("dncut") == 3: raise StopIteration
                (A0, BA0), (B0, BB0), (A1, BA1), (B1, BB1) = T_["A0"], T_["B0"], T_["A1"], T_["B1"]
                (X0, BX0), (X1, BX1) = T_["X0"], T_["X1"]
                p, bp = self.ps()
                self.mm(p[:, 0:128], kT, kT, True, True, r=[BQKV], w=[bp])
                self.V((lambda p_: lambda e, gcol=gcol, bcol=bcol, qT=qT, kT=kT, vT=vT: e.tensor_tensor(out=tmp, in0=p_[:, 0:128], in1=E, op=ALU.mult))(p), r=[bp, BE], w=[Btmp])
                self.V(lambda e, gcol=gcol, bcol=bcol, qT=qT, kT=kT, vT=vT: e.scalar_tensor_tensor(out=A0, in0=tmp, scalar=bcol, in1=C["m_low"][:, :], op0=ALU.mult, op1=ALU.mult),
                       r=[Btmp, BCOLS, self.BC], w=[BA0])
                self.tr(p[:, 128:256], A0, ident[:, :], r=[BA0, self.BC], w=[bp])
                self.A((lambda p_: lambda e, gcol=gcol, bcol=bcol, qT=qT, kT=kT, vT=vT: e.copy(out=B0, in_=p_[:, 128:256]))(p), r=[bp], w=[BB0])
                self.V(lambda e, gcol=gcol, bcol=bcol, qT=qT, kT=kT, vT=vT: e.tensor_tensor(out=X0, in0=ident[:, :], in1=B0, op=ALU.subtract), r=[self.BC, BB0], w=[BX0])
                if self.cfg.get("dncut") == 4: raise StopIteration
                PA, BPA, PB, BPB = A0, BA0, B0, BB0
                NA, BNA, NB_, BNB = A1, BA1, B1, BB1
                XC, BXC, XN, BXN = X0, BX0, X1, BX1
                for step in range(self.cfg.get("nsteps", 6)):
                    p, bp = self.ps()
                    self.mm(p[:, 0:128], PB, PA, True, True, r=[BPA, BPB], w=[bp])
                    if step < 5:
                        self.mm(p[:, 128:256], PA, PB, True, True, r=[BPA, BPB], w=[bp])
                    self.V((lambda p_, d_: lambda e, gcol=gcol, bcol=bcol, qT=qT, kT=kT, vT=vT: e.tensor_copy(out=d_, in_=p_[:, 0:128]))(p, NA), r=[bp], w=[BNA])
                    if step < 5:
                        self.V((lambda p_, d_: lambda e, gcol=gcol, bcol=bcol, qT=qT, kT=kT, vT=vT: e.tensor_copy(out=d_, in_=p_[:, 128:256]))(p, NB_), r=[bp], w=[BNB])
                    self.mm(p[:, 256:384], NA, XC, True, True, r=[BNA, BXC], w=[bp])
                    self.V((lambda p_, d_, s_: lambda e, gcol=gcol, bcol=bcol, qT=qT, kT=kT, vT=vT: e.tensor_tensor(out=d_, in0=p_[:, 256:384], in1=s_, op=ALU.add))(p, XN, XC), r=[bp, BXC], w=[BXN])
                    PA, BPA, NA, BNA = NA, BNA, PA, BPA
                    PB, BPB, NB_, BNB = NB_, BNB, PB, BPB
                    XC, BXC, XN, BXN = XN, BXN, XC, BXC
                TT_, BTT_ = XC, BXC
                if self.cfg.get("dncut") == 5: raise StopIteration
                (bv, Bbv), (kbg, Bkbg), (kdec, Bkdec), (nwT, BnwT), (u, Bu) = T_["bv"], T_["kbg"], T_["kdec"], T_["nwT"], T_["u"]
                (qkT, BqkT), (eGb, BeGb), (qdT, BqdT) = T_["qkT"], T_["eGb"], T_["qdT"]
                p, bp = self.ps()
                self.tr(p[:, 0:128], kT, ident[:, :], r=[BQKV, self.BC], w=[bp])
                self.tr(p[:, 128:256], vT, ident[:, :], r=[BQKV, self.BC], w=[bp])
                self.V((lambda p_: lambda e, gcol=gcol, bcol=bcol, qT=qT, kT=kT, vT=vT: e.tensor_scalar(out=bv, in0=p_[:, 128:256], scalar1=bcol, scalar2=None, op0=ALU.mult))(p), r=[bp, BCOLS], w=[Bbv])
                self.V((lambda p_: lambda e, gcol=gcol, bcol=bcol, qT=qT, kT=kT, vT=vT: e.tensor_scalar(out=kbg, in0=p_[:, 0:128], scalar1=SC[:, 2:3], scalar2=None, op0=ALU.mult))(p), r=[bp, BSC], w=[Bkbg])
                self.V((lambda p_: lambda e, gcol=gcol, bcol=bcol, qT=qT, kT=kT, vT=vT: e.tensor_scalar(out=kdec, in0=p_[:, 0:128], scalar1=SC[:, 3:4], scalar2=None, op0=ALU.mult))(p), r=[bp, BSC], w=[Bkdec])
                p, bp = self.ps()
                self.mm(p[:, 0:128], kbg, TT_, True, True, r=[Bkbg, BTT_], w=[bp])
                self.A((lambda p_: lambda e, gcol=gcol, bcol=bcol, qT=qT, kT=kT, vT=vT: e.mul(out=nwT, in_=p_[:, 0:128], mul=-1.0))(p), r=[bp], w=[BnwT])
                self.mm(p[:, 128:256], TT_, bv, True, False, r=[BTT_, Bbv], w=[bp], sig=True)
                self.mm(p[:, 128:256], nwT, Sh, False, True, r=[BnwT, BSST], w=[bp])
                self.V((lambda p_: lambda e, gcol=gcol, bcol=bcol, qT=qT, kT=kT, vT=vT: e.tensor_copy(out=u, in_=p_[:, 128:256]))(p), r=[bp], w=[Bu])
                if self.cfg.get("dncut") == 6: raise StopIteration
                p, bp = self.ps()
                self.mm(p[:, 0:128], kT, qT, True, True, r=[BQKV], w=[bp])
                self.V((lambda p_: lambda e, gcol=gcol, bcol=bcol, qT=qT, kT=kT, vT=vT: e.tensor_tensor(out=tmp, in0=p_[:, 0:128], in1=ET, op=ALU.mult))(p), r=[bp, BET], w=[Btmp])
                self.V(lambda e, gcol=gcol, bcol=bcol, qT=qT, kT=kT, vT=vT: e.tensor_tensor(out=qkT, in0=tmp, in1=C["m_upi"][:, :], op=ALU.mult), r=[Btmp, self.BC], w=[BqkT])
                self.A(lambda e, gcol=gcol, bcol=bcol, qT=qT, kT=kT, vT=vT: e.activation(out=eGb, in_=Gb, func=AF.Exp), r=[BGb], w=[BeGb])
                self.V(lambda e, gcol=gcol, bcol=bcol, qT=qT, kT=kT, vT=vT: e.tensor_tensor(out=qdT, in0=qT, in1=eGb, op=ALU.mult), r=[BQKV, BeGb], w=[BqdT])
                self.mm(p[:, 128:256], Sh, qdT, True, False, r=[BSST, BqdT], w=[bp], sig=True)
                self.mm(p[:, 128:256], u, qkT, False, True, r=[Bu, BqkT], w=[bp])
                self.evac(OT[:, cs0:cs0 + 128], p[:, 128:256], r=[bp], w=[BOT])
                self.mm(p[:, 256:384], kdec, u, True, True, r=[Bkdec, Bu], w=[bp])
                self.V((lambda p_, Sh_: lambda e, gcol=gcol, bcol=bcol, qT=qT, kT=kT, vT=vT: e.scalar_tensor_tensor(out=Sh_, in0=Sh_, scalar=SC[:, 4:5], in1=p_[:, 256:384], op0=ALU.mult, op1=ALU.add))(p, Sh),
                       r=[BSST, BSC, bp], w=[BSST])
            self.A(lambda e: e.activation(out=SQ[:, 0:N], in_=OT[:, 0:N], func=AF.Square), r=[BOT], w=[BSQ])
            pn, bpn = self.ps()
            self.mm(pn[:, 0:N], self.ONESF[:, :], SQ[:, 0:N], True, True, r=[self.BC, BSQ], w=[bpn])
            self.A((lambda pn_: lambda e: e.activation(out=RI[:, 0:N], in_=pn_[:, 0:N], func=AF.Sqrt, bias=self.EPSC[:, 1:2], scale=1.0 / 128.0))(pn),
                   r=[bpn, self.BC], w=[BRI])
            self.V(lambda e: e.reciprocal(out=RI[:, 0:N], in_=RI[:, 0:N]), r=[BRI], w=[BRI])
            self.V(lambda e: e.scalar_tensor_tensor(out=OT[:, 0:N], in0=OT[:, 0:N], scalar=self.NG[:, l:l + 1], in1=RI[:, 0:N], op0=ALU.mult, op1=ALU.mult),
                   r=[BOT, BRI, self.BC], w=[BOT])
            pz, bpz = self.proj_fm(l, W, 4608 + h * 128, XB, N, BXB)
            self.A((lambda pz_: lambda e: e.activation(out=SQ[:, 0:N], in_=pz_[:, 0:N], func=AF.Silu))(pz), r=[bpz], w=[BSQ])
            self.V((lambda h_: lambda e: e.tensor_tensor(out=MIX[:, (12 + h_) * N:(13 + h_) * N], in0=OT[:, 0:N], in1=SQ[:, 0:N], op=ALU.mult))(h),
                   r=[BOT, BSQ], w=[BMIX[3]])
        if t == self.NT - 1:
            S.dma("scalar", lambda e: e.dma_start(out=self.ndelta_d[l].rearrange("h k v -> k h v"), in_=SST[:, :].rearrange("p (h v) -> p h v", h=4)),
                  r=[BSST], is_output=True)

    def sample_tile(self, l):
        S, NS, L = self.S, self.NS, self.L
        X, XB, BX, BXB = self.XS, self.XSB, self.BXS, self.BXSB
        N = NS
        if l == 0:
            stg = self.ARENA[:, 0:D]
            S.dma("scalar", lambda e: e.dma_start(out=stg[0:NS, :], in_=self.xs_d), w=[self.BACT])
            for c in range(KC):
                p, bp = self.ps()
                self.tr(p[:, 0:NS], stg[0:NS, c * 128:(c + 1) * 128], self.C["ident"][0:NS, 0:NS], r=[self.BACT, self.BC], w=[bp])
                self.evac(X[:, c * N:(c + 1) * N], p[:, 0:N], r=[bp], w=[BX])
            self.V(lambda e: e.tensor_copy(out=XB[:, :], in_=X[:, :]), r=[BX], w=[BXB])
        self.ffn(l, 0, X, XB, N, BX, BXB)
        self.layernorm(l, 0, X, XB, N, BX, BXB)
        MIX, BMIX = self.MIXS, self.BMIXS
        self.V(lambda e: e.memset(MIX[:, :], 0.0), w=list(BMIX))
        PRS = self.ARENA[0:NS, 0:5120]
        W = self.win
        for c4 in range(10):
            p, bp = self.ps()
            for cc in range(4):
                c0 = (c4 * 4 + cc) * 128
                w, bw = self.load_w(W[l, :, c0:c0 + 128], 16)
                for k in range(KC):
                    self.mm(p[0:NS, cc * 128:(cc + 1) * 128], XB[:, k * N:(k + 1) * N], w[:, k * 128:(k + 1) * 128],
                            k == 0, k == KC - 1, r=[bw, BXB], w=[bp])
            self.evac(PRS[:, c4 * 512:(c4 + 1) * 512], p[0:NS, 0:512], r=[bp], w=[self.BACT])
        o = lambda fn: S.dma("scalar", fn, r=[self.BACT], is_output=True)
        o(lambda e: e.dma_start(out=self.nks_d[l], in_=PRS[:, 2048:2560]))
        o(lambda e: e.dma_start(out=self.nvs_d[l], in_=PRS[:, 2560:3072]))
        o(lambda e: e.dma_start(out=self.nsgus_d[l], in_=PRS[:, 1024:1536]))
        o(lambda e: e.dma_start(out=self.npools_d[l, :, 14, :], in_=PRS[:, 0:512]))
        o(lambda e: e.dma_start(out=self.nconvs_d[l, :, 2, :], in_=PRS[:, 3072:4608]))
        S.dma("scalar", lambda e: e.dma_start(out=self.npools_d[l, :, 0:14, :], in_=self.spool_d[l, :, 1:15, :]), is_output=True)
        S.dma("scalar", lambda e: e.dma_start(out=self.nconvs_d[l, :, 0:2, :], in_=self.sconv_d[l, :, 1:3, :]), is_output=True)
        if self.stage >= 3:
            self.sample_mixers(l)
        self.wout_res(l, X, N, BX, MIX, BMIX)
        self.layernorm(l, 1, X, XB, N, BX, BXB)
        self.ffn(l, 1, X, XB, N, BX, BXB)
        self.layernorm(l, 2, X, XB, N, BX, BXB)
        if l == L - 1:
            self.store_y_tile(0, X, BX, N, self.ys_d)


    def sample_mixers(self, l):
        S, NS, L, NPG = self.S, self.NS, self.L, self.NPG
        XB, BXB = self.XSB, self.BXSB
        MIX, BMIX = self.MIXS, self.BMIXS
        W = self.win
        C = self.C
        ident = C["ident"]
        N = NS
        ar = self.ARENA
        off = [5120]
        PRS = ar[0:NS, 0:5120]

        allb = []

        def al(n, name):
            a = ar[:, off[0]:off[0] + n]
            off[0] += n
            assert off[0] <= 11264, off[0]
            b = Buf(name)
            S.alias([self.BACT], [b])
            allb.append(b)
            return a, b
        en = self.cfg.get("smixers", "pool,sgu,sb,dn")
        STG, BSTG = al(512, "p_stg")
        STT, BSTT = al(4 * 240, "p_stt")
        SM, BSM = al(64, "p_sm")
        DB_, BDB = al(32, "p_db")
        DBb = DB_[:, :].bitcast(BF16)
        for half in range(2):
            S.dma("scalar", (lambda hf: lambda e: e.dma_start(
                out=STG[0:120, 0:512], in_=self.spool_d[l, hf * 8:(hf + 1) * 8].rearrange("s r c -> (s r) c")))(half), w=[BSTG])
            p, bp = self.ps()
            for g in range(4):
                self.tr(p[:, g * 128:g * 128 + 120], STG[0:120, g * 128:(g + 1) * 128], ident[0:120, 0:120], r=[BSTG, self.BC], w=[bp])
            for g in range(4):
                self.V((lambda g_, hf, p_: lambda e: e.tensor_copy(out=STT[:, g_ * 240 + hf * 120:g_ * 240 + hf * 120 + 120],
                                                                     in_=p_[:, g_ * 128:g_ * 128 + 120]))(g, half, p), r=[bp], w=[BSTT])
        for g, wdw in enumerate(POOL_W):
            pa, bpa = self.proj_fm(l, W, g * 128, XB, N, BXB)
            view = STT[:, g * 240:(g + 1) * 240].rearrange("p (s r) -> p s r", r=15)
            self.V((lambda v_, w_, g_: lambda e: e.tensor_reduce(out=SM[:, g_ * 16:(g_ + 1) * 16], in_=v_[:, :, 15 - (w_ - 1):15],
                                                                  axis=AX.X, op=ALU.add))(view, wdw, g), r=[BSTT], w=[BSM])
            self.V((lambda g_, pa_: lambda e: e.tensor_tensor(out=SM[:, g_ * 16:(g_ + 1) * 16], in0=SM[:, g_ * 16:(g_ + 1) * 16],
                                                               in1=pa_[:, 0:N], op=ALU.add))(g, pa), r=[BSM, bpa], w=[BSM])
            self.V((lambda g_, pa_, w_: lambda e: e.scalar_tensor_tensor(out=DBb[:, g_ * 16:(g_ + 1) * 16], in0=SM[:, g_ * 16:(g_ + 1) * 16],
                                                                          scalar=1.0 / w_, in1=pa_[:, 0:N], op0=ALU.mult, op1=ALU.subtract))(g, pa, wdw),
                   r=[BSM, bpa], w=[BDB])
            p, bp = self.ps()
            self.mm(p[:, 0:N], self.POOLW[:, (l * 4 + g) * 128:(l * 4 + g + 1) * 128], DBb[:, g * 16:(g + 1) * 16], True, True, r=[self.BC, BDB], w=[bp])
            self.V((lambda g_, p_: lambda e: e.tensor_scalar(out=MIX[:, g_ * N:(g_ + 1) * N], in0=p_[:, 0:N],
                                                              scalar1=self.PSC[:, l * 4 + g_:l * 4 + g_ + 1], scalar2=None, op0=ALU.mult))(g, p),
                   r=[bp, self.BC], w=[BMIX[0]])
        TS, BTS = al(16, "s_t")
        for h in range(4):
            pv, bpv = self.proj_fm(l, W, 1024 + h * 128, XB, N, BXB)
            self.V((lambda h_, pv_: lambda e: e.tensor_scalar(out=TS[:, 0:N], in0=pv_[:, 0:N], scalar1=self.SW00[:, l * 4 + h_:l * 4 + h_ + 1],
                                                               scalar2=self.SGUB[:, (l * 4 + h_) * 128:(l * 4 + h_) * 128 + 1],
                                                               op0=ALU.mult, op1=ALU.add))(h, pv), r=[bpv, self.BC], w=[BTS])
            pu, bpu = self.proj_fm(l, W, 512 + h * 128, XB, N, BXB)
            self.V((lambda h_, pu_: lambda e: e.tensor_tensor(out=MIX[:, (4 + h_) * N:(5 + h_) * N], in0=pu_[:, 0:N], in1=TS[:, 0:N], op=ALU.mult))(h, pu),
                   r=[bpu, BTS], w=[BMIX[1]])
        mark = off[0]
        n_mark = len(allb)
        for gi_, nm in enumerate(("pool", "sgu")):
            if nm not in en:
                self.V((lambda g_: lambda e: e.memset(MIX[:, g_ * 4 * N:(g_ + 1) * 4 * N], 0.0))(gi_), w=[BMIX[gi_]])
        if "sb" in en:
            KP = [al(512, "kp%d" % i) for i in range(2)]
            VP = [al(512, "vp%d" % i) for i in range(2)]
            QB, BQB = al(512, "qb")
            PROD, BPROD = al(512, "prod")
            NZ = NPG * 4
            Z, BZ = al(NZ, "z"); ZS, BZS = al(NZ, "zs"); SP, BSP = al(NZ, "sp"); TOT, BTOT = al(NZ, "tot")
            R, BR = al(NZ, "r"); ATTs, BATTs = al(NZ, "att"); BIASR, BBIASR = al(NZ, "biasr")
            ATTP, BATTP = al(NPG * 16, "attp")
            for pg in range(NPG):
                self.V((lambda pg_: lambda e: e.tensor_copy(out=BIASR[:, pg_ * 4:(pg_ + 1) * 4], in_=self.SBB[:, l * 4:(l + 1) * 4]))(pg),
                       r=[self.BC], w=[BBIASR])
            if l == 0:
                self.V(lambda e: e.tensor_scalar(out=self.IDX[:, :], in0=self.PTAB[:, :], scalar1=7, scalar2=self.IOTA[:, 0:1],
                                                 op0=ALU.logical_shift_left, op1=ALU.bitwise_or), r=[self.BC], w=[self.BC])
            ck = self.ck_d[l]
            cv = self.cv_d[l]
            scale = 128.0 ** -0.5
            POH = [self.ps(hold=True) for _ in range(4)]
            gi = 0
            for s in range(NS):
                if s == 0:
                    S.dma("scalar", lambda e: e.dma_start(out=self.qscr, in_=PRS[:, 1536:2048]), r=[self.BACT], w=[self.BQS])
                S.dma("scalar", (lambda s_: lambda e: e.dma_start(out=QB[:, :], in_=self.qscr[s_:s_ + 1, :].partition_broadcast(128)))(s),
                      r=[self.BQS], w=[BQB])
                for pg in range(NPG):
                    kp, bkp = KP[gi % 2]
                    gi += 1
                    col = s * NPG + pg
                    S.dma("gpsimd", (lambda kp_, col_: lambda e: e.indirect_dma_start(
                        out=kp_[:, :], out_offset=None, in_=ck,
                        in_offset=bass.IndirectOffsetOnAxis(ap=self.IDX[:, col_:col_ + 1], axis=0)))(kp, col), r=[self.BC], w=[bkp])
                    self.V((lambda kp_: lambda e: e.tensor_tensor(out=PROD[:, :], in0=kp_[:, :], in1=QB[:, :], op=ALU.mult))(kp), r=[bkp, BQB], w=[BPROD])
                    self.V((lambda pg_: lambda e: e.tensor_reduce(out=Z[:, pg_ * 4:(pg_ + 1) * 4], in_=PROD[:, :].rearrange("p (h d) -> p h d", h=4),
                                                                  axis=AX.X, op=ALU.add))(pg), r=[BPROD], w=[BZ])
                self.V(lambda e: e.scalar_tensor_tensor(out=ZS[:, :], in0=Z[:, :], scalar=scale, in1=BIASR[:, :], op0=ALU.mult, op1=ALU.add),
                       r=[BZ, BBIASR], w=[BZS])
                self.A(lambda e: e.activation(out=SP[:, :], in_=ZS[:, :], func=AF.Exp), r=[BZS], w=[BSP])
                self.A(lambda e: e.activation(out=SP[:, :], in_=SP[:, :], func=AF.Ln, bias=1.0), r=[BSP], w=[BSP])
                pst, bpst = self.ps()
                self.mm(pst[:, 0:NZ], C["u_ge"][:, :], SP[:, :], True, True, r=[self.BC, BSP], w=[bpst])
                self.mm(pst[:, 128:128 + NZ], self.ONESF[:, :], SP[:, :], True, True, r=[self.BC, BSP], w=[bpst])
                self.V((lambda p_: lambda e: e.tensor_copy(out=TOT[:, :], in_=p_[:, 128:128 + NZ]))(pst), r=[bpst], w=[BTOT])
                self.V(lambda e: e.memset(R[:, (NPG - 1) * 4:NPG * 4], 0.0), w=[BR])
                for pg in range(NPG - 2, -1, -1):
                    self.V((lambda pg_: lambda e: e.tensor_tensor(out=R[:, pg_ * 4:(pg_ + 1) * 4], in0=R[:, (pg_ + 1) * 4:(pg_ + 2) * 4],
                                                                  in1=TOT[:, (pg_ + 1) * 4:(pg_ + 2) * 4], op=ALU.add))(pg), r=[BR, BTOT], w=[BR])
                self.V((lambda p_: lambda e: e.tensor_tensor(out=ATTs[:, :], in0=ZS[:, :], in1=p_[:, 0:NZ], op=ALU.subtract))(pst), r=[BZS, bpst], w=[BATTs])
                self.V(lambda e: e.tensor_tensor(out=ATTs[:, :], in0=ATTs[:, :], in1=R[:, :], op=ALU.subtract), r=[BATTs, BR], w=[BATTs])
                self.A(lambda e: e.activation(out=ATTs[:, :], in_=ATTs[:, :], func=AF.Exp), r=[BATTs], w=[BATTs])
                self.V(lambda e: e.tensor_copy(out=ATTP[:, :].rearrange("p (g c) -> p g c", c=16)[:, :, 0:4],
                                               in_=ATTs[:, :].rearrange("p (g c) -> p g c", c=4)), r=[BATTs], w=[BATTP])
                for pg in range(NPG):
                    vp, bvp = VP[pg % 2]
                    col = s * NPG + pg
                    S.dma("gpsimd", (lambda vp_, col_: lambda e: e.indirect_dma_start(
                        out=vp_[:, :], out_offset=None, in_=cv,
                        in_offset=bass.IndirectOffsetOnAxis(ap=self.IDX[:, col_:col_ + 1], axis=0)))(vp, col), r=[self.BC], w=[bvp])
                    for h in range(4):
                        self.mm(POH[h][0][:, s * 4:s * 4 + 4], vp[:, h * 128:(h + 1) * 128], ATTP[:, pg * 16:pg * 16 + 4],
                                pg == 0, pg == NPG - 1, r=[bvp, BATTP], w=[POH[h][1]], sig=True)
            for h in range(4):
                self.V((lambda h_, po_: lambda e: e.tensor_copy(
                    out=MIX[:, (8 + h_) * NS:(9 + h_) * NS],
                    in_=po_[:, 0:NS * 4].rearrange("p (s c) -> p s c", c=4)[:, :, h_]))(h, POH[h][0]), r=[POH[h][1]], w=[BMIX[2]])
            for h in range(4):
                self.ps_release(POH[h][0])
            if self.cfg.get("debug"):
                DBG, BDBG = al(64, "dbg")
                self.V(lambda e: e.tensor_copy(out=DBG[:, :], in_=MIX[:, 8 * NS:12 * NS]), r=[BMIX[2]], w=[BDBG])
                S.dma("scalar", lambda e: e.dma_start(out=self.dbg_d[l], in_=DBG[:, :]), r=[BDBG], is_output=True)
                S.dma("scalar", lambda e: e.dma_start(out=self.dbgq_d[l], in_=PRS[:, 1536:2048]), r=[self.BACT], is_output=True)
        if "dn" in en:
            off[0] = mark
            sb_bufs = allb[n_mark:]
            n_dn = len(allb)
            CSTG, BCSTG = al(1536, "c_stg")
            STC, BSTC = al(12 * 48, "c_stt")
            QKVs, BQKVs = al(12 * 16, "c_qkv")
            SQs, BSQs = al(16, "c_sq"); RIs, BRIs = al(16, "c_ri")
            BRW, BBRW = al(64, "c_b"); EGR, BEGR = al(64, "c_eg"); NBE, BNBE = al(64, "c_nbe")
            COLA, BCOLA = al(64, "c_cola"); COLB, BCOLB = al(64, "c_colb"); OS, BOS = al(64, "c_os")
            BCA = [al(128, "c_bca%d" % i) for i in range(2)]
            BCB = [al(128, "c_bcb%d" % i) for i in range(2)]
            S0 = [al(128, "c_s0%d" % i) for i in range(2)]
            SN = [al(128, "c_sn%d" % i) for i in range(2)]
            SQ4, BSQ4 = al(64, "c_sq4"); RI4, BRI4 = al(64, "c_ri4")
            S.alias(sb_bufs, allb[n_dn:])
            for j in range(3):
                S.dma("scalar", (lambda j_: lambda e: e.dma_start(out=CSTG[j_ * 16:(j_ + 1) * 16, 0:1536], in_=self.sconv_d[l, :, j_, :]))(j), w=[BCSTG])
            for c4 in range(3):
                p, bp = self.ps()
                for cc in range(4):
                    cch = c4 * 4 + cc
                    self.tr(p[:, cc * 128:cc * 128 + 48], CSTG[0:48, cch * 128:(cch + 1) * 128], ident[0:48, 0:48], r=[BCSTG, self.BC], w=[bp])
                for cc in range(4):
                    cch = c4 * 4 + cc
                    self.V((lambda cc_, cch_, p_: lambda e: e.tensor_copy(out=STC[:, cch_ * 48:(cch_ + 1) * 48], in_=p_[:, cc_ * 128:cc_ * 128 + 48]))(cc, cch, p),
                           r=[bp], w=[BSTC])
            for cch in range(12):
                gi_, h = cch // 4, cch % 4
                pr, bpr = self.proj_fm(l, W, 3072 + cch * 128, XB, N, BXB)
                dst = QKVs[:, cch * 16:(cch + 1) * 16]
                wc = lambda j: self.CW[:, (l * 4 + j) * 12 + cch:(l * 4 + j) * 12 + cch + 1]
                self.V((lambda d_, pr_, w_: lambda e: e.tensor_scalar(out=d_, in0=pr_[:, 0:N], scalar1=w_, scalar2=None, op0=ALU.mult))(dst, pr, wc(3)),
                       r=[bpr, self.BC], w=[BQKVs])
                for j in range(3):
                    self.V((lambda d_, s_, w_: lambda e: e.scalar_tensor_tensor(out=d_, in0=s_, scalar=w_, in1=d_, op0=ALU.mult, op1=ALU.add))(
                        dst, STC[:, cch * 48 + j * 16:cch * 48 + (j + 1) * 16], wc(j)), r=[BSTC, self.BC, BQKVs], w=[BQKVs])
                self.A((lambda d_: lambda e: e.activation(out=d_, in_=d_, func=AF.Silu))(dst), r=[BQKVs], w=[BQKVs])
                if gi_ < 2:
                    self.A((lambda d_: lambda e: e.activation(out=SQs[:, 0:N], in_=d_, func=AF.Square))(dst), r=[BQKVs], w=[BSQs])
                    pn, bpn = self.ps()
                    self.mm(pn[:, 0:N], self.ONESF[:, :], SQs[:, 0:N], True, True, r=[self.BC, BSQs], w=[bpn])
                    self.A((lambda pn_: lambda e: e.activation(out=RIs[:, 0:N], in_=pn_[:, 0:N], func=AF.Sqrt, bias=self.EPSC[:, 1:2]))(pn),
                           r=[bpn, self.BC], w=[BRIs])
                    self.V(lambda e: e.reciprocal(out=RIs[:, 0:N], in_=RIs[:, 0:N]), r=[BRIs], w=[BRIs])
                    sc = (128.0 ** -0.5) if gi_ == 0 else 1.0
                    self.V((lambda d_, sc_: lambda e: e.scalar_tensor_tensor(out=d_, in0=d_, scalar=sc_, in1=RIs[:, 0:N], op0=ALU.mult, op1=ALU.mult))(dst, sc),
                           r=[BQKVs, BRIs], w=[BQKVs])
            for h in range(4):
                pb_, bpb = self.proj_fm(l, self.wba, h * 128, XB, N, BXB)
                self.A((lambda h_, p_: lambda e: e.activation(out=BRW[:, h_ * 16:(h_ + 1) * 16], in_=p_[:, 0:N], func=AF.Sigmoid))(h, pb_), r=[bpb], w=[BBRW])
                pa_, bpa = self.proj_fm(l, self.wba, (4 + h) * 128, XB, N, BXB)
                self.A((lambda h_, p_: lambda e: e.activation(out=EGR[:, h_ * 16:(h_ + 1) * 16], in_=p_[:, 0:N], func=AF.Exp,
                                                               bias=self.DTB[:, l * 4 + h_:l * 4 + h_ + 1]))(h, pa_), r=[bpa, self.BC], w=[BEGR])
                self.A((lambda h_: lambda e: e.activation(out=EGR[:, h_ * 16:(h_ + 1) * 16], in_=EGR[:, h_ * 16:(h_ + 1) * 16], func=AF.Ln, bias=1.0))(h),
                       r=[BEGR], w=[BEGR])
                self.A((lambda h_: lambda e: e.activation(out=EGR[:, h_ * 16:(h_ + 1) * 16], in_=EGR[:, h_ * 16:(h_ + 1) * 16], func=AF.Exp,
                                                          scale=self.NEGA[:, l * 4 + h_:l * 4 + h_ + 1]))(h), r=[BEGR, self.BC], w=[BEGR])
            self.V(lambda e: e.scalar_tensor_tensor(out=NBE[:, :], in0=BRW[:, :], scalar=-1.0, in1=EGR[:, :], op0=ALU.mult, op1=ALU.mult), r=[BBRW, BEGR], w=[BNBE])
            self.V(lambda e: e.tensor_tensor(out=COLA[:, :], in0=QKVs[:, 8 * 16:12 * 16], in1=BRW[:, :], op=ALU.mult), r=[BQKVs, BBRW], w=[BCOLA])
            self.V(lambda e: e.tensor_tensor(out=COLB[:, :], in0=QKVs[:, 4 * 16:8 * 16], in1=NBE[:, :], op=ALU.mult), r=[BQKVs, BNBE], w=[BCOLB])
            it = 0
            for s in range(NS):
                for h in range(4):
                    i2 = it % 2
                    it += 1
                    (bca, Bbca), (bcb, Bbcb), (s0, Bs0), (sn, Bsn) = BCA[i2], BCB[i2], S0[i2], SN[i2]
                    cidx = h * 16 + s
                    S.dma("scalar", (lambda s0_, s_, h_: lambda e: e.dma_start(out=s0_[:, :], in_=self.sdelta_d[l, s_, h_]))(s0, s, h), w=[Bs0])
                    self.V((lambda d_, c_: lambda e: e.tensor_scalar(out=d_, in0=self.ONESF[:, :], scalar1=COLA[:, c_:c_ + 1], scalar2=None, op0=ALU.mult))(bca, cidx),
                           r=[self.BC, BCOLA], w=[Bbca])
                    self.V((lambda d_, c_: lambda e: e.tensor_scalar(out=d_, in0=self.ONESF[:, :], scalar1=COLB[:, c_:c_ + 1], scalar2=None, op0=ALU.mult))(bcb, cidx),
                           r=[self.BC, BCOLB], w=[Bbcb])
                    pu, bpu = self.ps()
                    self.mm(pu[:, 0:128], bca, ident[:, :], True, False, r=[Bbca, self.BC], w=[bpu], sig=True)
                    self.mm(pu[:, 0:128], bcb, s0, False, True, r=[Bbcb, Bs0], w=[bpu])
                    self.V((lambda sn_, s0_, c_: lambda e: e.tensor_scalar(out=sn_, in0=s0_, scalar1=EGR[:, c_:c_ + 1], scalar2=None, op0=ALU.mult))(sn, s0, cidx),
                           r=[Bs0, BEGR], w=[Bsn])
                    kc = (4 + h) * 16 + s
                    self.V((lambda sn_, pu_, kc_: lambda e: e.scalar_tensor_tensor(out=sn_, in0=pu_[:, 0:128], scalar=QKVs[:, kc_:kc_ + 1], in1=sn_,
                                                                                  op0=ALU.mult, op1=ALU.add))(sn, pu, kc), r=[bpu, BQKVs, Bsn], w=[Bsn])
                    po2, bpo2 = self.ps()
                    self.mm(po2[:, 0:16], sn, QKVs[:, h * 16:(h + 1) * 16], True, True, r=[Bsn, BQKVs], w=[bpo2])
                    self.V((lambda p_, c_, s_: lambda e: e.tensor_copy(out=OS[:, c_:c_ + 1], in_=p_[:, s_:s_ + 1]))(po2, cidx, s), r=[bpo2], w=[BOS])
                    S.dma("scalar", (lambda sn_, s_, h_: lambda e: e.dma_start(out=self.ndeltas_d[l, s_, h_], in_=sn_[:, :]))(sn, s, h), r=[Bsn], is_output=True)
            self.A(lambda e: e.activation(out=SQ4[:, :], in_=OS[:, :], func=AF.Square), r=[BOS], w=[BSQ4])
            pn, bpn = self.ps()
            self.mm(pn[:, 0:64], self.ONESF[:, :], SQ4[:, :], True, True, r=[self.BC, BSQ4], w=[bpn])
            self.A((lambda pn_: lambda e: e.activation(out=RI4[:, :], in_=pn_[:, 0:64], func=AF.Sqrt, bias=self.EPSC[:, 1:2], scale=1.0 / 128.0))(pn),
                   r=[bpn, self.BC], w=[BRI4])
            self.V(lambda e: e.reciprocal(out=RI4[:, :], in_=RI4[:, :]), r=[BRI4], w=[BRI4])
            self.V(lambda e: e.scalar_tensor_tensor(out=OS[:, :], in0=OS[:, :], scalar=self.NG[:, l:l + 1], in1=RI4[:, :], op0=ALU.mult, op1=ALU.mult),
                   r=[BOS, BRI4, self.BC], w=[BOS])
            for h in range(4):
                pz, bpz = self.proj_fm(l, W, 4608 + h * 128, XB, N, BXB)
                self.A((lambda pz_: lambda e: e.activation(out=SQ4[:, 0:N], in_=pz_[:, 0:N], func=AF.Silu))(pz), r=[bpz], w=[BSQ4])
                self.V((lambda h_: lambda e: e.tensor_tensor(out=MIX[:, (12 + h_) * N:(13 + h_) * N], in0=OS[:, h_ * 16:(h_ + 1) * 16], in1=SQ4[:, 0:N],
                                                             op=ALU.mult))(h), r=[BOS, BSQ4], w=[BMIX[3]])
        S.alias(allb + [self.BACT], [self.BACT])


def make_cfg(SEQ=2048, DFF=5632, NS=16, NPG=16, DEPTH=2, NPHYS=2560, **kw):
    d = dict(SEQ=SEQ, DFF=DFF, NS=NS, NPG=NPG, DEPTH=DEPTH, NPHYS=NPHYS)
    d.update(kw)
    return d


def build_program(cfg):
    kb = KB(cfg)
    with kb.es:
        kb.PHIST = kb.sb("PHIST", [128, 64]); kb.BPH = Buf("PHIST")
        nc = kb.build()
    return nc, kb


def prepare_inputs(cfg, inp, n_cores=8):
    L, NS = cfg["DEPTH"], cfg["NS"]
    f = lambda a: np.ascontiguousarray(np.asarray(a, dtype=np.float32))
    consts = host_consts(512)
    lng, lnb = f(inp["ln_g"]), f(inp["ln_b"])
    lngb = np.stack([lng, lnb], axis=2)
    lngb = lngb.reshape(L, 3, 2, 16, 128).transpose(4, 0, 1, 2, 3).reshape(128, L * 3 * 2 * 16)
    pscale = f(inp["pool_scale"]).reshape(L, 4, 128).transpose(2, 0, 1).reshape(128, L * 4)
    sguwT = f(inp["sgu_w"]).transpose(0, 1, 3, 2)
    win = f(inp["w_in"])
    wba = np.repeat(win[:, :, 5120:5128], 128, axis=2)
    shared = {
        "w1a": f(inp["w_ffn1_in"]), "w1b": f(inp["w_ffn2_in"]), "w2a": f(inp["w_ffn1_out"]), "w2b": f(inp["w_ffn2_out"]),
        "win": win, "wout": f(inp["w_out"]), "wba": np.ascontiguousarray(wba),
        "lngb": np.ascontiguousarray(lngb), "poolw": f(inp["pool_w"]), "pscale": np.ascontiguousarray(pscale),
        "sguwT": np.ascontiguousarray(sguwT), "sgub": f(inp["sgu_b"]).reshape(1, -1), "sbbias": f(inp["sb_bias"]).reshape(1, -1),
    }
    cw = f(inp["dn_conv_w"]).reshape(L, 4, 12, 128).transpose(3, 0, 1, 2).reshape(128, L * 4 * 12)
    shared["convw"] = np.ascontiguousarray(cw)
    shared["alog"] = f(inp["dn_a_log"]).reshape(1, -1)
    shared["dtb"] = f(inp["dn_dt_bias"]).reshape(1, -1)
    shared["normg"] = np.ascontiguousarray(f(inp["dn_norm_g"]).T)
    NPHYS = cfg["NPHYS"]
    ck_, cv_ = f(inp["cache_k"]), f(inp["cache_v"])
    for l_ in range(L):
        shared["ck%d" % l_] = ck_[l_].reshape(NPHYS * 128, 512)
        shared["cv%d" % l_] = cv_[l_].reshape(NPHYS * 128, 512)
    shared["iota"] = np.arange(128, dtype=np.int32).reshape(128, 1)
    shared["sguw00"] = np.ascontiguousarray(f(inp["sgu_w"])[:, :, 0, 0]).reshape(1, -1)
    for k, v in consts.items():
        shared["c_" + k] = v
    xp, xs = f(inp["x_prompt"]), f(inp["x_sample"])
    spool, sconv = f(inp["state_pool"]), f(inp["state_conv"])
    ptab = np.asarray(inp["page_table"], dtype=np.int32)
    sdelta = f(inp["state_delta"])
    maps = []
    for c in range(n_cores):
        m = dict(shared)
        m["x"] = xp[(c // 2) % xp.shape[0]]
        m["xs"] = np.ascontiguousarray(xs[c * NS:(c + 1) * NS, 0, :])
        m["spool"] = np.ascontiguousarray(spool[:, c * NS:(c + 1) * NS])
        m["ptab"] = np.ascontiguousarray(ptab[c * NS:(c + 1) * NS]).reshape(1, -1)
        m["sdelta"] = np.ascontiguousarray(sdelta[:, c * NS:(c + 1) * NS])
        m["sconv"] = np.ascontiguousarray(sconv[:, c * NS:(c + 1) * NS])
        maps.append(m)
    return maps


def assemble(cfg, res, B, n_cores=8):
    L, NS, SEQ = cfg["DEPTH"], cfg["NS"], cfg["SEQ"]
    pc = [res[2 * b] for b in range(B)]
    y_p = np.stack([r["y"] for r in pc])
    y_s = np.concatenate([r["ys"] for r in res], 0)[:, None, :]
    nk_p = np.stack([r["nk"] for r in pc], 1).reshape(L, B, SEQ, 4, 128)
    nv_p = np.stack([r["nv"] for r in pc], 1).reshape(L, B, SEQ, 4, 128)
    npool_p = np.stack([r["npool"] for r in pc], 1)
    nconv_p = np.stack([r["nconv"] for r in pc], 1)
    ndelta_p = np.stack([r["ndelta"] for r in pc], 1)
    cat = lambda k: np.concatenate([r[k] for r in res], 1)
    nk_s = cat("nks").reshape(L, -1, 1, 4, 128)
    nv_s = cat("nvs").reshape(L, -1, 1, 4, 128)
    npool_s = cat("npools")
    nconv_s = cat("nconvs")
    ndelta_s = cat("ndeltas")
    nsgu_s = cat("nsgus")[:, :, None, :]
    return (y_p, y_s, nk_p, nv_p, npool_p, nconv_p, ndelta_p, nk_s, nv_s, npool_s, nconv_s, ndelta_s, nsgu_s)


def kernel(**inputs):
    cfg = make_cfg()
    nc, kb = build_program(cfg)
    maps = prepare_inputs(cfg, inputs)
    res = run_bass_kernel_spmd(nc, maps, core_ids=list(range(8)))
    outs = assemble(cfg, res.results, 4)
    return tuple(np.ascontiguousarray(o, dtype=np.float32) for o in outs)
```
